# Optimizing a Trainium2 kernel written in Bass

```python
import jax, jax.numpy as jnp
from jax import lax
import numpy as np

D_MODEL = 1024
BATCH = 2
SEQ = 8192
DEPTH = 2

GRID_W = 64
CTX_LEN = 256
N_HEADS = 8
N_KV_HEADS = 2
GROUP = N_HEADS // N_KV_HEADS
HEAD_DIM = 64
ATTN_WIDTH = N_HEADS * HEAD_DIM
KV_WIDTH = N_KV_HEADS * HEAD_DIM
WINDOW = 128
BLOCK = 128
ATTN_SCALE = HEAD_DIM ** -0.5
ROPE_BASE = 10000.0
ROPE_FREQS = HEAD_DIM // 4
CONV_WIDTH = 512
CONV_KERNEL = 31
CONV_PAD = (CONV_KERNEL - 1) // 2
LRU_WIDTH = 512
LRU_BLOCKS = 8
LRU_BLOCK_DIM = LRU_WIDTH // LRU_BLOCKS
LRU_CONV = 4
LRU_PAD = (2, 1)
LRU_C = 8.0
N_BRANCH = 3
FFN_HIDDEN = -(-8 * D_MODEL // (3 * 256)) * 256
SPLITS = (ATTN_WIDTH, KV_WIDTH, KV_WIDTH, CONV_WIDTH, CONV_WIDTH, LRU_WIDTH, LRU_WIDTH)
IN_COLS = sum(SPLITS) + N_BRANCH * D_MODEL
EPS = 1e-6
NEG_INF = -1e30

kernel_name = 'hybrid_gated_attn_conv_rglru_dit'


def rms_norm(x, g):
    xf = x.astype(jnp.float32)
    y = xf * lax.rsqrt(jnp.mean(xf * xf, axis=-1, keepdims=True) + EPS)
    return (y * g.astype(jnp.float32)).astype(x.dtype)


def layer_norm(x, g, b):
    xf = x.astype(jnp.float32)
    mu = jnp.mean(xf, axis=-1, keepdims=True)
    var = jnp.mean(jnp.square(xf - mu), axis=-1, keepdims=True)
    return ((xf - mu) * lax.rsqrt(var + EPS) * g.astype(jnp.float32) + b.astype(jnp.float32)).astype(x.dtype)


def modulate(h, shift, scale):
    return h * (1 + scale) + shift


def split_proj(z):
    idx = [int(i) for i in np.cumsum(SPLITS)]
    return jnp.split(z, idx, axis=-1)


def axial_rope(rows, dtype):
    row = jnp.repeat(jnp.arange(rows, dtype=jnp.float32), GRID_W)
    col = jnp.tile(jnp.arange(GRID_W, dtype=jnp.float32), rows)
    inv = jnp.power(ROPE_BASE, -jnp.arange(ROPE_FREQS, dtype=jnp.float32) / ROPE_FREQS)
    ang = jnp.concatenate([row[:, None] * inv[None], col[:, None] * inv[None]], axis=-1)
    return jnp.cos(ang).astype(dtype), jnp.sin(ang).astype(dtype)


def apply_rope(x, cos, sin):
    x1, x2 = x[..., :HEAD_DIM // 2], x[..., HEAD_DIM // 2:]
    cos, sin = cos[None, :, None, :], sin[None, :, None, :]
    return jnp.concatenate([x1 * cos - x2 * sin, x1 * sin + x2 * cos], axis=-1)


def softmax_with_sink(s, sink_logit):
    m = jnp.maximum(jnp.max(s, axis=-1, keepdims=True), sink_logit)
    p = jnp.exp(s - m)
    return p / (jnp.sum(p, axis=-1, keepdims=True) + jnp.exp(sink_logit - m))


def band_blocks(t):
    B, S = t.shape[0], t.shape[1]
    nb = S // BLOCK
    tp = jnp.pad(t, ((0, 0), (BLOCK, BLOCK), (0, 0), (0, 0))).reshape(B, nb + 2, BLOCK, *t.shape[2:])
    return jnp.concatenate([tp[:, :-2], tp[:, 1:-1], tp[:, 2:]], axis=2)


def windowed_attention(q, k, v, k_ctx, v_ctx, sink):
    B, S = q.shape[0], q.shape[1]
    nb = S // BLOCK
    qb = q.reshape(B, nb, BLOCK, N_KV_HEADS, GROUP, HEAD_DIM) * ATTN_SCALE
    kb, vb = band_blocks(k), band_blocks(v)
    s_loc = jnp.einsum('bnqhgd,bnjhd->bnhgqj', qb, kb).astype(jnp.float32)
    blk = jnp.arange(nb)[:, None]
    q_pos = blk * BLOCK + jnp.arange(BLOCK)[None, :]
    k_pos = (blk - 1) * BLOCK + jnp.arange(3 * BLOCK)[None, :]
    valid = ((jnp.abs(q_pos[:, :, None] - k_pos[:, None, :]) <= WINDOW)
             & (k_pos >= 0)[:, None, :] & (k_pos < S)[:, None, :])
    s_loc = jnp.where(valid[None, :, None, None], s_loc, NEG_INF)
    s_ctx = jnp.einsum('bnqhgd,bchd->bnhgqc', qb, k_ctx).astype(jnp.float32)
    sink_b = sink.astype(jnp.float32).reshape(1, 1, N_KV_HEADS, GROUP, 1, 1)
    p = softmax_with_sink(jnp.concatenate([s_loc, s_ctx], axis=-1), sink_b).astype(v.dtype)
    out = (jnp.einsum('bnhgqj,bnjhd->bnqhgd', p[..., :3 * BLOCK], vb)
           + jnp.einsum('bnhgqc,bchd->bnqhgd', p[..., 3 * BLOCK:], v_ctx))
    return out.reshape(B, S, ATTN_WIDTH)


def context_attention(q, k, v, sink):
    B, C = q.shape[0], q.shape[1]
    qg = q.reshape(B, C, N_KV_HEADS, GROUP, HEAD_DIM) * ATTN_SCALE
    s = jnp.einsum('bqhgd,bjhd->bhgqj', qg, k).astype(jnp.float32)
    p = softmax_with_sink(s, sink.astype(jnp.float32).reshape(1, N_KV_HEADS, GROUP, 1, 1)).astype(v.dtype)
    return jnp.einsum('bhgqj,bjhd->bqhgd', p, v).reshape(B, C, ATTN_WIDTH)


def depthwise_conv(x, w, b, pad):
    out = lax.conv_general_dilated(x, w[:, None, :], window_strides=(1,), padding=[pad],
                                   dimension_numbers=('NWC', 'WIO', 'NWC'),
                                   feature_group_count=x.shape[-1])
    return out + b


def conformer_conv(val, gate, dw_w, dw_b, ln_g, ln_b):
    h = val * jax.nn.sigmoid(gate)
    h = depthwise_conv(h, dw_w, dw_b, (CONV_PAD, CONV_PAD))
    return jax.nn.silu(layer_norm(h, ln_g, ln_b))


def block_diag(x, w, b):
    xb = x.reshape(*x.shape[:-1], LRU_BLOCKS, LRU_BLOCK_DIM)
    return jnp.einsum('btnd,nde->btne', xb, w).reshape(x.shape) + b


def rglru_coeffs(u, wa, ba, wx, bx, lam):
    r = jax.nn.sigmoid(block_diag(u, wa, ba).astype(jnp.float32))
    i = jax.nn.sigmoid(block_diag(u, wx, bx).astype(jnp.float32))
    log_a = -LRU_C * r * jax.nn.softplus(-lam.astype(jnp.float32))
    a = jnp.exp(log_a)
    b = jnp.sqrt(-jnp.expm1(2.0 * log_a)) * (i * u.astype(jnp.float32))
    return a, b


def _combine(left, right):
    a_l, b_l = left
    a_r, b_r = right
    return a_l * a_r, a_r * b_l + b_r


def linear_scan(a, b, h0, reverse):
    if reverse:
        a, b = jnp.flip(a, axis=1), jnp.flip(b, axis=1)
    if h0 is not None:
        b = b.at[:, 0].add(a[:, 0] * h0)
    _, h = lax.associative_scan(_combine, (a, b), axis=1)
    return jnp.flip(h, axis=1) if reverse else h


def rglru_branch(x_lat, g_lat, x_ctx, g_ctx, need_ctx, conv_w, conv_b, wa, ba, wx, bx, lam):
    u_lat = depthwise_conv(x_lat, conv_w, conv_b, LRU_PAD)
    u_ctx = depthwise_conv(x_ctx, conv_w, conv_b, LRU_PAD)
    lat_dirs, ctx_dirs = [], []
    for d, reverse in enumerate((False, True)):
        a_c, b_c = rglru_coeffs(u_ctx, wa[d], ba[d], wx[d], bx[d], lam[d])
        h_c = linear_scan(a_c, b_c, None, reverse)
        h0 = h_c[:, 0] if reverse else h_c[:, -1]
        a_l, b_l = rglru_coeffs(u_lat, wa[d], ba[d], wx[d], bx[d], lam[d])
        lat_dirs.append(linear_scan(a_l, b_l, h0, reverse))
        ctx_dirs.append(h_c)
    y_lat = (lat_dirs[0] + lat_dirs[1]).astype(x_lat.dtype) * jax.nn.gelu(g_lat)
    y_ctx = (ctx_dirs[0] + ctx_dirs[1]).astype(x_ctx.dtype) * jax.nn.gelu(g_ctx) if need_ctx else None
    return y_lat, y_ctx


def token_mixers(a_lat, a_ctx, cos, sin, need_ctx, w_in, attn_sink, conv_dw_w, conv_dw_b, conv_ln_g,
                 conv_ln_b, lru_conv_w, lru_conv_b, lru_wa, lru_ba, lru_wx, lru_bx, lru_lam,
                 w_o_attn, w_o_conv, w_o_lru, w_out):
    q_l, k_l, v_l, cva_l, cvg_l, lx_l, lg_l, gate_l = split_proj(a_lat @ w_in)
    q_c, k_c, v_c, cva_c, cvg_c, lx_c, lg_c, gate_c = split_proj(a_ctx @ w_in)
    heads = lambda t, n: t.reshape(*t.shape[:-1], n, HEAD_DIM)
    k_ctx, v_ctx = heads(k_c, N_KV_HEADS), heads(v_c, N_KV_HEADS)
    y_attn_l = windowed_attention(apply_rope(heads(q_l, N_HEADS), cos, sin),
                                  apply_rope(heads(k_l, N_KV_HEADS), cos, sin),
                                  heads(v_l, N_KV_HEADS), k_ctx, v_ctx, attn_sink)
    y_conv_l = conformer_conv(cva_l, cvg_l, conv_dw_w, conv_dw_b, conv_ln_g, conv_ln_b)
    y_lru_l, y_lru_c = rglru_branch(lx_l, lg_l, lx_c, lg_c, need_ctx, lru_conv_w, lru_conv_b,
                                    lru_wa, lru_ba, lru_wx, lru_bx, lru_lam)

    def merge(ya, yb, yc, gates):
        ga, gb, gc = jnp.split(jax.nn.sigmoid(gates), N_BRANCH, axis=-1)
        return (ga * (ya @ w_o_attn) + gb * (yb @ w_o_conv) + gc * (yc @ w_o_lru)) @ w_out

    out_lat = merge(y_attn_l, y_conv_l, y_lru_l, gate_l)
    out_ctx = None
    if need_ctx:
        y_attn_c = context_attention(heads(q_c, N_HEADS), k_ctx, v_ctx, attn_sink)
        y_conv_c = conformer_conv(cva_c, cvg_c, conv_dw_w, conv_dw_b, conv_ln_g, conv_ln_b)
        out_ctx = merge(y_attn_c, y_conv_c, y_lru_c, gate_c)
    return out_lat, out_ctx


def swiglu(h, w_up, w_down):
    up, gate = jnp.split(h @ w_up, 2, axis=-1)
    return (jax.nn.silu(gate) * up) @ w_down


def setup_inputs(seed: int = 0) -> dict:
    key = jax.random.key(seed)
    ks = iter(jax.random.split(key, 32))
    L = DEPTH
    nrm = lambda shape, scale: jax.random.normal(next(ks), shape, jnp.float32) * scale
    gain = lambda shape: 1.0 + nrm(shape, 0.02)
    a0 = jax.random.uniform(next(ks), (L, 2, LRU_WIDTH), jnp.float32, minval=0.9, maxval=0.999)
    lam = jnp.log(a0) - jnp.log1p(-a0)
    return {
        'x': nrm((BATCH, SEQ, D_MODEL), 1.0),
        'c': nrm((BATCH, D_MODEL), 1.0),
        'ctx': nrm((BATCH, CTX_LEN, D_MODEL), 1.0),
        'c_ctx': nrm((D_MODEL,), 1.0),
        'mod_w': nrm((L, D_MODEL, 6 * D_MODEL), 0.5 * D_MODEL ** -0.5),
        'mod_b': nrm((L, 6 * D_MODEL), 0.02),
        'norm1_g': gain((L, D_MODEL)),
        'norm2_g': gain((L, D_MODEL)),
        'w_in': nrm((L, D_MODEL, IN_COLS), D_MODEL ** -0.5),
        'attn_sink': nrm((L, N_HEADS), 0.5),
        'conv_dw_w': nrm((L, CONV_KERNEL, CONV_WIDTH), CONV_KERNEL ** -0.5),
        'conv_dw_b': nrm((L, CONV_WIDTH), 0.02),
        'conv_ln_g': gain((L, CONV_WIDTH)),
        'conv_ln_b': nrm((L, CONV_WIDTH), 0.02),
        'lru_conv_w': nrm((L, LRU_CONV, LRU_WIDTH), LRU_CONV ** -0.5),
        'lru_conv_b': nrm((L, LRU_WIDTH), 0.02),
        'lru_wa': nrm((L, 2, LRU_BLOCKS, LRU_BLOCK_DIM, LRU_BLOCK_DIM), LRU_BLOCK_DIM ** -0.5),
        'lru_ba': nrm((L, 2, LRU_WIDTH), 0.02),
        'lru_wx': nrm((L, 2, LRU_BLOCKS, LRU_BLOCK_DIM, LRU_BLOCK_DIM), LRU_BLOCK_DIM ** -0.5),
        'lru_bx': nrm((L, 2, LRU_WIDTH), 0.02),
        'lru_lam': lam,
        'w_o_attn': nrm((L, ATTN_WIDTH, D_MODEL), ATTN_WIDTH ** -0.5),
        'w_o_conv': nrm((L, CONV_WIDTH, D_MODEL), CONV_WIDTH ** -0.5),
        'w_o_lru': nrm((L, LRU_WIDTH, D_MODEL), LRU_WIDTH ** -0.5),
        'w_out': nrm((L, D_MODEL, D_MODEL), D_MODEL ** -0.5),
        'ffn_w_up': nrm((L, D_MODEL, 2 * FFN_HIDDEN), D_MODEL ** -0.5),
        'ffn_w_down': nrm((L, FFN_HIDDEN, D_MODEL), FFN_HIDDEN ** -0.5),
        'final_norm_g': gain((D_MODEL,)),
    }


def reference(x, c, ctx, c_ctx, mod_w, mod_b, norm1_g, norm2_g, w_in, attn_sink, conv_dw_w, conv_dw_b,
              conv_ln_g, conv_ln_b, lru_conv_w, lru_conv_b, lru_wa, lru_ba, lru_wx, lru_bx, lru_lam,
              w_o_attn, w_o_conv, w_o_lru, w_out, ffn_w_up, ffn_w_down, final_norm_g):
    rows = x.shape[1] // GRID_W
    cos, sin = axial_rope(rows, x.dtype)
    silu_c = jax.nn.silu(c)
    silu_cc = jax.nn.silu(c_ctx)
    h_lat, h_ctx = x, ctx
    for l in range(DEPTH):
        need_ctx = l < DEPTH - 1
        sh1, sc1, g1, sh2, sc2, g2 = jnp.split((silu_c @ mod_w[l] + mod_b[l])[:, None, :], 6, axis=-1)
        csh1, csc1, cg1, csh2, csc2, cg2 = jnp.split((silu_cc @ mod_w[l] + mod_b[l])[None, None, :], 6, axis=-1)
        a_lat = modulate(rms_norm(h_lat, norm1_g[l]), sh1, sc1)
        a_ctx = modulate(rms_norm(h_ctx, norm1_g[l]), csh1, csc1)
        m_lat, m_ctx = token_mixers(a_lat, a_ctx, cos, sin, need_ctx, w_in[l], attn_sink[l], conv_dw_w[l],
                                    conv_dw_b[l], conv_ln_g[l], conv_ln_b[l], lru_conv_w[l], lru_conv_b[l],
                                    lru_wa[l], lru_ba[l], lru_wx[l], lru_bx[l], lru_lam[l],
                                    w_o_attn[l], w_o_conv[l], w_o_lru[l], w_out[l])
        h_lat = h_lat + g1 * m_lat
        h_lat = h_lat + g2 * swiglu(modulate(rms_norm(h_lat, norm2_g[l]), sh2, sc2), ffn_w_up[l], ffn_w_down[l])
        if need_ctx:
            h_ctx = h_ctx + cg1 * m_ctx
            h_ctx = h_ctx + cg2 * swiglu(modulate(rms_norm(h_ctx, norm2_g[l]), csh2, csc2),
                                         ffn_w_up[l], ffn_w_down[l])
    return rms_norm(h_lat, final_norm_g)
```

```python
import contextlib
import numpy as np
import concourse.bass as bass
import concourse.mybir as mybir
from concourse.bass_utils import run_bass_kernel_spmd

F32 = mybir.dt.float32
BF16 = mybir.dt.bfloat16
AF = mybir.ActivationFunctionType
ALU = mybir.AluOpType
AX = mybir.AxisListType

ENGS = ['pe', 'act', 'dve', 'pool', 'sp']
NDMA = 4

S = 8192
C = 256
NT = S + C
D = 1024
L = 2
FH = 2816
EPS = 1e-6
TILES = [(0, 256)] + [(256 + 512 * i, 512) for i in range(16)]
NBLK = NT // 128


class Prog:
    def __init__(self, nc):
        self.nc = nc
        self.thunks = {e: [] for e in ENGS}
        self.count = {e: 0 for e in ENGS}
        self.waited = {e: {} for e in ENGS}
        self.res_w = {}
        self.res_r = {}
        self.dma_rr = {e: 0 for e in ENGS}
        self.dma_cnt = {}
        self.sem = {}
        self.semnames = list(ENGS[:4]) + ['pe_1', 'pe_2', 'pe_3', 'pe_4', 'pe_5']
        self.engsem = {e: e for e in ENGS}
        self.nspare = 0
        for q in ('sp', 'pool'):
            for i in range(NDMA):
                self.semnames.append('d_%s_%d' % (q, i))

    def alloc_sems(self, stack):
        for s in self.semnames:
            self.sem[s] = stack.enter_context(self.nc.semaphore(s))

    def _collect(self, eng, reads, writes, extra=()):
        waits = {}

        def need(ev):
            if ev is None:
                return
            sk, val, src = ev
            if src == eng and eng == 'pe':
                return
            if self.waited[eng].get(sk, 0) >= val:
                return
            if waits.get(sk, 0) < val:
                waits[sk] = val
        for k in reads:
            need(self.res_w.get(k))
            if isinstance(k, str) and k.startswith('ps'):
                for ev in self.res_r.get(k, {}).values():
                    if ev[2] != eng:
                        need(ev)
        for k in writes:
            need(self.res_w.get(k))
            for ev in self.res_r.get(k, {}).values():
                if ev[2] == eng and eng != 'sp':
                    continue
                need(ev)
        for ev in extra:
            need(ev)
        for sk, val in waits.items():
            self.waited[eng][sk] = val
        return list(waits.items())

    def _record(self, ev, reads, writes):
        for k in reads:
            d = self.res_r.setdefault(k, {})
            old = d.get(ev[0])
            if old is None or old[1] < ev[1]:
                d[ev[0]] = ev
        for k in writes:
            self.res_w[k] = ev
            self.res_r[k] = {}

    def op(self, eng, fn, reads=(), writes=()):
        wl = self._collect(eng, reads, writes)
        self.count[eng] += 1
        mysem = self.engsem[eng]
        ev = (mysem, self.count[eng], eng)
        sem = self.sem

        def thunk(e):
            for sk, val in wl:
                e.wait_ge(sem[sk], val)
            fn(e).then_inc(sem[mysem], 1)
        self.thunks[eng].append(thunk)
        self._record(ev, reads, writes)

    def dma(self, q, out, in_, reads=(), writes=(), **kw):
        i = self.dma_rr[q] % NDMA
        self.dma_rr[q] += 1
        sk = 'd_%s_%d' % (q, i)
        n = self.dma_cnt.get(sk, 0) + 1
        self.dma_cnt[sk] = n
        extra = [(sk, 16 * (n - 1), 'dma')] if n > 1 else []
        wl = self._collect(q, reads, writes, extra)
        ev = (sk, 16 * n, 'dma')
        sem = self.sem

        def thunk(e):
            for s, val in wl:
                e.wait_ge(sem[s], val)
            e.dma_start(out=out, in_=in_, **kw).then_inc(sem[sk], 16)
        self.thunks[q].append(thunk)
        self._record(ev, reads, writes)

    def barrier(self, final=False):
        tot = []
        for e in ENGS[:4]:
            if self.count[e]:
                tot.append((self.engsem[e], self.count[e], e))
        for sk, n in self.dma_cnt.items():
            tot.append((sk, 16 * n, 'dma'))
        sem = self.sem
        for eng in (['sp'] if final else ENGS):
            wl = []
            for sk, val, src in tot:
                if src == eng:
                    continue
                if self.waited[eng].get(sk, 0) >= val:
                    continue
                self.waited[eng][sk] = val
                wl.append((sk, val))

            def thunk(e, wl=wl):
                for s, val in wl:
                    e.wait_ge(sem[s], val)
            self.thunks[eng].append(thunk)
        self.res_w.clear()
        self.res_r.clear()
        if not final and self.count['pe'] > 12000 and self.nspare < 5:
            self.nspare += 1
            self.engsem['pe'] = 'pe_%d' % self.nspare
            self.count['pe'] = 0

    def flush(self):
        nc = self.nc
        th = self.thunks
        with nc.Block() as block:
            @block.sync
            def _(e):
                for t in th['sp']:
                    t(e)

            @block.tensor
            def _(e):
                for t in th['pe']:
                    t(e)

            @block.scalar
            def _(e):
                for t in th['act']:
                    t(e)

            @block.vector
            def _(e):
                for t in th['dve']:
                    t(e)

            @block.gpsimd
            def _(e):
                for t in th['pool']:
                    t(e)
        self.thunks = {e: [] for e in ENGS}


class K:
    pass


def build(stop=None, debug=False):
    nc = bass.Bass("TRN2", target_bir_lowering=False)
    k = K()
    k.nc = nc

    def din(name, shape):
        return nc.dram_tensor(name, list(shape), F32, kind="ExternalInput").ap()

    def dscr(name, shape, dt):
        if debug:
            return nc.dram_tensor(name, list(shape), dt, kind="ExternalOutput").ap()
        return nc.dram_tensor(name, list(shape), dt).ap()
    I = {}
    I['x'] = din('x', [S, D])
    I['ctx'] = din('ctx', [C, D])
    I['cvec'] = din('cvec', [2, D])
    I['mod_w'] = din('mod_w', [L, D, 6 * D])
    I['mod_b'] = din('mod_b', [L, 6 * D])
    I['norm1_g'] = din('norm1_g', [L, D])
    I['norm2_g'] = din('norm2_g', [L, D])
    I['w_in'] = din('w_in', [L, D, 5888])
    I['attn_sink'] = din('attn_sink', [L, 8])
    I['conv_dw_w'] = din('conv_dw_w', [L, 31, 512])
    I['conv_dw_b'] = din('conv_dw_b', [L, 512])
    I['conv_ln_g'] = din('conv_ln_g', [L, 512])
    I['conv_ln_b'] = din('conv_ln_b', [L, 512])
    I['lru_conv_w'] = din('lru_conv_w', [L, 4, 512])
    I['lru_conv_b'] = din('lru_conv_b', [L, 512])
    I['lru_wa'] = din('lru_wa', [L, 2, 8, 64, 64])
    I['lru_ba'] = din('lru_ba', [L, 2, 512])
    I['lru_wx'] = din('lru_wx', [L, 2, 8, 64, 64])
    I['lru_bx'] = din('lru_bx', [L, 2, 512])
    I['lru_lam'] = din('lru_lam', [L, 2, 512])
    I['w_o_attn'] = din('w_o_attn', [L, 512, D])
    I['w_o_conv'] = din('w_o_conv', [L, 512, D])
    I['w_o_lru'] = din('w_o_lru', [L, 512, D])
    I['w_out'] = din('w_out', [L, D, D])
    I['ffn_w_up'] = din('ffn_w_up', [L, D, 2 * FH])
    I['ffn_w_down'] = din('ffn_w_down', [L, FH, D])
    I['final_norm_g'] = din('final_norm_g', [D])
    I['cos2'] = din('cos2', [128, S])
    I['sin2'] = din('sin2', [128, S])
    I['rotm'] = din('rotm', [128, 128])
    I['ident'] = din('ident', [128, 128])
    I['nm_prev'] = din('nm_prev', [128, 512])
    I['nm_next'] = din('nm_next', [128, 512])
    k.I = I
    k.out = nc.dram_tensor('out', [S, D], F32, kind="ExternalOutput").ap()
    k.hT = dscr('hT', [128, 8, NT], F32)
    k.aT = dscr('aT', [128, 8, NT], BF16)
    k.qT = dscr('qT', [128, 4, NT], BF16)
    k.kT = dscr('kT', [128, NT], BF16)
    k.V = dscr('Vtok', [NT, 128], BF16)
    k.hcT = dscr('hcT', [128, 4, NT], BF16)
    k.lxT = dscr('lxT', [128, 4, NT], BF16)
    k.GT = dscr('GT', [128, 4, NT], BF16)
    k.yaT = dscr('yaT', [128, 4, NT], BF16)
    k.ycT = dscr('ycT', [128, 4, NT], BF16)
    k.ylT = dscr('ylT', [128, 4, NT], BF16)

    P = Prog(nc)
    k.P = P
    with contextlib.ExitStack() as top:
        P.alloc_sems(top)
        sbt = lambda name, shape, dt: top.enter_context(nc.sbuf_tensor(name, list(shape), dt))
        k.ps = [top.enter_context(nc.psum_tensor('ps%d' % i, [128, 512], F32)) for i in range(8)]
        k.identF = sbt('identF', [128, 128], F32)
        k.identB = sbt('identB', [128, 128], BF16)
        k.onesB = sbt('onesB', [128, 128], BF16)
        k.rotB = sbt('rotB', [128, 128], BF16)
        k.nmp = sbt('nmp', [128, 512], BF16)
        k.nmn = sbt('nmn', [128, 512], BF16)
        k.mods = sbt('mods', [128, 2, 6, 8], F32)
        k.A1 = sbt('A1', [128, 2, 8], F32)
        k.A2 = sbt('A2', [128, 2, 8], F32)
        P.dma('sp', k.identF[:], I['ident'], writes=['identF'])
        P.dma('pool', k.identB[:], I['ident'], writes=['identB'])
        P.dma('pool', k.rotB[:], I['rotm'], writes=['rotB'])
        P.dma('pool', k.nmp[:], I['nm_prev'], writes=['nmp'])
        P.dma('pool', k.nmn[:], I['nm_next'], writes=['nmn'])
        P.op('dve', lambda e: e.memset(k.onesB[:], 1.0), writes=['onesB'])
        k.epsT = sbt('epsT', [128, 1], F32)
        k.oneT = sbt('oneT', [128, 1], F32)
        P.op('dve', lambda e: e.memset(k.epsT[:], EPS), writes=['epsT'])
        P.op('dve', lambda e: e.memset(k.oneT[:], 1.0), writes=['oneT'])
        P.barrier()
        P.flush()
        plist = [('in', lambda: phase_in(k))]
        for l in range(L):
            plist += [('mod%d' % l, lambda l=l: phase_mod(k, l)), ('a%d' % l, lambda l=l: phase_a(k, l)),
                      ('conv%d' % l, lambda l=l: phase_conv(k, l)),
                      ('att%d' % l, lambda l=l: phase_attlru(k, l)), ('merge%d' % l, lambda l=l: phase_merge(k, l)),
                      ('ffn%d' % l, lambda l=l: phase_ffn(k, l))]
        plist.append(('out', lambda: phase_out(k)))
        for name, fn in plist:
            fn()
            if stop == name:
                break
        if stop is not None and stop != 'out':
            with Ph(k, 'pfin') as ph:
                if debug:
                    dbg = ph.sb('dbg', [128, 2 * 6 * 8 + 32], F32)
                    k.dbgout = nc.dram_tensor('dbgout', [128, 128], F32, kind="ExternalOutput").ap()
                    P.op('dve', lambda e: e.tensor_copy(out=dbg[:, 0:96], in_=k.mods[:, :, :, :].rearrange("p a b c -> p (a b c)")), reads=['mods'], writes=['dbg'])
                    P.op('dve', lambda e: e.tensor_copy(out=dbg[:, 96:112], in_=k.A1[:, :, :].rearrange("p a b -> p (a b)")), reads=['A1', 'dbg'], writes=['dbg'])
                    P.op('dve', lambda e: e.tensor_copy(out=dbg[:, 112:128], in_=k.A2[:, :, :].rearrange("p a b -> p (a b)")), reads=['A2', 'dbg'], writes=['dbg'])
                    P.dma('sp', k.dbgout, dbg[:], reads=['dbg'])
    return nc


class Ph:
    def __init__(self, k, name):
        self.k = k
        self.name = name
        self.st = contextlib.ExitStack()
        self.n = 0

    def __enter__(self):
        self.st.__enter__()
        return self

    def sb(self, name, shape, dt):
        self.n += 1
        return self.st.enter_context(self.k.nc.sbuf_tensor('%s_%s' % (self.name, name), list(shape), dt))

    def __exit__(self, *a):
        self.k.P.barrier()
        self.k.P.flush()
        return self.st.__exit__(*a)


def load_vec_fm(k, ph, src_rows_ap, n, dst_ap, tag, bank=7):
    P = k.P
    stg = ph.sb('vst_' + tag, [128, 128], F32)
    key = 'vst_' + tag
    P.dma('sp', stg[0:n, :], src_rows_ap, writes=[key])
    psb = k.ps[bank]
    pk = 'ps%d' % bank
    P.op('pe', lambda e: e.transpose(out=psb[:, 0:n], in_=stg[0:n, :], identity=k.identF[0:n, 0:n]),
         reads=[key, 'identF'], writes=[pk])
    P.op('dve', lambda e: e.tensor_copy(out=dst_ap, in_=psb[:, 0:n]), reads=[pk], writes=[tag])


def norm_rstd(k, ph, hT_t, hkey, n, sq, sqkey, rstd, rkey, tmp, tkey, bank=7):
    P = k.P
    psb = k.ps[bank]
    pk = 'ps%d' % bank
    P.op('act', lambda e: e.activation(out=sq[:, :, 0:n], in_=hT_t[:, :, 0:n], func=AF.Square), reads=[hkey], writes=[sqkey])
    for c in range(8):
        P.op('pe', lambda e, c=c: e.matmul(psb[:, 0:n], lhsT=k.onesB[:], rhs=sq[:, c, 0:n], start=(c == 0), stop=(c == 7)),
             reads=[sqkey, 'onesB'], writes=[pk])
    P.op('act', lambda e: e.activation(out=tmp[:, 0:n], in_=psb[:, 0:n], func=AF.Sqrt, scale=1.0 / D, bias=k.epsT[:, 0:1]),
         reads=[pk, 'epsT'], writes=[tkey])
    P.op('dve', lambda e: e.reciprocal(out=rstd[:, 0:n], in_=tmp[:, 0:n]), reads=[tkey], writes=[rkey])


def wload(k, dst, src, key):
    k.P.dma('pool', dst, src, writes=[key])


def phase_in(k):
    P = k.P
    with Ph(k, 'pin') as ph:
        xin = [ph.sb('xin%d' % i, [128, D], F32) for i in range(2)]
        hst = [ph.sb('hst%d' % i, [128, 8, 128], F32) for i in range(2)]
        def load(s):
            src = k.I['ctx'][s * 128:(s + 1) * 128, :] if s < 2 else k.I['x'][(s - 2) * 128:(s - 1) * 128, :]
            P.dma('sp', xin[s % 2][:], src, writes=['xin%d' % (s % 2)])
        load(0)
        for s in range(NBLK):
            par = s % 2
            xk, hk = 'xin%d' % par, 'hst%d' % par
            if s + 1 < NBLK:
                load(s + 1)
            for half in range(2):
                b = 2 * par + half
                pk = 'ps%d' % b
                for j in range(4):
                    c = 4 * half + j
                    P.op('pe', lambda e, b=b, j=j, c=c, par=par: e.transpose(out=k.ps[b][:, j * 128:(j + 1) * 128],
                                                                             in_=xin[par][:, c * 128:(c + 1) * 128], identity=k.identF[:]),
                         reads=[xk, 'identF'], writes=[pk])
                eng = 'act' if half == 0 else 'dve'
                if eng == 'act':
                    P.op('act', lambda e, b=b, half=half, par=par: e.activation(
                        out=hst[par][:, 4 * half:4 * half + 4, :], in_=k.ps[b][:, :].rearrange("p (j t) -> p j t", t=128), func=AF.Copy),
                        reads=[pk], writes=[hk])
                else:
                    P.op('dve', lambda e, b=b, half=half, par=par: e.tensor_copy(
                        out=hst[par][:, 4 * half:4 * half + 4, :], in_=k.ps[b][:, :].rearrange("p (j t) -> p j t", t=128)),
                        reads=[pk], writes=[hk])
            P.dma('sp', k.hT[:, :, s * 128:(s + 1) * 128], hst[par][:], reads=[hk], writes=[('hT', s)])


def phase_mod(k, l):
    P = k.P
    I = k.I
    with Ph(k, 'pmod%d' % l) as ph:
        cT = ph.sb('cT', [128, 16], F32)
        load_vec_fm(k, ph, I['cvec'].rearrange("r (c p) -> (r c) p", p=128), 16, cT[:], 'cT')
        scT = ph.sb('scT', [128, 16], F32)
        P.op('act', lambda e: e.activation(out=scT[:], in_=cT[:], func=AF.Silu), reads=['cT'], writes=['scT'])
        modb = ph.sb('modb', [128, 48], F32)
        load_vec_fm(k, ph, I['mod_b'][l].rearrange("(c p) -> c p", p=128), 48, modb[:], 'modb')
        g1 = ph.sb('g1n', [128, 8], F32)
        g2 = ph.sb('g2n', [128, 8], F32)
        load_vec_fm(k, ph, I['norm1_g'][l].rearrange("(c p) -> c p", p=128), 8, g1[:], 'g1n')
        load_vec_fm(k, ph, I['norm2_g'][l].rearrange("(c p) -> c p", p=128), 8, g2[:], 'g2n')
        wm = [ph.sb('wm%d' % i, [128, 8, 1024], F32) for i in range(2)]
        psb = k.ps[0]
        import os
        LVL = int(os.environ.get('MODLVL', '9'))
        if LVL < 1:
            return
        for piece in range(6):
            par = piece % 2
            wk = 'wm%d' % par
            for kc in range(8):
                P.dma('sp', wm[par][:, kc, :], I['mod_w'][l, kc * 128:(kc + 1) * 128, piece * 1024:(piece + 1) * 1024], writes=[wk])
            if LVL < 2:
                continue
            for oc in range(8):
                col = (piece * 8 + oc) * 2
                for kc in range(8):
                    P.op('pe', lambda e, par=par, oc=oc, kc=kc, col=col: e.matmul(
                        psb[:, col:col + 2], lhsT=wm[par][:, kc, oc * 128:(oc + 1) * 128],
                        rhs=scT[:, :].rearrange("p (r c) -> p c r", c=8)[:, kc, :], start=(kc == 0), stop=(kc == 7)),
                        reads=[wk, 'scT'], writes=['ps0'])
        if LVL < 3:
            return
        for r in range(2):
            P.op('dve', lambda e, r=r: e.tensor_tensor(
                out=k.mods[:, r, :, :].rearrange("p a b -> p (a b)"),
                in0=psb[:, 0:96].rearrange("p (o r) -> p r o", r=2)[:, r, :], in1=modb[:], op=ALU.add),
                reads=['ps0', 'modb'], writes=['mods'])
            if LVL < 4:
                continue
            P.op('dve', lambda e, r=r: e.scalar_tensor_tensor(out=k.A1[:, r, :], in0=k.mods[:, r, 1, :], scalar=1.0, in1=g1[:],
                                                              op0=ALU.add, op1=ALU.mult), reads=['mods', 'g1n'], writes=['A1'])
            P.op('dve', lambda e, r=r: e.scalar_tensor_tensor(out=k.A2[:, r, :], in0=k.mods[:, r, 4, :], scalar=1.0, in1=g2[:],
                                                              op0=ALU.add, op1=ALU.mult), reads=['mods', 'g2n'], writes=['A2'])


def phase_a(k, l):
    P = k.P
    I = k.I
    W = I['w_in'][l]
    with Ph(k, 'pa%d' % l) as ph:
        wq = ph.sb('wq', [128, 8, 512], BF16)
        wk_ = ph.sb('wk', [128, 8, 128], BF16)
        wv = ph.sb('wv', [128, 8, 128], BF16)
        wcols = ph.sb('wcols', [128, 8, 2048], BF16)
        Wr = W.rearrange("(kc p) n -> p kc n", p=128)
        for c in range(4):
            wload(k, wq[:, :, c * 128:c * 128 + 64], Wr[:, :, c * 64:(c + 1) * 64], 'wq')
            wload(k, wq[:, :, c * 128 + 64:(c + 1) * 128], Wr[:, :, (c + 4) * 64:(c + 5) * 64], 'wq')
        wload(k, wk_[:], Wr[:, :, 512:640], 'wk')
        wload(k, wv[:], Wr[:, :, 640:768], 'wv')
        for kc in range(8):
            wload(k, wcols[:, kc, :], W[kc * 128:(kc + 1) * 128, 768:2816], 'wcols')
        hT_t = [ph.sb('hT%d' % i, [128, 8, 512], F32) for i in range(2)]
        sq = ph.sb('sq', [128, 8, 512], BF16)
        tmpn = ph.sb('tmpn', [128, 512], F32)
        rstd = ph.sb('rstd', [128, 512], F32)
        hn = ph.sb('hn', [128, 8, 512], F32)
        aT_t = [ph.sb('aT%d' % i, [128, 8, 512], BF16) for i in range(2)]
        cs = [ph.sb('cs%d' % i, [128, 2, 512], F32) for i in range(2)]
        qb = [ph.sb('qb%d' % i, [128, 512], BF16) for i in range(2)]
        t1 = [ph.sb('t1%d' % i, [128, 512], F32) for i in range(2)]
        t2 = [ph.sb('t2%d' % i, [128, 512], F32) for i in range(2)]
        qo = [ph.sb('qo%d' % i, [128, 5, 512], BF16) for i in range(2)]
        vo = [ph.sb('vo%d' % i, [128, 4, 128], BF16) for i in range(2)]
        sg = [ph.sb('sg%d' % i, [128, 512], F32) for i in range(2)]
        oc4 = [ph.sb('oc4%d' % i, [128, 4, 512], BF16) for i in range(3)]
        st = {'cnt': 0}

        def load(ti):
            t0, n = TILES[ti]
            par = ti % 2
            P.dma('sp', hT_t[par][:, :, 0:n], k.hT[:, :, t0:t0 + n], reads=[('hT', b) for b in range(t0 // 128, (t0 + n) // 128)], writes=['hT%d' % par])
            if ti > 0:
                P.dma('sp', cs[par][:, 0, 0:n], I['cos2'][:, t0 - C:t0 - C + n], writes=['cs%d' % par])
                P.dma('sp', cs[par][:, 1, 0:n], I['sin2'][:, t0 - C:t0 - C + n], writes=['cs%d' % par])

        def part1(ti):
            t0, n = TILES[ti]
            par = ti % 2
            norm_rstd(k, ph, hT_t[par], 'hT%d' % par, n, sq, 'sq', rstd, 'rstd', tmpn, 'tmpn')

        def part2(ti):
            t0, n = TILES[ti]
            par = ti % 2
            r = 1 if ti == 0 else 0
            hk, ak = 'hT%d' % par, 'aT%d' % par
            P.op('dve', lambda e: e.tensor_tensor(out=hn[:, :, 0:n], in0=hT_t[par][:, :, 0:n],
                                                   in1=rstd[:, 0:n].unsqueeze(1).to_broadcast([128, 8, n]), op=ALU.mult),
                 reads=[hk, 'rstd'], writes=['hn'])
            for c in range(8):
                P.op('act', lambda e, c=c: e.activation(out=aT_t[par][:, c, 0:n], in_=hn[:, c, 0:n], func=AF.Identity,
                                                        scale=k.A1[:, r, c:c + 1], bias=k.mods[:, r, 0, c:c + 1]),
                     reads=['hn', 'A1', 'mods'], writes=[ak])
            P.dma('sp', k.aT[:, :, t0:t0 + n], aT_t[par][:, :, 0:n], reads=[ak], writes=[('aT', ti)])

        def proj(wsl, wkey, par, n, ak):
            b = st['cnt'] % 4
            st['cnt'] += 1
            pk = 'ps%d' % b
            for kc in range(8):
                P.op('pe', lambda e, kc=kc: e.matmul(k.ps[b][:, 0:n], lhsT=wsl(kc), rhs=aT_t[par][:, kc, 0:n], start=(kc == 0), stop=(kc == 7)),
                     reads=[wkey, ak], writes=[pk])
            return b, pk

        def rope_tail(c, b, pk, p2, par, n):
            b2 = 4 + p2
            pk2 = 'ps%d' % b2
            P.op('pe', lambda e: e.matmul(k.ps[b2][:, 0:n], lhsT=k.rotB[:], rhs=qb[p2][:, 0:n], start=True, stop=True),
                 reads=['rotB', 'qb%d' % p2], writes=[pk2])
            P.op('dve', lambda e: e.tensor_tensor(out=t1[p2][:, 0:n], in0=k.ps[b][:, 0:n], in1=cs[par][:, 0, 0:n], op=ALU.mult),
                 reads=[pk, 'cs%d' % par, 'qb%d' % p2], writes=['t1%d' % p2])
            P.op('dve', lambda e: e.tensor_tensor(out=t2[p2][:, 0:n], in0=k.ps[b2][:, 0:n], in1=cs[par][:, 1, 0:n], op=ALU.mult),
                 reads=[pk2, 'cs%d' % par], writes=['t2%d' % p2])
            P.op('pool', lambda e: e.tensor_tensor(out=qo[par][:, c, 0:n], in0=t1[p2][:, 0:n], in1=t2[p2][:, 0:n], op=ALU.add),
                 reads=['t1%d' % p2, 't2%d' % p2], writes=['qo%d' % par])

        def group(ti, grp):
            t0, n = TILES[ti]
            par = ti % 2
            ak = 'aT%d' % par
            ob = oc4[grp]
            okey = 'oc4%d' % grp
            for c in range(4):
                if grp == 0:
                    b, pk = proj(lambda kc, c=c: wcols[:, kc, c * 128:(c + 1) * 128], 'wcols', par, n, ak)
                    b2 = 4 + (st['cnt'] % 2)
                    pk2 = 'ps%d' % b2
                    s2 = st['cnt'] % 2
                    for kc in range(8):
                        P.op('pe', lambda e, kc=kc, c=c, b2=b2: e.matmul(k.ps[b2][:, 0:n], lhsT=wcols[:, kc, 512 + c * 128:512 + (c + 1) * 128],
                                                                        rhs=aT_t[par][:, kc, 0:n], start=(kc == 0), stop=(kc == 7)),
                             reads=['wcols', ak], writes=[pk2])
                    P.op('act', lambda e, b2=b2, s2=s2: e.activation(out=sg[s2][:, 0:n], in_=k.ps[b2][:, 0:n], func=AF.Sigmoid),
                         reads=[pk2], writes=['sg%d' % s2])
                    P.op('dve', lambda e, b=b, s2=s2, c=c: e.tensor_tensor(out=ob[:, c, 0:n], in0=k.ps[b][:, 0:n], in1=sg[s2][:, 0:n], op=ALU.mult),
                         reads=[pk, 'sg%d' % s2], writes=[okey])
                else:
                    off = 1024 if grp == 1 else 1536
                    b, pk = proj(lambda kc, c=c, off=off: wcols[:, kc, off + c * 128:off + (c + 1) * 128], 'wcols', par, n, ak)
                    fn = AF.Copy if grp == 1 else AF.Gelu_apprx_tanh
                    P.op('act', lambda e, b=b, c=c, fn=fn: e.activation(out=ob[:, c, 0:n], in_=k.ps[b][:, 0:n], func=fn),
                         reads=[pk], writes=[okey])
            dst = (k.hcT, k.lxT, k.GT)[grp]
            P.dma('sp', dst[:, :, t0:t0 + n], ob[:, :, 0:n], reads=[okey], writes=[('hcT', 'lxT', 'GT')[grp]])

        load(0)
        part1(0)
        part2(0)
        for ti, (t0, n) in enumerate(TILES):
            par = ti % 2
            r = 1 if ti == 0 else 0
            ak = 'aT%d' % par
            nxt = ti + 1 if ti + 1 < len(TILES) else None
            if nxt is not None:
                load(nxt)
            pend = None
            for c in range(5):
                wsl = (lambda kc, c=c: wq[:, kc, c * 128:(c + 1) * 128]) if c < 4 else (lambda kc: wk_[:, kc, :])
                b, pk = proj(wsl, 'wq' if c < 4 else 'wk', par, n, ak)
                if r == 1:
                    P.op('act', lambda e, b=b, c=c, par=par, n=n: e.activation(out=qo[par][:, c, 0:n], in_=k.ps[b][:, 0:n], func=AF.Copy),
                         reads=[pk], writes=['qo%d' % par])
                else:
                    p2 = c % 2
                    P.op('act', lambda e, b=b, p2=p2, n=n: e.activation(out=qb[p2][:, 0:n], in_=k.ps[b][:, 0:n], func=AF.Copy),
                         reads=[pk], writes=['qb%d' % p2])
                    if pend is not None:
                        rope_tail(*pend)
                    pend = (c, b, pk, p2, par, n)
            if pend is not None:
                rope_tail(*pend)
            P.dma('sp', k.qT[:, :, t0:t0 + n], qo[par][:, 0:4, 0:n], reads=['qo%d' % par], writes=['qT'])
            P.dma('sp', k.kT[:, t0:t0 + n], qo[par][:, 4, 0:n], reads=['qo%d' % par], writes=['kT'])
            for sblk in range(n // 128):
                b = st['cnt'] % 4
                st['cnt'] += 1
                pk = 'ps%d' % b
                for kc in range(8):
                    P.op('pe', lambda e, b=b, kc=kc, sblk=sblk, par=par: e.matmul(k.ps[b][:, 0:128], lhsT=aT_t[par][:, kc, sblk * 128:(sblk + 1) * 128],
                                                                        rhs=wv[:, kc, :], start=(kc == 0), stop=(kc == 7)),
                         reads=['wv', ak], writes=[pk])
                P.op('dve', lambda e, b=b, sblk=sblk, par=par: e.tensor_copy(out=vo[par][:, sblk, :], in_=k.ps[b][:, 0:128]),
                     reads=[pk], writes=['vo%d' % par])
            P.dma('sp', k.V[t0:t0 + n, :].rearrange("(s p) d -> p s d", p=128), vo[par][:, 0:n // 128, :], reads=['vo%d' % par], writes=['V'])
            group(ti, 0)
            if nxt is not None:
                part1(nxt)
            group(ti, 1)
            if nxt is not None:
                part2(nxt)
            group(ti, 2)


def phase_lru(k, l):
    P = k.P
    I = k.I
    with Ph(k, 'plru%d' % l) as ph:
        cw = ph.sb('cw', [128, 16], F32)
        cb = ph.sb('cb', [128, 4], F32)
        ba = ph.sb('ba', [128, 8], F32)
        bx = ph.sb('bx', [128, 8], F32)
        lam = ph.sb('lam', [128, 8], F32)
        c1 = ph.sb('c1', [128, 8], F32)
        load_vec_fm(k, ph, I['lru_conv_w'][l].rearrange("t (c p) -> (t c) p", p=128), 16, cw[:], 'cw')
        load_vec_fm(k, ph, I['lru_conv_b'][l].rearrange("(c p) -> c p", p=128), 4, cb[:], 'cb')
        load_vec_fm(k, ph, I['lru_ba'][l].rearrange("d (c p) -> (d c) p", p=128), 8, ba[:], 'ba')
        load_vec_fm(k, ph, I['lru_bx'][l].rearrange("d (c p) -> (d c) p", p=128), 8, bx[:], 'bx')
        load_vec_fm(k, ph, I['lru_lam'][l].rearrange("d (c p) -> (d c) p", p=128), 8, lam[:], 'lam')
        P.op('act', lambda e: e.activation(out=c1[:], in_=lam[:], func=AF.Exp, scale=-1.0), reads=['lam'], writes=['c1'])
        P.op('dve', lambda e: e.tensor_scalar(out=c1[:], in0=c1[:], scalar1=1.0, scalar2=None, op0=ALU.add), reads=['c1'], writes=['c1'])
        P.op('act', lambda e: e.activation(out=c1[:], in_=c1[:], func=AF.Ln), reads=['c1'], writes=['c1'])
        P.op('dve', lambda e: e.tensor_scalar(out=c1[:], in0=c1[:], scalar1=-4.0, scalar2=None, op0=ALU.mult), reads=['c1'], writes=['c1'])
        for t_, nm in ((cb, 'cb'), (ba, 'ba'), (bx, 'bx')):
            P.op('dve', lambda e, t_=t_: e.tensor_scalar(out=t_[:], in0=t_[:], scalar1=0.5, scalar2=None, op0=ALU.mult), reads=[nm], writes=[nm])
        bd = ph.sb('bd', [128, 16, 128], BF16)
        P.op('pool', lambda e: e.memset(bd[:], 0.0), writes=['bd'])
        for d in range(2):
            for gi, wn in enumerate(('lru_wa', 'lru_wx')):
                for ch in range(4):
                    idx = (d * 2 + gi) * 4 + ch
                    for hb in range(2):
                        wload(k, bd[hb * 64:(hb + 1) * 64, idx, hb * 64:(hb + 1) * 64], I[wn][l, d, ch * 2 + hb], 'bd')
        dgl = ph.sb('dgl', [128, 16, 128], BF16)
        P.op('dve', lambda e: e.tensor_tensor(out=dgl[:, :, :], in0=k.identF[:, :].unsqueeze(1).to_broadcast([128, 16, 128]),
                                              in1=cw[:, :].unsqueeze(2).to_broadcast([128, 16, 128]), op=ALU.mult),
             reads=['identF', 'cw'], writes=['dgl'])
        xp = ph.sb('xp0', [128, NT + 8], BF16)
        Uh = ph.sb('Uh0', [128, NT], BF16)
        A = ph.sb('A', [128, NT], F32)
        B = ph.sb('B', [128, NT], F32)
        Hf = ph.sb('Hf', [128, NT], F32)
        tRs = [ph.sb('tR%d' % i, [128, 2048], F32) for i in range(2)]
        tIs = [ph.sb('tI%d' % i, [128, 2048], F32) for i in range(2)]
        Ts = [ph.sb('T%d' % i, [128, 2048], F32) for i in range(2)]
        gt = [ph.sb('gt%d' % i, [128, 2048], BF16) for i in range(2)]
        yo_ = ph.sb('yo0', [128, 2048], BF16)
        yo = [yo_, yo_]
        h0 = ph.sb('h0', [128, 1], F32)
        CO, LO = 2, 261
        xk, uk = 'xp0', 'Uh0'
        st = {'cnt': 0, 'g': 0, 'c2': 0, 'p': 0}
        spans = [(-1, 0, C)] + [(h, C + 2048 * h, 2048) for h in range(4)]

        def gates(ch, d, sid, s0, sn):
            gp = st['g'] % 2
            st['g'] += 1
            tR, tI, T = tRs[gp], tIs[gp], Ts[gp]
            kR, kI, kT_ = 'tR%d' % gp, 'tI%d' % gp, 'T%d' % gp
            for gi, (dstG, gk, bias) in enumerate(((tR, kR, ba), (tI, kI, bx))):
                idx = (d * 2 + gi) * 4 + ch
                for sub in range(0, sn, 512):
                    m = min(512, sn - sub)
                    b = st['cnt'] % 6
                    st['cnt'] += 1
                    pk = 'ps%d' % b
                    P.op('pe', lambda e, b=b, idx=idx, sub=sub, m=m: e.matmul(k.ps[b][:, 0:m], lhsT=bd[:, idx, :], rhs=Uh[:, s0 + sub:s0 + sub + m], start=True, stop=True),
                         reads=['bd', uk], writes=[pk])
                    P.op('act', lambda e, b=b, dstG=dstG, sub=sub, m=m, bias=bias: e.activation(out=dstG[:, sub:sub + m], in_=k.ps[b][:, 0:m], func=AF.Tanh,
                                                                                               bias=bias[:, d * 4 + ch:d * 4 + ch + 1]),
                         reads=[pk, 'ba', 'bx'], writes=[gk])
            P.op('act', lambda e: e.activation(out=A[:, s0:s0 + sn], in_=tR[:, 0:sn], func=AF.Exp, scale=c1[:, d * 4 + ch:d * 4 + ch + 1],
                                               bias=c1[:, d * 4 + ch:d * 4 + ch + 1]), reads=[kR, 'c1'], writes=[('A', sid)])
            P.op('dve', lambda e: e.tensor_tensor(out=T[:, 0:sn], in0=A[:, s0:s0 + sn], in1=A[:, s0:s0 + sn], op=ALU.mult), reads=[('A', sid)], writes=[kT_])
            P.op('act', lambda e: e.activation(out=T[:, 0:sn], in_=T[:, 0:sn], func=AF.Sqrt, scale=-1.0, bias=k.oneT[:, 0:1]), reads=[kT_, 'oneT'], writes=[kT_])
            P.op('dve', lambda e: e.scalar_tensor_tensor(out=T[:, 0:sn], in0=tI[:, 0:sn], scalar=1.0, in1=T[:, 0:sn], op0=ALU.add, op1=ALU.mult),
                 reads=[kT_, kI], writes=[kT_])
            P.op('pool', lambda e: e.tensor_tensor(out=B[:, s0:s0 + sn], in0=T[:, 0:sn], in1=Uh[:, s0:s0 + sn], op=ALU.mult), reads=[kT_, uk], writes=[('B', sid)])

        def combine(ch, sid, s0, sn):
            par = st['p'] % 2
            st['p'] += 1
            P.dma('sp', gt[par][:, 0:sn], k.GT[:, ch, s0:s0 + sn], reads=['GT'], writes=['gt%d' % par])
            P.op('pool', lambda e: e.tensor_tensor(out=Hf[:, s0:s0 + sn], in0=Hf[:, s0:s0 + sn], in1=B[:, s0:s0 + sn], op=ALU.add), reads=[('Hf', sid), ('B', sid)], writes=[('Hf', sid)])
            P.op('dve', lambda e: e.tensor_tensor(out=yo[par][:, 0:sn], in0=Hf[:, s0:s0 + sn], in1=gt[par][:, 0:sn], op=ALU.mult),
                 reads=[('Hf', sid), 'gt%d' % par], writes=['yo0'])
            P.dma('sp', k.ylT[:, ch, s0:s0 + sn], yo[par][:, 0:sn], reads=['yo0'], writes=['ylT'])

        for ch in range(4):
            P.op('pool', lambda e: e.memset(xp[:, 0:2], 0.0), writes=[xk])
            P.op('pool', lambda e: e.memset(xp[:, 258:261], 0.0), writes=[xk])
            P.op('pool', lambda e: e.memset(xp[:, LO + S:LO + S + 3], 0.0), writes=[xk])
            P.dma('sp', xp[:, CO:CO + C], k.lxT[:, ch, 0:C], reads=['lxT'], writes=[xk])
            P.dma('sp', xp[:, LO:LO + S], k.lxT[:, ch, C:NT], reads=['lxT'], writes=[xk])
            for (o, u0, n_) in ((CO, 0, C), (LO, C, S)):
                for t0 in range(0, n_, 512):
                    m = min(512, n_ - t0)
                    b = 6 + (st['c2'] % 2)
                    st['c2'] += 1
                    pk = 'ps%d' % b
                    for tap in range(4):
                        P.op('pe', lambda e, b=b, tap=tap, ch=ch, o=o, t0=t0, m=m: e.matmul(k.ps[b][:, 0:m], lhsT=dgl[:, tap * 4 + ch, :],
                                                                                           rhs=xp[:, o + t0 + tap - 2:o + t0 + tap - 2 + m], start=(tap == 0), stop=(tap == 3)),
                             reads=['dgl', xk], writes=[pk])
                    P.op('act', lambda e, b=b, u0=u0, t0=t0, m=m, ch=ch: e.activation(out=Uh[:, u0 + t0:u0 + t0 + m], in_=k.ps[b][:, 0:m], func=AF.Identity,
                                                                                     scale=0.5, bias=cb[:, ch:ch + 1]), reads=[pk, 'cb'], writes=[uk])
            gates(ch, 0, *spans[0])
            P.op('dve', lambda e: e.tensor_tensor_scan(out=Hf[:, 0:C], data0=A[:, 0:C], data1=B[:, 0:C], initial=0.0, op0=ALU.mult, op1=ALU.add),
                 reads=[('A', -1), ('B', -1)], writes=[('Hf', -1)])
            for h in range(4):
                sid, s0, sn = spans[1 + h]
                gates(ch, 0, sid, s0, sn)
                P.op('dve', lambda e, s0=s0: e.tensor_tensor_scan(out=Hf[:, s0:s0 + 2048], data0=A[:, s0:s0 + 2048], data1=B[:, s0:s0 + 2048], initial=Hf[:, s0 - 1:s0],
                                                                  op0=ALU.mult, op1=ALU.add), reads=[('A', sid), ('B', sid), ('Hf', sid - 1)], writes=[('Hf', sid)])
            gates(ch, 1, *spans[0])
            P.op('dve', lambda e: e.tensor_tensor_scan(out=B[:, 0:C][:, ::-1], data0=A[:, 0:C][:, ::-1], data1=B[:, 0:C][:, ::-1], initial=0.0, op0=ALU.mult, op1=ALU.add),
                 reads=[('A', -1), ('B', -1)], writes=[('B', -1)])
            P.op('dve', lambda e: e.tensor_copy(out=h0[:], in_=B[:, 0:1]), reads=[('B', -1)], writes=['h0'])
            combine(ch, *spans[0])
            for h in range(3, -1, -1):
                sid, s0, sn = spans[1 + h]
                gates(ch, 1, sid, s0, sn)
                init = h0[:, 0:1] if h == 3 else B[:, s0 + 2048:s0 + 2049]
                ik = ['h0'] if h == 3 else [('B', sid + 1)]
                P.op('dve', lambda e, s0=s0, init=init: e.tensor_tensor_scan(out=B[:, s0:s0 + 2048][:, ::-1], data0=A[:, s0:s0 + 2048][:, ::-1],
                                                                             data1=B[:, s0:s0 + 2048][:, ::-1], initial=init, op0=ALU.mult, op1=ALU.add),
                     reads=[('A', sid), ('B', sid)] + ik, writes=[('B', sid)])
                combine(ch, sid, s0, sn)


def phase_conv(k, l):
    P = k.P
    I = k.I
    need_ctx = l < L - 1
    with Ph(k, 'pcv%d' % l) as ph:
        dw = ph.sb('dw', [128, 124], F32)
        vb = ph.sb('vb', [128, 12], F32)
        load_vec_fm(k, ph, I['conv_dw_w'][l].rearrange("t (c p) -> (t c) p", p=128), 124, dw[:], 'dw')
        load_vec_fm(k, ph, I['conv_dw_b'][l].rearrange("(c p) -> c p", p=128), 4, vb[:, 0:4], 'vb0')
        load_vec_fm(k, ph, I['conv_ln_g'][l].rearrange("(c p) -> c p", p=128), 4, vb[:, 4:8], 'vb1')
        load_vec_fm(k, ph, I['conv_ln_b'][l].rearrange("(c p) -> c p", p=128), 4, vb[:, 8:12], 'vb2')
        vkeys = ['vb0', 'vb1', 'vb2']
        dg = ph.sb('dg', [128, 124, 128], BF16)
        for j0 in range(0, 124, 31):
            P.op('dve', lambda e, j0=j0: e.tensor_tensor(out=dg[:, j0:j0 + 31, :], in0=k.identF[:, :].unsqueeze(1).to_broadcast([128, 31, 128]),
                                                         in1=dw[:, j0:j0 + 31].unsqueeze(2).to_broadcast([128, 31, 128]), op=ALU.mult),
                 reads=['identF', 'dw'], writes=['dg'])
        hin = [ph.sb('hin%d' % i, [128, 4, 542], BF16) for i in range(2)]
        cxs = [ph.sb('cx%d' % i, [128, 4, 512], F32) for i in range(2)]
        xbs = [ph.sb('xb%d' % i, [128, 4, 512], BF16) for i in range(2)]
        xss = [ph.sb('xs%d' % i, [128, 4, 512], BF16) for i in range(2)]
        mean = ph.sb('mean', [128, 512], F32)
        var = ph.sb('var', [128, 512], F32)
        rs = ph.sb('rs', [128, 512], F32)
        yc = [ph.sb('yc%d' % i, [128, 4, 512], BF16) for i in range(2)]
        tiles = [ti for ti in range(len(TILES)) if not (ti == 0 and not need_ctx)]

        def load(ti):
            t0, n = TILES[ti]
            par = ti % 2
            hk = 'hin%d' % par
            lo_seq, hi_seq = (0, C) if ti == 0 else (C, NT)
            a0, a1 = max(t0 - 15, lo_seq), min(t0 + n + 15, hi_seq)
            P.op('pool', lambda e: e.memset(hin[par][:], 0.0), writes=[hk])
            P.dma('sp', hin[par][:, :, a0 - (t0 - 15):a1 - (t0 - 15)], k.hcT[:, :, a0:a1], reads=['hcT'], writes=[hk])

        def conv_mm(ti, chs):
            t0, n = TILES[ti]
            par = ti % 2
            hk = 'hin%d' % par
            cx, xb, xs = cxs[par], xbs[par], xss[par]
            ck, bk, sk = 'cx%d' % par, 'xb%d' % par, 'xs%d' % par
            for ch in chs:
                pk = 'ps%d' % ch
                for tap in range(31):
                    P.op('pe', lambda e, ch=ch, tap=tap: e.matmul(k.ps[ch][:, 0:n], lhsT=dg[:, tap * 4 + ch, :], rhs=hin[par][:, ch, tap:tap + n],
                                                                 start=(tap == 0), stop=(tap == 30)), reads=['dg', hk], writes=[pk])
                P.op('act', lambda e, ch=ch: e.activation(out=cx[:, ch, 0:n], in_=k.ps[ch][:, 0:n], func=AF.Identity, bias=vb[:, ch:ch + 1]),
                     reads=[pk] + vkeys, writes=[ck])
                P.op('act', lambda e, ch=ch: e.activation(out=xb[:, ch, 0:n], in_=k.ps[ch][:, 0:n], func=AF.Identity, bias=vb[:, ch:ch + 1]),
                     reads=[pk] + vkeys, writes=[bk])
                P.op('act', lambda e, ch=ch: e.activation(out=xs[:, ch, 0:n], in_=k.ps[ch][:, 0:n], func=AF.Square, bias=vb[:, ch:ch + 1]),
                     reads=[pk] + vkeys, writes=[sk])

        def stats_mm(ti):
            t0, n = TILES[ti]
            par = ti % 2
            xb, xs = xbs[par], xss[par]
            bk, sk = 'xb%d' % par, 'xs%d' % par
            for ch in range(4):
                P.op('pe', lambda e, ch=ch: e.matmul(k.ps[4][:, 0:n], lhsT=k.onesB[:], rhs=xb[:, ch, 0:n], start=(ch == 0), stop=(ch == 3)),
                     reads=['onesB', bk], writes=['ps4'])
            for ch in range(4):
                P.op('pe', lambda e, ch=ch: e.matmul(k.ps[5][:, 0:n], lhsT=k.onesB[:], rhs=xs[:, ch, 0:n], start=(ch == 0), stop=(ch == 3)),
                     reads=['onesB', sk], writes=['ps5'])

        def ln_tail(ti):
            t0, n = TILES[ti]
            par = ti % 2
            cx = cxs[par]
            ck = 'cx%d' % par
            P.op('dve', lambda e: e.tensor_scalar(out=mean[:, 0:n], in0=k.ps[4][:, 0:n], scalar1=1.0 / 512, scalar2=None, op0=ALU.mult), reads=['ps4'], writes=['mean'])
            P.op('dve', lambda e: e.tensor_tensor(out=var[:, 0:n], in0=mean[:, 0:n], in1=mean[:, 0:n], op=ALU.mult), reads=['mean'], writes=['var'])
            P.op('dve', lambda e: e.scalar_tensor_tensor(out=var[:, 0:n], in0=k.ps[5][:, 0:n], scalar=1.0 / 512, in1=var[:, 0:n], op0=ALU.mult, op1=ALU.subtract),
                 reads=['ps5', 'var'], writes=['var'])
            P.op('dve', lambda e: e.tensor_scalar(out=var[:, 0:n], in0=var[:, 0:n], scalar1=0.0, scalar2=None, op0=ALU.max), reads=['var'], writes=['var'])
            P.op('act', lambda e: e.activation(out=var[:, 0:n], in_=var[:, 0:n], func=AF.Sqrt, bias=k.epsT[:, 0:1]), reads=['var', 'epsT'], writes=['var'])
            P.op('dve', lambda e: e.reciprocal(out=rs[:, 0:n], in_=var[:, 0:n]), reads=['var'], writes=['rs'])
            P.op('dve', lambda e: e.tensor_tensor(out=cx[:, :, 0:n], in0=cx[:, :, 0:n], in1=mean[:, 0:n].unsqueeze(1).to_broadcast([128, 4, n]), op=ALU.subtract),
                 reads=[ck, 'mean'], writes=[ck])
            P.op('dve', lambda e: e.tensor_tensor(out=cx[:, :, 0:n], in0=cx[:, :, 0:n], in1=rs[:, 0:n].unsqueeze(1).to_broadcast([128, 4, n]), op=ALU.mult),
                 reads=[ck, 'rs'], writes=[ck])
            for ch in range(4):
                P.op('act', lambda e, ch=ch: e.activation(out=yc[par][:, ch, 0:n], in_=cx[:, ch, 0:n], func=AF.Silu,
                                                          scale=vb[:, 4 + ch:5 + ch], bias=vb[:, 8 + ch:9 + ch]),
                     reads=[ck] + vkeys, writes=['yc%d' % par])
            P.dma('sp', k.ycT[:, :, t0:t0 + n], yc[par][:, :, 0:n], reads=['yc%d' % par], writes=['ycT'])

        load(tiles[0])
        for idx, ti in enumerate(tiles):
            nxt = tiles[idx + 1] if idx + 1 < len(tiles) else None
            if nxt is not None:
                load(nxt)
            conv_mm(ti, [0])
            if idx > 0:
                stats_mm(tiles[idx - 1])
                ln_tail(tiles[idx - 1])
            conv_mm(ti, [1, 2, 3])
        stats_mm(tiles[-1])
        ln_tail(tiles[-1])


def att_gen(k, l, ph, sbanks):
    P = k.P
    I = k.I
    need_ctx = l < L - 1
    if True:
        kTs = ph.sb('kTs', [128, NT], BF16)
        Vs = ph.sb('Vs', [128, NBLK, 128], BF16)
        P.dma('sp', kTs[:], k.kT, reads=['kT'], writes=['kTs'])
        for part in range(0, NBLK, 11):
            P.dma('sp', Vs[:, part:part + 11, :], k.V[part * 128:(part + 11) * 128, :].rearrange("(s p) d -> p s d", p=128), reads=['V'], writes=['Vs'])
        snk = ph.sb('snk', [1, 8], F32)
        sx = ph.sb('sx', [128, 4], F32)
        P.dma('sp', snk[:], I['attn_sink'][l:l + 1, :], writes=['snk'])
        onesF = ph.sb('onesF', [1, 128], F32)
        P.op('dve', lambda e: e.memset(onesF[:], 1.0), writes=['onesF'])
        P.op('pe', lambda e: e.matmul(k.ps[7][:, 0:8], lhsT=onesF[:], rhs=snk[:], start=True, stop=True), reads=['onesF', 'snk'], writes=['ps7'])
        P.op('act', lambda e: e.activation(out=sx[0:64, :], in_=k.ps[7][0:64, 0:4], func=AF.Exp), reads=['ps7'], writes=['sx'])
        P.op('act', lambda e: e.activation(out=sx[64:128, :], in_=k.ps[7][64:128, 4:8], func=AF.Exp), reads=['ps7'], writes=['sx'])
        qs = [ph.sb('qs%d' % i, [128, 4, 128], BF16) for i in range(2)]
        pT = [ph.sb('pT%d' % i, [128, 512], BF16) for i in range(8)]
        den = ph.sb('den', [128, 512], F32)
        yo = [ph.sb('ayo%d' % i, [128, 4, 128], BF16) for i in range(2)]
        cnt = 0
        pcnt = 0
        qblocks = list(range(2, NBLK)) + ([0, 1] if need_ctx else [])
        def qload(qi):
            P.dma('sp', qs[qi % 2][:], k.qT[:, :, qblocks[qi] * 128:qblocks[qi] * 128 + 128], reads=['qT'], writes=['qs%d' % (qi % 2)])
        qload(0)
        yield
        for qi, qb in enumerate(qblocks):
            par = qi % 2
            t0 = qb * 128
            qk = 'qs%d' % par
            if qi + 1 < len(qblocks):
                qload(qi + 1)
            if qb >= 2:
                kbs = []
                if qb > 2:
                    kbs.append((qb - 1, k.nmp, 'nmp'))
                kbs.append((qb, None, None))
                if qb < NBLK - 1:
                    kbs.append((qb + 1, k.nmn, 'nmn'))
                kbs += [(0, None, None), (1, None, None)]
            else:
                kbs = [(0, None, None), (1, None, None)]
            bo = 4 + (qi % 2)
            bd_ = 6 + (qi % 2)
            pko, pkd = 'ps%d' % bo, 'ps%d' % bd_
            for g in range(2):
                pb = g * 64
                pl = []
                for (kb, nm, nmk) in kbs:
                    b = sbanks[cnt % len(sbanks)]
                    cnt += 1
                    pk = 'ps%d' % b
                    P.op('pe', lambda e, b=b, pb=pb, kb=kb, par=par, nm=nm: e.matmul(k.ps[b][:, :], lhsT=kTs[pb:pb + 64, kb * 128:(kb + 1) * 128],
                                                                                    rhs=qs[par][pb:pb + 64, :, :], start=True, stop=(nm is None)),
                         reads=['kTs', qk], writes=[pk])
                    if nm is not None:
                        P.op('pe', lambda e, b=b, nm=nm: e.matmul(k.ps[b][:, :], lhsT=k.identB[:], rhs=nm[:], start=False, stop=True),
                             reads=['identB', nmk], writes=[pk])
                    pi = pcnt % 8
                    pcnt += 1
                    P.op('act', lambda e, b=b, pi=pi: e.activation(out=pT[pi][:], in_=k.ps[b][:, :], func=AF.Exp, scale=0.125), reads=[pk], writes=['pT%d' % pi])
                    pl.append((kb, pi))
                for j, (kb, pi) in enumerate(pl):
                    P.op('pe', lambda e, bo=bo, pb=pb, kb=kb, pi=pi, j=j, nl=len(pl): e.matmul(k.ps[bo][pb:pb + 64, :], lhsT=Vs[:, kb, pb:pb + 64], rhs=pT[pi][:],
                                                                                              start=(j == 0), stop=(j == nl - 1)),
                         reads=['Vs', 'pT%d' % pi], writes=[pko])
                for j, (kb, pi) in enumerate(pl):
                    P.op('pe', lambda e, bd_=bd_, pb=pb, pi=pi, j=j, nl=len(pl): e.matmul(k.ps[bd_][pb:pb + 64, :], lhsT=k.onesB[:, 0:64], rhs=pT[pi][:],
                                                                                         start=(j == 0), stop=(j == nl - 1)),
                         reads=['onesB', 'pT%d' % pi], writes=[pkd])
            P.op('dve', lambda e, bd_=bd_: e.tensor_tensor(out=den[:, :].rearrange("p (c t) -> p c t", t=128), in0=k.ps[bd_][:, :].rearrange("p (c t) -> p c t", t=128),
                                                           in1=sx[:, :].unsqueeze(2).to_broadcast([128, 4, 128]), op=ALU.add), reads=[pkd, 'sx'], writes=['den'])
            P.op('dve', lambda e: e.reciprocal(out=den[:], in_=den[:]), reads=['den'], writes=['den'])
            P.op('dve', lambda e, bo=bo, par=par: e.tensor_tensor(out=yo[par][:, :, :].rearrange("p c t -> p (c t)"), in0=k.ps[bo][:, :], in1=den[:], op=ALU.mult),
                 reads=[pko, 'den'], writes=['ayo%d' % par])
            P.dma('sp', k.yaT[:, :, t0:t0 + 128], yo[par][:], reads=['ayo%d' % par], writes=['yaT'])
            yield


def lru_gen(k, l, ph, gbanks):
    P = k.P
    I = k.I
    if True:
        cw = ph.sb('cw', [128, 16], F32)
        cb = ph.sb('cb', [128, 4], F32)
        ba = ph.sb('ba', [128, 8], F32)
        bx = ph.sb('bx', [128, 8], F32)
        lam = ph.sb('lam', [128, 8], F32)
        c1 = ph.sb('c1', [128, 8], F32)
        load_vec_fm(k, ph, I['lru_conv_w'][l].rearrange("t (c p) -> (t c) p", p=128), 16, cw[:], 'cw')
        load_vec_fm(k, ph, I['lru_conv_b'][l].rearrange("(c p) -> c p", p=128), 4, cb[:], 'cb')
        load_vec_fm(k, ph, I['lru_ba'][l].rearrange("d (c p) -> (d c) p", p=128), 8, ba[:], 'ba')
        load_vec_fm(k, ph, I['lru_bx'][l].rearrange("d (c p) -> (d c) p", p=128), 8, bx[:], 'bx')
        load_vec_fm(k, ph, I['lru_lam'][l].rearrange("d (c p) -> (d c) p", p=128), 8, lam[:], 'lam')
        P.op('act', lambda e: e.activation(out=c1[:], in_=lam[:], func=AF.Exp, scale=-1.0), reads=['lam'], writes=['c1'])
        P.op('dve', lambda e: e.tensor_scalar(out=c1[:], in0=c1[:], scalar1=1.0, scalar2=None, op0=ALU.add), reads=['c1'], writes=['c1'])
        P.op('act', lambda e: e.activation(out=c1[:], in_=c1[:], func=AF.Ln), reads=['c1'], writes=['c1'])
        P.op('dve', lambda e: e.tensor_scalar(out=c1[:], in0=c1[:], scalar1=-4.0, scalar2=None, op0=ALU.mult), reads=['c1'], writes=['c1'])
        for t_, nm in ((cb, 'cb'), (ba, 'ba'), (bx, 'bx')):
            P.op('dve', lambda e, t_=t_: e.tensor_scalar(out=t_[:], in0=t_[:], scalar1=0.5, scalar2=None, op0=ALU.mult), reads=[nm], writes=[nm])
        bd = ph.sb('bd', [128, 16, 128], BF16)
        P.op('pool', lambda e: e.memset(bd[:], 0.0), writes=['bd'])
        for d in range(2):
            for gi, wn in enumerate(('lru_wa', 'lru_wx')):
                for ch in range(4):
                    idx = (d * 2 + gi) * 4 + ch
                    for hb in range(2):
                        wload(k, bd[hb * 64:(hb + 1) * 64, idx, hb * 64:(hb + 1) * 64], I[wn][l, d, ch * 2 + hb], 'bd')
        dgl = ph.sb('dgl', [128, 16, 128], BF16)
        P.op('dve', lambda e: e.tensor_tensor(out=dgl[:, :, :], in0=k.identF[:, :].unsqueeze(1).to_broadcast([128, 16, 128]),
                                              in1=cw[:, :].unsqueeze(2).to_broadcast([128, 16, 128]), op=ALU.mult),
             reads=['identF', 'cw'], writes=['dgl'])
        xp = ph.sb('xp0', [128, NT + 8], BF16)
        Uh = ph.sb('Uh0', [128, NT], BF16)
        Hf = ph.sb('Hf', [128, NT], F32)
        As = [ph.sb('As%d' % i, [128, 2048], F32) for i in range(2)]
        Bs = [ph.sb('Bs%d' % i, [128, 2048], F32) for i in range(2)]
        tRs = [ph.sb('tR%d' % i, [128, 2048], F32) for i in range(2)]
        tIs = [ph.sb('tI%d' % i, [128, 2048], F32) for i in range(2)]
        gt = [ph.sb('gt%d' % i, [128, 2048], BF16) for i in range(2)]
        lyo = ph.sb('lyo', [128, 2048], BF16)
        hc = ph.sb('hc', [128, 2], F32)
        CO, LO = 2, 261
        xk, uk = 'xp0', 'Uh0'
        st = {'cnt': 0, 'g': 0, 'p': 0}
        spans = [(-1, 0, C)] + [(h, C + 2048 * h, 2048) for h in range(4)]
        yield

        def nbank():
            b = gbanks[st['cnt'] % len(gbanks)]
            st['cnt'] += 1
            return b

        def gates(ch, d, s0, sn):
            gp = st['g'] % 2
            st['g'] += 1
            tR, tI, A, B = tRs[gp], tIs[gp], As[gp], Bs[gp]
            kR, kI, kA, kB = 'tR%d' % gp, 'tI%d' % gp, 'As%d' % gp, 'Bs%d' % gp
            for gi, (dstG, gk, bias) in enumerate(((tR, kR, ba), (tI, kI, bx))):
                idx = (d * 2 + gi) * 4 + ch
                for sub in range(0, sn, 512):
                    m = min(512, sn - sub)
                    b = nbank()
                    pk = 'ps%d' % b
                    P.op('pe', lambda e, b=b, idx=idx, sub=sub, m=m: e.matmul(k.ps[b][:, 0:m], lhsT=bd[:, idx, :], rhs=Uh[:, s0 + sub:s0 + sub + m], start=True, stop=True),
                         reads=['bd', uk], writes=[pk])
                    P.op('act', lambda e, b=b, dstG=dstG, sub=sub, m=m, bias=bias: e.activation(out=dstG[:, sub:sub + m], in_=k.ps[b][:, 0:m], func=AF.Tanh,
                                                                                               bias=bias[:, d * 4 + ch:d * 4 + ch + 1]),
                         reads=[pk, 'ba', 'bx'], writes=[gk])
            P.op('act', lambda e: e.activation(out=A[:, 0:sn], in_=tR[:, 0:sn], func=AF.Exp, scale=c1[:, d * 4 + ch:d * 4 + ch + 1],
                                               bias=c1[:, d * 4 + ch:d * 4 + ch + 1]), reads=[kR, 'c1'], writes=[kA])
            P.op('dve', lambda e: e.tensor_tensor(out=tR[:, 0:sn], in0=A[:, 0:sn], in1=A[:, 0:sn], op=ALU.mult), reads=[kA], writes=[kR])
            P.op('act', lambda e: e.activation(out=tR[:, 0:sn], in_=tR[:, 0:sn], func=AF.Sqrt, scale=-1.0, bias=k.oneT[:, 0:1]), reads=[kR, 'oneT'], writes=[kR])
            P.op('dve', lambda e: e.scalar_tensor_tensor(out=tR[:, 0:sn], in0=tI[:, 0:sn], scalar=1.0, in1=tR[:, 0:sn], op0=ALU.add, op1=ALU.mult),
                 reads=[kR, kI], writes=[kR])
            P.op('pool', lambda e: e.tensor_tensor(out=B[:, 0:sn], in0=tR[:, 0:sn], in1=Uh[:, s0:s0 + sn], op=ALU.mult), reads=[kR, uk], writes=[kB])
            return gp

        def combine(ch, gp, s0, sn):
            par = st['p'] % 2
            st['p'] += 1
            B = Bs[gp]
            kB = 'Bs%d' % gp
            P.dma('sp', gt[par][:, 0:sn], k.GT[:, ch, s0:s0 + sn], reads=['GT'], writes=['gt%d' % par])
            P.op('pool', lambda e: e.tensor_tensor(out=B[:, 0:sn], in0=Hf[:, s0:s0 + sn], in1=B[:, 0:sn], op=ALU.add), reads=['Hf', kB], writes=[kB])
            P.op('dve', lambda e: e.tensor_tensor(out=lyo[:, 0:sn], in0=B[:, 0:sn], in1=gt[par][:, 0:sn], op=ALU.mult),
                 reads=[kB, 'gt%d' % par], writes=['lyo'])
            P.dma('sp', k.ylT[:, ch, s0:s0 + sn], lyo[:, 0:sn], reads=['lyo'], writes=['ylT'])

        for ch in range(4):
            P.op('pool', lambda e: e.memset(xp[:, 0:2], 0.0), writes=[xk])
            P.op('pool', lambda e: e.memset(xp[:, 258:261], 0.0), writes=[xk])
            P.op('pool', lambda e: e.memset(xp[:, LO + S:LO + S + 3], 0.0), writes=[xk])
            P.dma('sp', xp[:, CO:CO + C], k.lxT[:, ch, 0:C], reads=['lxT'], writes=[xk])
            P.dma('sp', xp[:, LO:LO + S], k.lxT[:, ch, C:NT], reads=['lxT'], writes=[xk])
            for (o, u0, n_) in ((CO, 0, C), (LO, C, S)):
                for t0 in range(0, n_, 512):
                    m = min(512, n_ - t0)
                    b = nbank()
                    pk = 'ps%d' % b
                    for tap in range(4):
                        P.op('pe', lambda e, b=b, tap=tap, ch=ch, o=o, t0=t0, m=m: e.matmul(k.ps[b][:, 0:m], lhsT=dgl[:, tap * 4 + ch, :],
                                                                                           rhs=xp[:, o + t0 + tap - 2:o + t0 + tap - 2 + m], start=(tap == 0), stop=(tap == 3)),
                             reads=['dgl', xk], writes=[pk])
                    P.op('act', lambda e, b=b, u0=u0, t0=t0, m=m, ch=ch: e.activation(out=Uh[:, u0 + t0:u0 + t0 + m], in_=k.ps[b][:, 0:m], func=AF.Identity,
                                                                                     scale=0.5, bias=cb[:, ch:ch + 1]), reads=[pk, 'cb'], writes=[uk])
                    if t0 % 2048 == 1536:
                        yield
            yield
            for si, (sid, s0, sn) in enumerate(spans):
                gp = gates(ch, 0, s0, sn)
                init = 0.0 if si == 0 else Hf[:, s0 - 1:s0]
                P.op('dve', lambda e, gp=gp, s0=s0, sn=sn, init=init: e.tensor_tensor_scan(out=Hf[:, s0:s0 + sn], data0=As[gp][:, 0:sn], data1=Bs[gp][:, 0:sn], initial=init,
                                                                                         op0=ALU.mult, op1=ALU.add), reads=['As%d' % gp, 'Bs%d' % gp, 'Hf'], writes=['Hf'])
                yield
            order = [spans[0]] + spans[:0:-1]
            for si, (sid, s0, sn) in enumerate(order):
                gp = gates(ch, 1, s0, sn)
                init = 0.0 if si == 0 else hc[:, (si - 1) % 2:(si - 1) % 2 + 1]
                P.op('dve', lambda e, gp=gp, sn=sn, init=init: e.tensor_tensor_scan(out=Bs[gp][:, 0:sn][:, ::-1], data0=As[gp][:, 0:sn][:, ::-1], data1=Bs[gp][:, 0:sn][:, ::-1],
                                                                                  initial=init, op0=ALU.mult, op1=ALU.add), reads=['As%d' % gp, 'Bs%d' % gp, 'hc'], writes=['Bs%d' % gp])
                P.op('dve', lambda e, gp=gp, si=si: e.tensor_copy(out=hc[:, si % 2:si % 2 + 1], in_=Bs[gp][:, 0:1]), reads=['Bs%d' % gp], writes=['hc'])
                combine(ch, gp, s0, sn)
                yield


def phase_attlru(k, l):
    with Ph(k, 'pal%d' % l) as ph:
        ga = att_gen(k, l, ph, [0, 1])
        gl = lru_gen(k, l, ph, [2, 3])
        alive = [True, True]

        def adv(i, g, n):
            for _ in range(n):
                if alive[i]:
                    try:
                        next(g)
                    except StopIteration:
                        alive[i] = False
        adv(0, ga, 1)
        adv(1, gl, 1)
        while alive[0] or alive[1]:
            adv(1, gl, 1)
            adv(0, ga, 1)


def phase_att(k, l):
    P = k.P
    I = k.I
    need_ctx = l < L - 1
    with Ph(k, 'pat%d' % l) as ph:
        kTs = ph.sb('kTs', [128, NT], BF16)
        Vs = ph.sb('Vs', [128, NBLK, 128], BF16)
        P.dma('sp', kTs[:], k.kT, reads=['kT'], writes=['kTs'])
        for part in range(0, NBLK, 11):
            P.dma('sp', Vs[:, part:part + 11, :], k.V[part * 128:(part + 11) * 128, :].rearrange("(s p) d -> p s d", p=128), reads=['V'], writes=['Vs'])
        snk = ph.sb('snk', [1, 8], F32)
        sx = ph.sb('sx', [128, 4], F32)
        P.dma('sp', snk[:], I['attn_sink'][l:l + 1, :], writes=['snk'])
        onesF = ph.sb('onesF', [1, 128], F32)
        P.op('dve', lambda e: e.memset(onesF[:], 1.0), writes=['onesF'])
        P.op('pe', lambda e: e.matmul(k.ps[7][:, 0:8], lhsT=onesF[:], rhs=snk[:], start=True, stop=True), reads=['onesF', 'snk'], writes=['ps7'])
        P.op('act', lambda e: e.activation(out=sx[0:64, :], in_=k.ps[7][0:64, 0:4], func=AF.Exp), reads=['ps7'], writes=['sx'])
        P.op('act', lambda e: e.activation(out=sx[64:128, :], in_=k.ps[7][64:128, 4:8], func=AF.Exp), reads=['ps7'], writes=['sx'])
        qs = [ph.sb('qs%d' % i, [128, 4, 128], BF16) for i in range(2)]
        pT = [ph.sb('pT%d' % i, [128, 512], BF16) for i in range(8)]
        den = ph.sb('den', [128, 512], F32)
        yo = [ph.sb('yo%d' % i, [128, 4, 128], BF16) for i in range(2)]
        cnt = 0
        pcnt = 0
        qblocks = list(range(2, NBLK)) + ([0, 1] if need_ctx else [])
        def qload(qi):
            P.dma('sp', qs[qi % 2][:], k.qT[:, :, qblocks[qi] * 128:qblocks[qi] * 128 + 128], reads=['qT'], writes=['qs%d' % (qi % 2)])
        qload(0)
        for qi, qb in enumerate(qblocks):
            par = qi % 2
            t0 = qb * 128
            qk = 'qs%d' % par
            if qi + 1 < len(qblocks):
                qload(qi + 1)
            if qb >= 2:
                kbs = []
                if qb > 2:
                    kbs.append((qb - 1, k.nmp, 'nmp'))
                kbs.append((qb, None, None))
                if qb < NBLK - 1:
                    kbs.append((qb + 1, k.nmn, 'nmn'))
                kbs += [(0, None, None), (1, None, None)]
            else:
                kbs = [(0, None, None), (1, None, None)]
            bo = 4 + (qi % 2)
            bd_ = 6 + (qi % 2)
            pko, pkd = 'ps%d' % bo, 'ps%d' % bd_
            for g in range(2):
                pb = g * 64
                pl = []
                for (kb, nm, nmk) in kbs:
                    b = cnt % 4
                    cnt += 1
                    pk = 'ps%d' % b
                    P.op('pe', lambda e, b=b, pb=pb, kb=kb, par=par, nm=nm: e.matmul(k.ps[b][:, :], lhsT=kTs[pb:pb + 64, kb * 128:(kb + 1) * 128],
                                                                                    rhs=qs[par][pb:pb + 64, :, :], start=True, stop=(nm is None)),
                         reads=['kTs', qk], writes=[pk])
                    if nm is not None:
                        P.op('pe', lambda e, b=b, nm=nm: e.matmul(k.ps[b][:, :], lhsT=k.identB[:], rhs=nm[:], start=False, stop=True),
                             reads=['identB', nmk], writes=[pk])
                    pi = pcnt % 8
                    pcnt += 1
                    P.op('act', lambda e, b=b, pi=pi: e.activation(out=pT[pi][:], in_=k.ps[b][:, :], func=AF.Exp, scale=0.125), reads=[pk], writes=['pT%d' % pi])
                    pl.append((kb, pi))
                for j, (kb, pi) in enumerate(pl):
                    P.op('pe', lambda e, bo=bo, pb=pb, kb=kb, pi=pi, j=j, nl=len(pl): e.matmul(k.ps[bo][pb:pb + 64, :], lhsT=Vs[:, kb, pb:pb + 64], rhs=pT[pi][:],
                                                                                              start=(j == 0), stop=(j == nl - 1)),
                         reads=['Vs', 'pT%d' % pi], writes=[pko])
                for j, (kb, pi) in enumerate(pl):
                    P.op('pe', lambda e, bd_=bd_, pb=pb, pi=pi, j=j, nl=len(pl): e.matmul(k.ps[bd_][pb:pb + 64, :], lhsT=k.onesB[:, 0:64], rhs=pT[pi][:],
                                                                                         start=(j == 0), stop=(j == nl - 1)),
                         reads=['onesB', 'pT%d' % pi], writes=[pkd])
            P.op('dve', lambda e, bd_=bd_: e.tensor_tensor(out=den[:, :].rearrange("p (c t) -> p c t", t=128), in0=k.ps[bd_][:, :].rearrange("p (c t) -> p c t", t=128),
                                                           in1=sx[:, :].unsqueeze(2).to_broadcast([128, 4, 128]), op=ALU.add), reads=[pkd, 'sx'], writes=['den'])
            P.op('dve', lambda e: e.reciprocal(out=den[:], in_=den[:]), reads=['den'], writes=['den'])
            P.op('dve', lambda e, bo=bo, par=par: e.tensor_tensor(out=yo[par][:, :, :].rearrange("p c t -> p (c t)"), in0=k.ps[bo][:, :], in1=den[:], op=ALU.mult),
                 reads=[pko, 'den'], writes=['yo%d' % par])
            P.dma('sp', k.yaT[:, :, t0:t0 + 128], yo[par][:], reads=['yo%d' % par], writes=['yaT'])


def phase_merge(k, l):
    P = k.P
    I = k.I
    need_ctx = l < L - 1
    with Ph(k, 'pmg%d' % l) as ph:
        wg = ph.sb('wg', [128, 8, 3072], BF16)
        wo3 = ph.sb('wo3', [128, 3, 4, 1024], BF16)
        wout = ph.sb('wout', [128, 8, 1024], BF16)
        for kc in range(8):
            wload(k, wg[:, kc, :], I['w_in'][l, kc * 128:(kc + 1) * 128, 2816:5888], 'wg')
        for c in range(4):
            wload(k, wo3[0:64, 0, c, :], I['w_o_attn'][l, c * 64:(c + 1) * 64, :], 'wo3')
            wload(k, wo3[64:128, 0, c, :], I['w_o_attn'][l, (c + 4) * 64:(c + 5) * 64, :], 'wo3')
            wload(k, wo3[:, 1, c, :], I['w_o_conv'][l, c * 128:(c + 1) * 128, :], 'wo3')
            wload(k, wo3[:, 2, c, :], I['w_o_lru'][l, c * 128:(c + 1) * 128, :], 'wo3')
        for kc in range(8):
            wload(k, wout[:, kc, :], I['w_out'][l, kc * 128:(kc + 1) * 128, :], 'wout')
        aT_t = [ph.sb('aT%d' % i, [128, 8, 512], BF16) for i in range(2)]
        y3 = [ph.sb('y3%d' % i, [128, 3, 4, 512], BF16) for i in range(2)]
        hT_t = [ph.sb('hT%d' % i, [128, 8, 512], F32) for i in range(2)]
        gs = [ph.sb('gs%d' % i, [128, 3, 512], F32) for i in range(2)]
        m1 = ph.sb('m1', [128, 512], F32)
        m2 = ph.sb('m2', [128, 512], F32)
        mg = ph.sb('mg', [128, 8, 512], BF16)
        ysrc = (k.yaT, k.ycT, k.ylT)
        ykeys = ('yaT', 'ycT', 'ylT')
        cnt = 0
        mtiles = [ti for ti in range(len(TILES)) if not (ti == 0 and not need_ctx)]

        def mload(ti):
            t0, n = TILES[ti]
            par = ti % 2
            P.dma('sp', aT_t[par][:, :, 0:n], k.aT[:, :, t0:t0 + n], reads=[('aT', ti)], writes=['aT%d' % par])
            for br in range(3):
                P.dma('sp', y3[par][:, br, :, 0:n], ysrc[br][:, :, t0:t0 + n], reads=[ykeys[br]], writes=['y3%d' % par])
            P.dma('sp', hT_t[par][:, :, 0:n], k.hT[:, :, t0:t0 + n], reads=[('hT', b) for b in range(t0 // 128, (t0 + n) // 128)], writes=['hT%d' % par])
        mload(mtiles[0])
        for mi, ti in enumerate(mtiles):
            t0, n = TILES[ti]
            par = ti % 2
            r = 1 if ti == 0 else 0
            ak, yk, hk = 'aT%d' % par, 'y3%d' % par, 'hT%d' % par
            if mi + 1 < len(mtiles):
                mload(mtiles[mi + 1])
            for m in range(8):
                gp = cnt % 2
                cnt += 1
                for br in range(3):
                    pk = 'ps%d' % br
                    for kc in range(8):
                        P.op('pe', lambda e, br=br, kc=kc, m=m, par=par, n=n: e.matmul(k.ps[br][:, 0:n], lhsT=wg[:, kc, br * 1024 + m * 128:br * 1024 + (m + 1) * 128],
                                                                                      rhs=aT_t[par][:, kc, 0:n], start=(kc == 0), stop=(kc == 7)),
                             reads=['wg', ak], writes=[pk])
                    P.op('act', lambda e, br=br, gp=gp, n=n: e.activation(out=gs[gp][:, br, 0:n], in_=k.ps[br][:, 0:n], func=AF.Sigmoid), reads=[pk], writes=['gs%d' % gp])
                for br in range(3):
                    pk = 'ps%d' % (3 + br)
                    for kc in range(4):
                        P.op('pe', lambda e, br=br, kc=kc, m=m, par=par, n=n: e.matmul(k.ps[3 + br][:, 0:n], lhsT=wo3[:, br, kc, m * 128:(m + 1) * 128],
                                                                                      rhs=y3[par][:, br, kc, 0:n], start=(kc == 0), stop=(kc == 3)),
                             reads=['wo3', yk], writes=[pk])
                gk = 'gs%d' % gp
                P.op('dve', lambda e, gp=gp, n=n: e.tensor_tensor(out=m1[:, 0:n], in0=k.ps[3][:, 0:n], in1=gs[gp][:, 0, 0:n], op=ALU.mult), reads=['ps3', gk], writes=['m1'])
                P.op('dve', lambda e, gp=gp, n=n: e.tensor_tensor(out=m2[:, 0:n], in0=k.ps[4][:, 0:n], in1=gs[gp][:, 1, 0:n], op=ALU.mult), reads=['ps4', gk], writes=['m2'])
                P.op('pool', lambda e, n=n: e.tensor_tensor(out=m1[:, 0:n], in0=m1[:, 0:n], in1=m2[:, 0:n], op=ALU.add), reads=['m1', 'm2'], writes=['m1'])
                P.op('dve', lambda e, gp=gp, n=n: e.tensor_tensor(out=m2[:, 0:n], in0=k.ps[5][:, 0:n], in1=gs[gp][:, 2, 0:n], op=ALU.mult), reads=['ps5', gk], writes=['m2'])
                P.op('pool', lambda e, n=n, m=m: e.tensor_tensor(out=mg[:, m, 0:n], in0=m1[:, 0:n], in1=m2[:, 0:n], op=ALU.add), reads=['m1', 'm2'], writes=['mg'])
            for m in range(8):
                b = 6 + (m % 2)
                pk = 'ps%d' % b
                for kc in range(8):
                    P.op('pe', lambda e, b=b, kc=kc, m=m, n=n: e.matmul(k.ps[b][:, 0:n], lhsT=wout[:, kc, m * 128:(m + 1) * 128], rhs=mg[:, kc, 0:n],
                                                                       start=(kc == 0), stop=(kc == 7)), reads=['wout', 'mg'], writes=[pk])
                P.op('dve', lambda e, b=b, m=m, n=n, par=par, r=r: e.scalar_tensor_tensor(out=hT_t[par][:, m, 0:n], in0=k.ps[b][:, 0:n], scalar=k.mods[:, r, 2, m:m + 1],
                                                                                         in1=hT_t[par][:, m, 0:n], op0=ALU.mult, op1=ALU.add),
                     reads=[pk, 'mods', hk], writes=[hk])
            P.dma('sp', k.hT[:, :, t0:t0 + n], hT_t[par][:, :, 0:n], reads=[hk], writes=[('hT', b) for b in range(t0 // 128, (t0 + n) // 128)])


def phase_ffn(k, l):
    P = k.P
    I = k.I
    need_ctx = l < L - 1
    FT = 256
    n = FT
    with Ph(k, 'pff%d' % l) as ph:
        wup = ph.sb('wup', [128, 8, 2 * FH], BF16)
        wdn = ph.sb('wdn', [128, 22, 1024], BF16)
        for piece in (0, 2, 1, 3):
            for kc in range(8):
                wload(k, wup[:, kc, piece * 1408:(piece + 1) * 1408], I['ffn_w_up'][l, kc * 128:(kc + 1) * 128, piece * 1408:(piece + 1) * 1408], 'wup%d' % piece)
        for kc in range(22):
            wload(k, wdn[:, kc, :], I['ffn_w_down'][l, kc * 128:(kc + 1) * 128, :], 'wdn')
        hT_t = [ph.sb('hT%d' % i, [128, 8, FT], F32) for i in range(2)]
        sq = ph.sb('sq', [128, 8, FT], BF16)
        tmpn = ph.sb('tmpn', [128, FT], F32)
        rstd = ph.sb('rstd', [128, FT], F32)
        hn = ph.sb('hn', [128, 8, FT], F32)
        a2s = [ph.sb('a2%d' % i, [128, 8, FT], BF16) for i in range(2)]
        sg = [ph.sb('sg%d' % i, [128, FT], F32) for i in range(2)]
        hid = ph.sb('hid', [128, 22, FT], BF16)
        tiles = [ti for ti in range(NT // FT) if not (ti == 0 and not need_ctx)]

        def hkeys(ti):
            return [('hT', b) for b in range(ti * FT // 128, (ti * FT + n) // 128)]

        def load(ti):
            P.dma('sp', hT_t[ti % 2][:, :, 0:n], k.hT[:, :, ti * FT:ti * FT + n], reads=hkeys(ti), writes=['hT%d' % (ti % 2)])

        def part1(ti):
            norm_rstd(k, ph, hT_t[ti % 2], 'hT%d' % (ti % 2), n, sq, 'sq', rstd, 'rstd', tmpn, 'tmpn')

        def part2(ti):
            par = ti % 2
            r = 1 if ti == 0 else 0
            P.op('dve', lambda e: e.tensor_tensor(out=hn[:, :, 0:n], in0=hT_t[par][:, :, 0:n],
                                                   in1=rstd[:, 0:n].unsqueeze(1).to_broadcast([128, 8, n]), op=ALU.mult),
                 reads=['hT%d' % par, 'rstd'], writes=['hn'])
            for c in range(8):
                P.op('act', lambda e, c=c: e.activation(out=a2s[par][:, c, 0:n], in_=hn[:, c, 0:n], func=AF.Identity,
                                                        scale=k.A2[:, r, c:c + 1], bias=k.mods[:, r, 3, c:c + 1]),
                     reads=['hn', 'A2', 'mods'], writes=['a2%d' % par])

        def down(ti, m):
            par = ti % 2
            r = 1 if ti == 0 else 0
            hk = 'hT%d' % par
            b = 4 + (m % 3)
            pk = 'ps%d' % b
            for kc in range(22):
                P.op('pe', lambda e, kc=kc: e.matmul(k.ps[b][:, 0:n], lhsT=wdn[:, kc, m * 128:(m + 1) * 128], rhs=hid[:, kc, 0:n],
                                                     start=(kc == 0), stop=(kc == 21)), reads=['wdn', 'hid'], writes=[pk])
            P.op('dve', lambda e: e.scalar_tensor_tensor(out=hT_t[par][:, m, 0:n], in0=k.ps[b][:, 0:n], scalar=k.mods[:, r, 5, m:m + 1],
                                                         in1=hT_t[par][:, m, 0:n], op0=ALU.mult, op1=ALU.add),
                 reads=[pk, 'mods', hk], writes=[hk])

        load(tiles[0])
        part1(tiles[0])
        part2(tiles[0])
        for idx, ti in enumerate(tiles):
            par = ti % 2
            a2 = a2s[par]
            ak = 'a2%d' % par
            nxt = tiles[idx + 1] if idx + 1 < len(tiles) else None
            if nxt is not None:
                load(nxt)
            for j in range(22):
                bu, bg = 2 * (j % 2), 2 * (j % 2) + 1
                pku, pkg = 'ps%d' % bu, 'ps%d' % bg
                s2 = j % 2
                for kc in range(8):
                    P.op('pe', lambda e, bu=bu, kc=kc, j=j, a2=a2: e.matmul(k.ps[bu][:, 0:n], lhsT=wup[:, kc, j * 128:(j + 1) * 128], rhs=a2[:, kc, 0:n],
                                                                    start=(kc == 0), stop=(kc == 7)), reads=['wup%d' % (j // 11), ak], writes=[pku])
                for kc in range(8):
                    P.op('pe', lambda e, bg=bg, kc=kc, j=j, a2=a2: e.matmul(k.ps[bg][:, 0:n], lhsT=wup[:, kc, FH + j * 128:FH + (j + 1) * 128], rhs=a2[:, kc, 0:n],
                                                                    start=(kc == 0), stop=(kc == 7)), reads=['wup%d' % (2 + j // 11), ak], writes=[pkg])
                P.op('act', lambda e, bg=bg, s2=s2: e.activation(out=sg[s2][:, 0:n], in_=k.ps[bg][:, 0:n], func=AF.Silu), reads=[pkg], writes=['sg%d' % s2])
                P.op('dve', lambda e, bu=bu, s2=s2, j=j: e.tensor_tensor(out=hid[:, j, 0:n], in0=k.ps[bu][:, 0:n], in1=sg[s2][:, 0:n], op=ALU.mult),
                     reads=[pku, 'sg%d' % s2], writes=['hid'])
            for m in range(4):
                down(ti, m)
            if nxt is not None:
                part1(nxt)
                part2(nxt)
            for m in range(4, 8):
                down(ti, m)
            P.dma('sp', k.hT[:, :, ti * FT:ti * FT + n], hT_t[par][:, :, 0:n], reads=['hT%d' % par], writes=hkeys(ti))


def phase_out(k):
    P = k.P
    I = k.I
    with Ph(k, 'pout') as ph:
        gf = ph.sb('gf', [128, 8], F32)
        load_vec_fm(k, ph, I['final_norm_g'].rearrange("(c p) -> c p", p=128), 8, gf[:], 'gf')
        hT_t = [ph.sb('hT%d' % i, [128, 8, 512], F32) for i in range(2)]
        sq = ph.sb('sq', [128, 8, 512], BF16)
        tmpn = ph.sb('tmpn', [128, 512], F32)
        rstd = ph.sb('rstd', [128, 512], F32)
        hn = ph.sb('hn', [128, 8, 512], F32)
        ot = [ph.sb('ot%d' % i, [128, D], F32) for i in range(2)]
        cnt = 0

        def oload(ti):
            t0, n = TILES[ti]
            P.dma('sp', hT_t[ti % 2][:, :, 0:n], k.hT[:, :, t0:t0 + n], reads=[('hT', b) for b in range(t0 // 128, (t0 + n) // 128)], writes=['hT%d' % (ti % 2)])
        oload(1)
        for ti, (t0, n) in enumerate(TILES):
            if ti == 0:
                continue
            par = ti % 2
            hk = 'hT%d' % par
            if ti + 1 < len(TILES):
                oload(ti + 1)
            norm_rstd(k, ph, hT_t[par], hk, n, sq, 'sq', rstd, 'rstd', tmpn, 'tmpn')
            P.op('dve', lambda e, par=par, n=n: e.tensor_tensor(out=hn[:, :, 0:n], in0=hT_t[par][:, :, 0:n],
                                                                 in1=rstd[:, 0:n].unsqueeze(1).to_broadcast([128, 8, n]), op=ALU.mult),
                 reads=[hk, 'rstd'], writes=['hn'])
            for c in range(8):
                P.op('act', lambda e, n=n, c=c: e.activation(out=hn[:, c, 0:n], in_=hn[:, c, 0:n], func=AF.Identity, scale=gf[:, c:c + 1]),
                     reads=['hn', 'gf'], writes=['hn'])
            for sblk in range(4):
                op_ = cnt % 2
                cnt += 1
                ok = 'ot%d' % op_
                for half in range(2):
                    b = 2 * op_ + half
                    pk = 'ps%d' % b
                    for j in range(4):
                        c = 4 * half + j
                        P.op('pe', lambda e, b=b, j=j, c=c, sblk=sblk: e.transpose(out=k.ps[b][:, j * 128:(j + 1) * 128], in_=hn[:, c, sblk * 128:(sblk + 1) * 128],
                                                                                   identity=k.identF[:]), reads=['hn', 'identF'], writes=[pk])
                    if half == 0:
                        P.op('act', lambda e, b=b, op_=op_: e.activation(out=ot[op_][:, 0:512], in_=k.ps[b][:, :], func=AF.Copy), reads=[pk], writes=[ok])
                    else:
                        P.op('dve', lambda e, b=b, op_=op_: e.tensor_copy(out=ot[op_][:, 512:1024], in_=k.ps[b][:, :]), reads=[pk], writes=[ok])
                r0 = t0 - C + sblk * 128
                P.dma('sp', k.out[r0:r0 + 128, :], ot[op_][:], reads=[ok], writes=[('out', r0)])
        k.P.barrier(final=True)


def _consts():
    f32 = np.float32
    rows = S // 64
    row = np.repeat(np.arange(rows, dtype=f32), 64)
    col = np.tile(np.arange(64, dtype=f32), rows)
    inv = np.power(f32(10000.0), -np.arange(16, dtype=f32) / f32(16)).astype(f32)
    ang = np.concatenate([row[:, None] * inv[None], col[:, None] * inv[None]], axis=-1).astype(f32)
    cos, sin = np.cos(ang).astype(f32), np.sin(ang).astype(f32)
    pidx = (np.arange(128) % 64) % 32
    cos2 = np.ascontiguousarray(cos[:, pidx].T)
    sin2 = np.ascontiguousarray(sin[:, pidx].T)
    rot = np.zeros((128, 128), f32)
    for m in range(128):
        if (m % 64) < 32:
            rot[m + 32, m] = -1.0
        else:
            rot[m - 32, m] = 1.0
    ident = np.eye(128, dtype=f32)
    j = np.arange(128)[:, None]
    i = np.arange(128)[None, :]
    nm_prev = np.where(j >= i, 0.0, -30000.0).astype(f32)
    nm_next = np.where(j <= i, 0.0, -30000.0).astype(f32)
    return {'cos2': cos2, 'sin2': sin2, 'rotm': rot, 'ident': ident,
            'nm_prev': np.ascontiguousarray(np.tile(nm_prev, (1, 4))), 'nm_next': np.ascontiguousarray(np.tile(nm_next, (1, 4)))}


_NC = None


def kernel(**inputs):
    global _NC
    inp = {n: np.ascontiguousarray(np.asarray(v, dtype=np.float32)) for n, v in inputs.items()}
    if _NC is None:
        _NC = build()
    consts = _consts()
    shared = {n: inp[n] for n in ('mod_w', 'mod_b', 'norm1_g', 'norm2_g', 'w_in', 'attn_sink', 'conv_dw_w', 'conv_dw_b', 'conv_ln_g',
                                  'conv_ln_b', 'lru_conv_w', 'lru_conv_b', 'lru_wa', 'lru_ba', 'lru_wx', 'lru_bx', 'lru_lam',
                                  'w_o_attn', 'w_o_conv', 'w_o_lru', 'w_out', 'ffn_w_up', 'ffn_w_down', 'final_norm_g')}
    shared.update(consts)
    in_maps = []
    for core in range(8):
        b = core % 2
        m = dict(shared)
        m['x'] = inp['x'][b]
        m['ctx'] = inp['ctx'][b]
        m['cvec'] = np.ascontiguousarray(np.stack([inp['c'][b], inp['c_ctx']], axis=0))
        in_maps.append(m)
    res = run_bass_kernel_spmd(_NC, in_maps, core_ids=list(range(8)))
    return np.stack([res.results[0]['out'], res.results[1]['out']], axis=0).astype(np.float32)
```

```python
import contextlib
import numpy as np
import concourse.bass as bass
import concourse.mybir as mybir
from concourse.bass_utils import run_bass_kernel_spmd

F32 = mybir.dt.float32
BF16 = mybir.dt.bfloat16
AF = mybir.ActivationFunctionType
ALU = mybir.AluOpType
AX = mybir.AxisListType

ENGS = ['pe', 'act', 'dve', 'pool', 'sp']
NDMA = 8

S = 8192
C = 256
NT = S + C
D = 1024
L = 2
FH = 2816
EPS = 1e-6
TILES = [(0, 256)] + [(256 + 512 * i, 512) for i in range(16)]
NBLK = NT // 128


class _StopProbe:
    def __init__(self):
        self.stop = True

    def matmul(self, *a, **kw):
        self.stop = bool(kw.get('stop', True))
        return self

    def transpose(self, *a, **kw):
        self.stop = True
        return self


class Prog:
    def __init__(self, nc):
        self.nc = nc
        self.thunks = {e: [] for e in ENGS}
        self.count = {e: 0 for e in ENGS}
        self.waited = {e: {} for e in ENGS}
        self.res_w = {}
        self.res_r = {}
        self.dma_rr = {e: 0 for e in ENGS}
        self.dma_cnt = {}
        self.sem = {}
        self.semnames = list(ENGS[:4]) + ['pe_1', 'pe_2', 'pe_3', 'pe_4', 'pe_5']
        self.engsem = {e: e for e in ENGS}
        self.nspare = 0
        for q in ('sp', 'pool'):
            for i in range(NDMA):
                self.semnames.append('d_%s_%d' % (q, i))

    def alloc_sems(self, stack):
        for s in self.semnames:
            self.sem[s] = stack.enter_context(self.nc.semaphore(s))

    def _collect(self, eng, reads, writes, extra=()):
        waits = {}

        def need(ev):
            if ev is None:
                return
            sk, val, src = ev
            if src == eng and eng == 'pe':
                return
            if self.waited[eng].get(sk, 0) >= val:
                return
            if waits.get(sk, 0) < val:
                waits[sk] = val
        for k in reads:
            need(self.res_w.get(k))
            if isinstance(k, str) and k.startswith('ps'):
                for ev in self.res_r.get(k, {}).values():
                    if ev[2] != eng:
                        need(ev)
        for k in writes:
            need(self.res_w.get(k))
            for ev in self.res_r.get(k, {}).values():
                if ev[2] == eng and eng != 'sp':
                    continue
                need(ev)
        for ev in extra:
            need(ev)
        for sk, val in waits.items():
            self.waited[eng][sk] = val
        return list(waits.items())

    def _record(self, ev, reads, writes):
        for k in reads:
            d = self.res_r.setdefault(k, {})
            old = d.get(ev[0])
            if old is None or old[1] < ev[1]:
                d[ev[0]] = ev
        for k in writes:
            self.res_w[k] = ev
            self.res_r[k] = {}

    def op(self, eng, fn, reads=(), writes=()):
        wl = self._collect(eng, reads, writes)
        mysem = self.engsem[eng]
        inc = True
        if eng == 'pe':
            pr = _StopProbe()
            fn(pr)
            inc = pr.stop
        if inc:
            self.count[eng] += 1
            ev = (mysem, self.count[eng], eng)
        else:
            ev = (mysem, self.count[eng] + 1, eng)
        sem = self.sem

        def thunk(e):
            for sk, val in wl:
                e.wait_ge(sem[sk], val)
            ins = fn(e)
            if inc:
                ins.then_inc(sem[mysem], 1)
        self.thunks[eng].append(thunk)
        self._record(ev, reads, writes)

    def dma(self, q, out, in_, reads=(), writes=(), **kw):
        i = self.dma_rr[q] % NDMA
        self.dma_rr[q] += 1
        sk = 'd_%s_%d' % (q, i)
        n = self.dma_cnt.get(sk, 0) + 1
        self.dma_cnt[sk] = n
        extra = [(sk, 16 * (n - 1), 'dma')] if n > 1 else []
        wl = self._collect(q, reads, writes, extra)
        ev = (sk, 16 * n, 'dma')
        sem = self.sem

        def thunk(e):
            for s, val in wl:
                e.wait_ge(sem[s], val)
            e.dma_start(out=out, in_=in_, **kw).then_inc(sem[sk], 16)
        self.thunks[q].append(thunk)
        self._record(ev, reads, writes)

    def barrier(self, final=False):
        tot = []
        for e in ENGS[:4]:
            if self.count[e]:
                tot.append((self.engsem[e], self.count[e], e))
        for sk, n in self.dma_cnt.items():
            tot.append((sk, 16 * n, 'dma'))
        sem = self.sem
        for eng in (['sp'] if final else ENGS):
            wl = []
            for sk, val, src in tot:
                if src == eng:
                    continue
                if self.waited[eng].get(sk, 0) >= val:
                    continue
                self.waited[eng][sk] = val
                wl.append((sk, val))

            def thunk(e, wl=wl):
                for s, val in wl:
                    e.wait_ge(sem[s], val)
            self.thunks[eng].append(thunk)
        self.res_w.clear()
        self.res_r.clear()
        if not final and self.count['pe'] > 12000 and self.nspare < 5:
            self.nspare += 1
            self.engsem['pe'] = 'pe_%d' % self.nspare
            self.count['pe'] = 0

    def flush(self):
        nc = self.nc
        th = self.thunks
        with nc.Block() as block:
            @block.sync
            def _(e):
                for t in th['sp']:
                    t(e)

            @block.tensor
            def _(e):
                for t in th['pe']:
                    t(e)

            @block.scalar
            def _(e):
                for t in th['act']:
                    t(e)

            @block.vector
            def _(e):
                for t in th['dve']:
                    t(e)

            @block.gpsimd
            def _(e):
                for t in th['pool']:
                    t(e)
        self.thunks = {e: [] for e in ENGS}


class K:
    pass


def build(stop=None, debug=False):
    nc = bass.Bass("TRN2", target_bir_lowering=False)
    k = K()
    k.nc = nc

    def din(name, shape):
        return nc.dram_tensor(name, list(shape), F32, kind="ExternalInput").ap()

    def dscr(name, shape, dt):
        if debug:
            return nc.dram_tensor(name, list(shape), dt, kind="ExternalOutput").ap()
        return nc.dram_tensor(name, list(shape), dt).ap()
    I = {}
    I['x'] = din('x', [S, D])
    I['ctx'] = din('ctx', [C, D])
    I['cvec'] = din('cvec', [2, D])
    I['mod_w'] = din('mod_w', [L, D, 6 * D])
    I['mod_b'] = din('mod_b', [L, 6 * D])
    I['norm1_g'] = din('norm1_g', [L, D])
    I['norm2_g'] = din('norm2_g', [L, D])
    I['w_in'] = din('w_in', [L, D, 5888])
    I['attn_sink'] = din('attn_sink', [L, 8])
    I['conv_dw_w'] = din('conv_dw_w', [L, 31, 512])
    I['conv_dw_b'] = din('conv_dw_b', [L, 512])
    I['conv_ln_g'] = din('conv_ln_g', [L, 512])
    I['conv_ln_b'] = din('conv_ln_b', [L, 512])
    I['lru_conv_w'] = din('lru_conv_w', [L, 4, 512])
    I['lru_conv_b'] = din('lru_conv_b', [L, 512])
    I['lru_wa'] = din('lru_wa', [L, 2, 8, 64, 64])
    I['lru_ba'] = din('lru_ba', [L, 2, 512])
    I['lru_wx'] = din('lru_wx', [L, 2, 8, 64, 64])
    I['lru_bx'] = din('lru_bx', [L, 2, 512])
    I['lru_lam'] = din('lru_lam', [L, 2, 512])
    I['w_o_attn'] = din('w_o_attn', [L, 512, D])
    I['w_o_conv'] = din('w_o_conv', [L, 512, D])
    I['w_o_lru'] = din('w_o_lru', [L, 512, D])
    I['w_out'] = din('w_out', [L, D, D])
    I['ffn_w_up'] = din('ffn_w_up', [L, D, 2 * FH])
    I['ffn_w_down'] = din('ffn_w_down', [L, FH, D])
    I['final_norm_g'] = din('final_norm_g', [D])
    I['cos2'] = din('cos2', [128, S])
    I['sin2'] = din('sin2', [128, S])
    I['rotm'] = din('rotm', [128, 128])
    I['ident'] = din('ident', [128, 128])
    I['nm_prev'] = din('nm_prev', [128, 512])
    I['nm_next'] = din('nm_next', [128, 512])
    k.I = I
    k.out = nc.dram_tensor('out', [S, D], F32, kind="ExternalOutput").ap()
    k.hT = dscr('hT', [128, 8, NT], F32)
    k.aT = dscr('aT', [128, 8, NT], BF16)
    k.qT = dscr('qT', [128, 4, NT], BF16)
    k.kT = dscr('kT', [128, NT], BF16)
    k.V = dscr('Vtok', [NT, 128], BF16)
    k.hcT = dscr('hcT', [128, 4, NT], BF16)
    k.lxT = dscr('lxT', [128, 4, NT], BF16)
    k.GT = dscr('GT', [128, 4, NT], BF16)
    k.yaT = dscr('yaT', [128, 4, NT], BF16)
    k.ycT = dscr('ycT', [128, 4, NT], BF16)
    k.ylT = dscr('ylT', [128, 4, NT], BF16)

    P = Prog(nc)
    k.P = P
    with contextlib.ExitStack() as top:
        P.alloc_sems(top)
        sbt = lambda name, shape, dt: top.enter_context(nc.sbuf_tensor(name, list(shape), dt))
        k.ps = [top.enter_context(nc.psum_tensor('ps%d' % i, [128, 512], F32)) for i in range(8)]
        k.identF = sbt('identF', [128, 128], F32)
        k.identB = sbt('identB', [128, 128], BF16)
        k.onesB = sbt('onesB', [128, 128], BF16)
        k.rotB = sbt('rotB', [128, 128], BF16)
        k.nmp = sbt('nmp', [128, 512], BF16)
        k.nmn = sbt('nmn', [128, 512], BF16)
        k.mods = sbt('mods', [128, 2, 6, 8], F32)
        k.A1 = sbt('A1', [128, 2, 8], F32)
        k.A2 = sbt('A2', [128, 2, 8], F32)
        P.dma('sp', k.identF[:], I['ident'], writes=['identF'])
        P.dma('pool', k.identB[:], I['ident'], writes=['identB'])
        P.dma('pool', k.rotB[:], I['rotm'], writes=['rotB'])
        P.dma('pool', k.nmp[:], I['nm_prev'], writes=['nmp'])
        P.dma('pool', k.nmn[:], I['nm_next'], writes=['nmn'])
        P.op('dve', lambda e: e.memset(k.onesB[:], 1.0), writes=['onesB'])
        k.epsT = sbt('epsT', [128, 1], F32)
        k.oneT = sbt('oneT', [128, 1], F32)
        P.op('dve', lambda e: e.memset(k.epsT[:], EPS), writes=['epsT'])
        P.op('dve', lambda e: e.memset(k.oneT[:], 1.0), writes=['oneT'])
        P.barrier()
        P.flush()
        plist = [('in', lambda: phase_in(k))]
        for l in range(L):
            plist += [('mod%d' % l, lambda l=l: phase_mod(k, l)), ('a%d' % l, lambda l=l: phase_a(k, l)),
                      ('lru%d' % l, lambda l=l: phase_lru(k, l)), ('conv%d' % l, lambda l=l: phase_conv(k, l)),
                      ('att%d' % l, lambda l=l: phase_att(k, l)), ('merge%d' % l, lambda l=l: phase_merge(k, l)),
                      ('ffn%d' % l, lambda l=l: phase_ffn(k, l))]
        plist.append(('out', lambda: phase_out(k)))
        for name, fn in plist:
            fn()
            if stop == name:
                break
        if stop is not None and stop != 'out':
            with Ph(k, 'pfin') as ph:
                if debug:
                    dbg = ph.sb('dbg', [128, 2 * 6 * 8 + 32], F32)
                    k.dbgout = nc.dram_tensor('dbgout', [128, 128], F32, kind="ExternalOutput").ap()
                    P.op('dve', lambda e: e.tensor_copy(out=dbg[:, 0:96], in_=k.mods[:, :, :, :].rearrange("p a b c -> p (a b c)")), reads=['mods'], writes=['dbg'])
                    P.op('dve', lambda e: e.tensor_copy(out=dbg[:, 96:112], in_=k.A1[:, :, :].rearrange("p a b -> p (a b)")), reads=['A1', 'dbg'], writes=['dbg'])
                    P.op('dve', lambda e: e.tensor_copy(out=dbg[:, 112:128], in_=k.A2[:, :, :].rearrange("p a b -> p (a b)")), reads=['A2', 'dbg'], writes=['dbg'])
                    P.dma('sp', k.dbgout, dbg[:], reads=['dbg'])
    return nc


class Ph:
    def __init__(self, k, name):
        self.k = k
        self.name = name
        self.st = contextlib.ExitStack()
        self.n = 0

    def __enter__(self):
        self.st.__enter__()
        return self

    def sb(self, name, shape, dt):
        self.n += 1
        return self.st.enter_context(self.k.nc.sbuf_tensor('%s_%s' % (self.name, name), list(shape), dt))

    def __exit__(self, *a):
        self.k.P.barrier()
        self.k.P.flush()
        return self.st.__exit__(*a)


def load_vec_fm(k, ph, src_rows_ap, n, dst_ap, tag, bank=7):
    P = k.P
    stg = ph.sb('vst_' + tag, [128, 128], F32)
    key = 'vst_' + tag
    P.dma('sp', stg[0:n, :], src_rows_ap, writes=[key])
    psb = k.ps[bank]
    pk = 'ps%d' % bank
    P.op('pe', lambda e: e.transpose(out=psb[:, 0:n], in_=stg[0:n, :], identity=k.identF[0:n, 0:n]),
         reads=[key, 'identF'], writes=[pk])
    P.op('dve', lambda e: e.tensor_copy(out=dst_ap, in_=psb[:, 0:n]), reads=[pk], writes=[tag])


def norm_rstd(k, ph, hT_t, hkey, n, sq, sqkey, rstd, rkey, tmp, tkey, bank=7):
    P = k.P
    psb = k.ps[bank]
    pk = 'ps%d' % bank
    P.op('act', lambda e: e.activation(out=sq[:, :, 0:n], in_=hT_t[:, :, 0:n], func=AF.Square), reads=[hkey], writes=[sqkey])
    for c in range(8):
        P.op('pe', lambda e, c=c: e.matmul(psb[:, 0:n], lhsT=k.onesB[:], rhs=sq[:, c, 0:n], start=(c == 0), stop=(c == 7)),
             reads=[sqkey, 'onesB'], writes=[pk])
    P.op('act', lambda e: e.activation(out=tmp[:, 0:n], in_=psb[:, 0:n], func=AF.Sqrt, scale=1.0 / D, bias=k.epsT[:, 0:1]),
         reads=[pk, 'epsT'], writes=[tkey])
    P.op('dve', lambda e: e.reciprocal(out=rstd[:, 0:n], in_=tmp[:, 0:n]), reads=[tkey], writes=[rkey])


def wload(k, dst, src, key):
    k.P.dma('pool', dst, src, writes=[key])


def phase_in(k):
    P = k.P
    with Ph(k, 'pin') as ph:
        xin = [ph.sb('xin%d' % i, [128, D], F32) for i in range(2)]
        hst = [ph.sb('hst%d' % i, [128, 8, 128], F32) for i in range(2)]
        def load(s):
            src = k.I['ctx'][s * 128:(s + 1) * 128, :] if s < 2 else k.I['x'][(s - 2) * 128:(s - 1) * 128, :]
            P.dma('sp', xin[s % 2][:], src, writes=['xin%d' % (s % 2)])
        load(0)
        for s in range(NBLK):
            par = s % 2
            xk, hk = 'xin%d' % par, 'hst%d' % par
            if s + 1 < NBLK:
                load(s + 1)
            for half in range(2):
                b = 2 * par + half
                pk = 'ps%d' % b
                for j in range(4):
                    c = 4 * half + j
                    P.op('pe', lambda e, b=b, j=j, c=c, par=par: e.transpose(out=k.ps[b][:, j * 128:(j + 1) * 128],
                                                                             in_=xin[par][:, c * 128:(c + 1) * 128], identity=k.identF[:]),
                         reads=[xk, 'identF'], writes=[pk])
                eng = 'act' if half == 0 else 'dve'
                if eng == 'act':
                    P.op('act', lambda e, b=b, half=half, par=par: e.activation(
                        out=hst[par][:, 4 * half:4 * half + 4, :], in_=k.ps[b][:, :].rearrange("p (j t) -> p j t", t=128), func=AF.Copy),
                        reads=[pk], writes=[hk])
                else:
                    P.op('dve', lambda e, b=b, half=half, par=par: e.tensor_copy(
                        out=hst[par][:, 4 * half:4 * half + 4, :], in_=k.ps[b][:, :].rearrange("p (j t) -> p j t", t=128)),
                        reads=[pk], writes=[hk])
            P.dma('sp', k.hT[:, :, s * 128:(s + 1) * 128], hst[par][:], reads=[hk], writes=[('hT', s)])


def phase_mod(k, l):
    P = k.P
    I = k.I
    with Ph(k, 'pmod%d' % l) as ph:
        cT = ph.sb('cT', [128, 16], F32)
        load_vec_fm(k, ph, I['cvec'].rearrange("r (c p) -> (r c) p", p=128), 16, cT[:], 'cT')
        scT = ph.sb('scT', [128, 16], F32)
        P.op('act', lambda e: e.activation(out=scT[:], in_=cT[:], func=AF.Silu), reads=['cT'], writes=['scT'])
        modb = ph.sb('modb', [128, 48], F32)
        load_vec_fm(k, ph, I['mod_b'][l].rearrange("(c p) -> c p", p=128), 48, modb[:], 'modb')
        g1 = ph.sb('g1n', [128, 8], F32)
        g2 = ph.sb('g2n', [128, 8], F32)
        load_vec_fm(k, ph, I['norm1_g'][l].rearrange("(c p) -> c p", p=128), 8, g1[:], 'g1n')
        load_vec_fm(k, ph, I['norm2_g'][l].rearrange("(c p) -> c p", p=128), 8, g2[:], 'g2n')
        wm = [ph.sb('wm%d' % i, [128, 8, 1024], F32) for i in range(2)]
        psb = k.ps[0]
        import os
        LVL = int(os.environ.get('MODLVL', '9'))
        if LVL < 1:
            return
        for piece in range(6):
            par = piece % 2
            wk = 'wm%d' % par
            for kc in range(8):
                P.dma('sp', wm[par][:, kc, :], I['mod_w'][l, kc * 128:(kc + 1) * 128, piece * 1024:(piece + 1) * 1024], writes=[wk])
            if LVL < 2:
                continue
            for oc in range(8):
                col = (piece * 8 + oc) * 2
                for kc in range(8):
                    P.op('pe', lambda e, par=par, oc=oc, kc=kc, col=col: e.matmul(
                        psb[:, col:col + 2], lhsT=wm[par][:, kc, oc * 128:(oc + 1) * 128],
                        rhs=scT[:, :].rearrange("p (r c) -> p c r", c=8)[:, kc, :], start=(kc == 0), stop=(kc == 7)),
                        reads=[wk, 'scT'], writes=['ps0'])
        if LVL < 3:
            return
        for r in range(2):
            P.op('dve', lambda e, r=r: e.tensor_tensor(
                out=k.mods[:, r, :, :].rearrange("p a b -> p (a b)"),
                in0=psb[:, 0:96].rearrange("p (o r) -> p r o", r=2)[:, r, :], in1=modb[:], op=ALU.add),
                reads=['ps0', 'modb'], writes=['mods'])
            if LVL < 4:
                continue
            P.op('dve', lambda e, r=r: e.scalar_tensor_tensor(out=k.A1[:, r, :], in0=k.mods[:, r, 1, :], scalar=1.0, in1=g1[:],
                                                              op0=ALU.add, op1=ALU.mult), reads=['mods', 'g1n'], writes=['A1'])
            P.op('dve', lambda e, r=r: e.scalar_tensor_tensor(out=k.A2[:, r, :], in0=k.mods[:, r, 4, :], scalar=1.0, in1=g2[:],
                                                              op0=ALU.add, op1=ALU.mult), reads=['mods', 'g2n'], writes=['A2'])


def phase_a(k, l):
    P = k.P
    I = k.I
    W = I['w_in'][l]
    with Ph(k, 'pa%d' % l) as ph:
        wq = ph.sb('wq', [128, 8, 512], BF16)
        wk_ = ph.sb('wk', [128, 8, 128], BF16)
        wv = ph.sb('wv', [128, 8, 128], BF16)
        wcols = ph.sb('wcols', [128, 8, 2048], BF16)
        Wr = W.rearrange("(kc p) n -> p kc n", p=128)
        for c in range(4):
            wload(k, wq[:, :, c * 128:c * 128 + 64], Wr[:, :, c * 64:(c + 1) * 64], 'wq')
            wload(k, wq[:, :, c * 128 + 64:(c + 1) * 128], Wr[:, :, (c + 4) * 64:(c + 5) * 64], 'wq')
        wload(k, wk_[:], Wr[:, :, 512:640], 'wk')
        wload(k, wv[:], Wr[:, :, 640:768], 'wv')
        for kc in range(8):
            wload(k, wcols[:, kc, :], W[kc * 128:(kc + 1) * 128, 768:2816], 'wcols')
        hT_t = [ph.sb('hT%d' % i, [128, 8, 512], F32) for i in range(2)]
        sq = ph.sb('sq', [128, 8, 512], BF16)
        tmpn = ph.sb('tmpn', [128, 512], F32)
        rstd = ph.sb('rstd', [128, 512], F32)
        hn = ph.sb('hn', [128, 8, 512], F32)
        aT_t = [ph.sb('aT%d' % i, [128, 8, 512], BF16) for i in range(2)]
        cs = [ph.sb('cs%d' % i, [128, 2, 512], F32) for i in range(2)]
        qb = [ph.sb('qb%d' % i, [128, 512], BF16) for i in range(2)]
        t1 = [ph.sb('t1%d' % i, [128, 512], F32) for i in range(2)]
        t2 = [ph.sb('t2%d' % i, [128, 512], F32) for i in range(2)]
        qo = [ph.sb('qo%d' % i, [128, 5, 512], BF16) for i in range(2)]
        vo = [ph.sb('vo%d' % i, [128, 4, 128], BF16) for i in range(2)]
        sg = [ph.sb('sg%d' % i, [128, 512], F32) for i in range(2)]
        oc4 = [ph.sb('oc4%d' % i, [128, 4, 512], BF16) for i in range(3)]
        st = {'cnt': 0}

        def load(ti):
            t0, n = TILES[ti]
            par = ti % 2
            P.dma('sp', hT_t[par][:, :, 0:n], k.hT[:, :, t0:t0 + n], reads=[('hT', b) for b in range(t0 // 128, (t0 + n) // 128)], writes=['hT%d' % par])
            if ti > 0:
                P.dma('sp', cs[par][:, 0, 0:n], I['cos2'][:, t0 - C:t0 - C + n], writes=['cs%d' % par])
                P.dma('sp', cs[par][:, 1, 0:n], I['sin2'][:, t0 - C:t0 - C + n], writes=['cs%d' % par])

        def part1(ti):
            t0, n = TILES[ti]
            par = ti % 2
            norm_rstd(k, ph, hT_t[par], 'hT%d' % par, n, sq, 'sq', rstd, 'rstd', tmpn, 'tmpn')

        def part2(ti):
            t0, n = TILES[ti]
            par = ti % 2
            r = 1 if ti == 0 else 0
            hk, ak = 'hT%d' % par, 'aT%d' % par
            P.op('dve', lambda e: e.tensor_tensor(out=hn[:, :, 0:n], in0=hT_t[par][:, :, 0:n],
                                                   in1=rstd[:, 0:n].unsqueeze(1).to_broadcast([128, 8, n]), op=ALU.mult),
                 reads=[hk, 'rstd'], writes=['hn'])
            for c in range(8):
                P.op('act', lambda e, c=c: e.activation(out=aT_t[par][:, c, 0:n], in_=hn[:, c, 0:n], func=AF.Identity,
                                                        scale=k.A1[:, r, c:c + 1], bias=k.mods[:, r, 0, c:c + 1]),
                     reads=['hn', 'A1', 'mods'], writes=[ak])
            P.dma('sp', k.aT[:, :, t0:t0 + n], aT_t[par][:, :, 0:n], reads=[ak], writes=[('aT', ti)])

        def proj(wsl, wkey, par, n, ak):
            b = st['cnt'] % 4
            st['cnt'] += 1
            pk = 'ps%d' % b
            for kc in range(8):
                P.op('pe', lambda e, kc=kc: e.matmul(k.ps[b][:, 0:n], lhsT=wsl(kc), rhs=aT_t[par][:, kc, 0:n], start=(kc == 0), stop=(kc == 7)),
                     reads=[wkey, ak], writes=[pk])
            return b, pk

        def rope_tail(c, b, pk, p2, par, n):
            b2 = 4 + p2
            pk2 = 'ps%d' % b2
            P.op('pe', lambda e: e.matmul(k.ps[b2][:, 0:n], lhsT=k.rotB[:], rhs=qb[p2][:, 0:n], start=True, stop=True),
                 reads=['rotB', 'qb%d' % p2], writes=[pk2])
            P.op('dve', lambda e: e.tensor_tensor(out=t1[p2][:, 0:n], in0=k.ps[b][:, 0:n], in1=cs[par][:, 0, 0:n], op=ALU.mult),
                 reads=[pk, 'cs%d' % par, 'qb%d' % p2], writes=['t1%d' % p2])
            P.op('dve', lambda e: e.tensor_tensor(out=t2[p2][:, 0:n], in0=k.ps[b2][:, 0:n], in1=cs[par][:, 1, 0:n], op=ALU.mult),
                 reads=[pk2, 'cs%d' % par], writes=['t2%d' % p2])
            P.op('pool', lambda e: e.tensor_tensor(out=qo[par][:, c, 0:n], in0=t1[p2][:, 0:n], in1=t2[p2][:, 0:n], op=ALU.add),
                 reads=['t1%d' % p2, 't2%d' % p2], writes=['qo%d' % par])

        def group(ti, grp):
            t0, n = TILES[ti]
            par = ti % 2
            ak = 'aT%d' % par
            ob = oc4[grp]
            okey = 'oc4%d' % grp
            for c in range(4):
                if grp == 0:
                    b, pk = proj(lambda kc, c=c: wcols[:, kc, c * 128:(c + 1) * 128], 'wcols', par, n, ak)
                    b2 = 4 + (st['cnt'] % 2)
                    pk2 = 'ps%d' % b2
                    s2 = st['cnt'] % 2
                    for kc in range(8):
                        P.op('pe', lambda e, kc=kc, c=c, b2=b2: e.matmul(k.ps[b2][:, 0:n], lhsT=wcols[:, kc, 512 + c * 128:512 + (c + 1) * 128],
                                                                        rhs=aT_t[par][:, kc, 0:n], start=(kc == 0), stop=(kc == 7)),
                             reads=['wcols', ak], writes=[pk2])
                    P.op('act', lambda e, b2=b2, s2=s2: e.activation(out=sg[s2][:, 0:n], in_=k.ps[b2][:, 0:n], func=AF.Sigmoid),
                         reads=[pk2], writes=['sg%d' % s2])
                    P.op('dve', lambda e, b=b, s2=s2, c=c: e.tensor_tensor(out=ob[:, c, 0:n], in0=k.ps[b][:, 0:n], in1=sg[s2][:, 0:n], op=ALU.mult),
                         reads=[pk, 'sg%d' % s2], writes=[okey])
                else:
                    off = 1024 if grp == 1 else 1536
                    b, pk = proj(lambda kc, c=c, off=off: wcols[:, kc, off + c * 128:off + (c + 1) * 128], 'wcols', par, n, ak)
                    fn = AF.Copy if grp == 1 else AF.Gelu_apprx_tanh
                    P.op('act', lambda e, b=b, c=c, fn=fn: e.activation(out=ob[:, c, 0:n], in_=k.ps[b][:, 0:n], func=fn),
                         reads=[pk], writes=[okey])
            dst = (k.hcT, k.lxT, k.GT)[grp]
            P.dma('sp', dst[:, :, t0:t0 + n], ob[:, :, 0:n], reads=[okey], writes=[('hcT', 'lxT', 'GT')[grp]])

        load(0)
        part1(0)
        part2(0)
        for ti, (t0, n) in enumerate(TILES):
            par = ti % 2
            r = 1 if ti == 0 else 0
            ak = 'aT%d' % par
            nxt = ti + 1 if ti + 1 < len(TILES) else None
            if nxt is not None:
                load(nxt)
            pend = None
            for c in range(5):
                wsl = (lambda kc, c=c: wq[:, kc, c * 128:(c + 1) * 128]) if c < 4 else (lambda kc: wk_[:, kc, :])
                b, pk = proj(wsl, 'wq' if c < 4 else 'wk', par, n, ak)
                if r == 1:
                    P.op('act', lambda e, b=b, c=c, par=par, n=n: e.activation(out=qo[par][:, c, 0:n], in_=k.ps[b][:, 0:n], func=AF.Copy),
                         reads=[pk], writes=['qo%d' % par])
                else:
                    p2 = c % 2
                    P.op('act', lambda e, b=b, p2=p2, n=n: e.activation(out=qb[p2][:, 0:n], in_=k.ps[b][:, 0:n], func=AF.Copy),
                         reads=[pk], writes=['qb%d' % p2])
                    if pend is not None:
                        rope_tail(*pend)
                    pend = (c, b, pk, p2, par, n)
            if pend is not None:
                rope_tail(*pend)
            P.dma('sp', k.qT[:, :, t0:t0 + n], qo[par][:, 0:4, 0:n], reads=['qo%d' % par], writes=['qT'])
            P.dma('sp', k.kT[:, t0:t0 + n], qo[par][:, 4, 0:n], reads=['qo%d' % par], writes=['kT'])
            for sblk in range(n // 128):
                b = st['cnt'] % 4
                st['cnt'] += 1
                pk = 'ps%d' % b
                for kc in range(8):
                    P.op('pe', lambda e, b=b, kc=kc, sblk=sblk, par=par: e.matmul(k.ps[b][:, 0:128], lhsT=aT_t[par][:, kc, sblk * 128:(sblk + 1) * 128],
                                                                        rhs=wv[:, kc, :], start=(kc == 0), stop=(kc == 7)),
                         reads=['wv', ak], writes=[pk])
                P.op('dve', lambda e, b=b, sblk=sblk, par=par: e.tensor_copy(out=vo[par][:, sblk, :], in_=k.ps[b][:, 0:128]),
                     reads=[pk], writes=['vo%d' % par])
            P.dma('sp', k.V[t0:t0 + n, :].rearrange("(s p) d -> p s d", p=128), vo[par][:, 0:n // 128, :], reads=['vo%d' % par], writes=['V'])
            group(ti, 0)
            if nxt is not None:
                part1(nxt)
            group(ti, 1)
            if nxt is not None:
                part2(nxt)
            group(ti, 2)


def phase_lru(k, l):
    P = k.P
    I = k.I
    with Ph(k, 'plru%d' % l) as ph:
        cw = ph.sb('cw', [128, 16], F32)
        cb = ph.sb('cb', [128, 4], F32)
        ba = ph.sb('ba', [128, 8], F32)
        bx = ph.sb('bx', [128, 8], F32)
        lam = ph.sb('lam', [128, 8], F32)
        c1 = ph.sb('c1', [128, 8], F32)
        load_vec_fm(k, ph, I['lru_conv_w'][l].rearrange("t (c p) -> (t c) p", p=128), 16, cw[:], 'cw')
        load_vec_fm(k, ph, I['lru_conv_b'][l].rearrange("(c p) -> c p", p=128), 4, cb[:], 'cb')
        load_vec_fm(k, ph, I['lru_ba'][l].rearrange("d (c p) -> (d c) p", p=128), 8, ba[:], 'ba')
        load_vec_fm(k, ph, I['lru_bx'][l].rearrange("d (c p) -> (d c) p", p=128), 8, bx[:], 'bx')
        load_vec_fm(k, ph, I['lru_lam'][l].rearrange("d (c p) -> (d c) p", p=128), 8, lam[:], 'lam')
        P.op('act', lambda e: e.activation(out=c1[:], in_=lam[:], func=AF.Exp, scale=-1.0), reads=['lam'], writes=['c1'])
        P.op('dve', lambda e: e.tensor_scalar(out=c1[:], in0=c1[:], scalar1=1.0, scalar2=None, op0=ALU.add), reads=['c1'], writes=['c1'])
        P.op('act', lambda e: e.activation(out=c1[:], in_=c1[:], func=AF.Ln), reads=['c1'], writes=['c1'])
        P.op('dve', lambda e: e.tensor_scalar(out=c1[:], in0=c1[:], scalar1=-4.0, scalar2=None, op0=ALU.mult), reads=['c1'], writes=['c1'])
        for t_, nm in ((cb, 'cb'), (ba, 'ba'), (bx, 'bx')):
            P.op('dve', lambda e, t_=t_: e.tensor_scalar(out=t_[:], in0=t_[:], scalar1=0.5, scalar2=None, op0=ALU.mult), reads=[nm], writes=[nm])
        bd = ph.sb('bd', [128, 16, 128], BF16)
        P.op('pool', lambda e: e.memset(bd[:], 0.0), writes=['bd'])
        for d in range(2):
            for gi, wn in enumerate(('lru_wa', 'lru_wx')):
                for ch in range(4):
                    idx = (d * 2 + gi) * 4 + ch
                    for hb in range(2):
                        wload(k, bd[hb * 64:(hb + 1) * 64, idx, hb * 64:(hb + 1) * 64], I[wn][l, d, ch * 2 + hb], 'bd')
        dgl = ph.sb('dgl', [128, 16, 128], BF16)
        P.op('dve', lambda e: e.tensor_tensor(out=dgl[:, :, :], in0=k.identF[:, :].unsqueeze(1).to_broadcast([128, 16, 128]),
                                              in1=cw[:, :].unsqueeze(2).to_broadcast([128, 16, 128]), op=ALU.mult),
             reads=['identF', 'cw'], writes=['dgl'])
        xp = ph.sb('xp0', [128, NT + 8], BF16)
        Uh_ = ph.sb('Uh0', [128, NT], BF16)
        Uhs = [Uh_, Uh_]
        Hf = ph.sb('Hf', [128, NT], F32)
        NA, NR, NI, NB = 4, 4, 3, 2
        As = [ph.sb('As%d' % i, [128, 2048], F32) for i in range(NA)]
        tRs = [ph.sb('tR%d' % i, [128, 2048], F32) for i in range(NR)]
        tIs = [ph.sb('tI%d' % i, [128, 2048], F32) for i in range(NI)]
        Bs = [ph.sb('Bs%d' % i, [128, 2048], F32) for i in range(NB)]
        gt = [ph.sb('gt%d' % i, [128, 2048], BF16) for i in range(2)]
        lyo = ph.sb('lyo', [128, 2048], BF16)
        hc = ph.sb('hc', [128, 2], F32)
        CO, LO = 2, 261
        xk = 'xp0'
        st = {'cnt': 0, 'c2': 0}
        spans = [(0, C)] + [(C + 2048 * h, 2048) for h in range(4)]
        items = []
        for ch in range(4):
            for d in range(2):
                order = spans if d == 0 else [spans[0]] + spans[:0:-1]
                for si, (s0, sn) in enumerate(order):
                    items.append((ch, d, si, s0, sn))

        def conv4(ch):
            Uh = Uhs[ch % 2]
            uk = 'Uh0'
            P.op('pool', lambda e: e.memset(xp[:, 0:2], 0.0), writes=[xk])
            P.op('pool', lambda e: e.memset(xp[:, 258:261], 0.0), writes=[xk])
            P.op('pool', lambda e: e.memset(xp[:, LO + S:LO + S + 3], 0.0), writes=[xk])
            P.dma('sp', xp[:, CO:CO + C], k.lxT[:, ch, 0:C], reads=['lxT'], writes=[xk])
            P.dma('sp', xp[:, LO:LO + S], k.lxT[:, ch, C:NT], reads=['lxT'], writes=[xk])
            for (o, u0, n_) in ((CO, 0, C), (LO, C, S)):
                for t0 in range(0, n_, 512):
                    m = min(512, n_ - t0)
                    b = 6 + (st['c2'] % 2)
                    st['c2'] += 1
                    pk = 'ps%d' % b
                    for tap in range(4):
                        P.op('pe', lambda e, b=b, tap=tap, o=o, t0=t0, m=m: e.matmul(k.ps[b][:, 0:m], lhsT=dgl[:, tap * 4 + ch, :],
                                                                                    rhs=xp[:, o + t0 + tap - 2:o + t0 + tap - 2 + m], start=(tap == 0), stop=(tap == 3)),
                             reads=['dgl', xk], writes=[pk])
                    P.op('act', lambda e, b=b, u0=u0, t0=t0, m=m: e.activation(out=Uh[:, u0 + t0:u0 + t0 + m], in_=k.ps[b][:, 0:m], func=AF.Identity,
                                                                              scale=0.5, bias=cb[:, ch:ch + 1]), reads=[pk, 'cb'], writes=[uk])

        def s1(i):
            ch, d, si, s0, sn = items[i]
            Uh, uk = Uhs[ch % 2], 'Uh0'
            tR, tI, A = tRs[i % NR], tIs[i % NI], As[i % NA]
            kR, kI, kA = 'tR%d' % (i % NR), 'tI%d' % (i % NI), 'As%d' % (i % NA)
            for gi, (dstG, gk, bias) in enumerate(((tR, kR, ba), (tI, kI, bx))):
                idx = (d * 2 + gi) * 4 + ch
                for sub in range(0, sn, 512):
                    m = min(512, sn - sub)
                    b = st['cnt'] % 6
                    st['cnt'] += 1
                    pk = 'ps%d' % b
                    P.op('pe', lambda e, b=b, idx=idx, sub=sub, m=m: e.matmul(k.ps[b][:, 0:m], lhsT=bd[:, idx, :], rhs=Uh[:, s0 + sub:s0 + sub + m], start=True, stop=True),
                         reads=['bd', uk], writes=[pk])
                    P.op('act', lambda e, b=b, dstG=dstG, sub=sub, m=m, bias=bias: e.activation(out=dstG[:, sub:sub + m], in_=k.ps[b][:, 0:m], func=AF.Tanh,
                                                                                               bias=bias[:, d * 4 + ch:d * 4 + ch + 1]),
                         reads=[pk, 'ba', 'bx'], writes=[gk])
            P.op('act', lambda e: e.activation(out=A[:, 0:sn], in_=tR[:, 0:sn], func=AF.Exp, scale=c1[:, d * 4 + ch:d * 4 + ch + 1],
                                               bias=c1[:, d * 4 + ch:d * 4 + ch + 1]), reads=[kR, 'c1'], writes=[kA])

        def s2a(i):
            ch, d, si, s0, sn = items[i]
            tR, A = tRs[i % NR], As[i % NA]
            P.op('dve', lambda e: e.tensor_tensor(out=tR[:, 0:sn], in0=A[:, 0:sn], in1=A[:, 0:sn], op=ALU.mult), reads=['As%d' % (i % NA)], writes=['tR%d' % (i % NR)])

        def s2b(i):
            ch, d, si, s0, sn = items[i]
            tR = tRs[i % NR]
            kR = 'tR%d' % (i % NR)
            P.op('act', lambda e: e.activation(out=tR[:, 0:sn], in_=tR[:, 0:sn], func=AF.Sqrt, scale=-1.0, bias=k.oneT[:, 0:1]), reads=[kR, 'oneT'], writes=[kR])

        def s3(i):
            ch, d, si, s0, sn = items[i]
            Uh, uk = Uhs[ch % 2], 'Uh0'
            tR, tI, B = tRs[i % NR], tIs[i % NI], Bs[i % NB]
            kR, kI, kB = 'tR%d' % (i % NR), 'tI%d' % (i % NI), 'Bs%d' % (i % NB)
            P.op('dve', lambda e: e.scalar_tensor_tensor(out=tR[:, 0:sn], in0=tI[:, 0:sn], scalar=1.0, in1=tR[:, 0:sn], op0=ALU.add, op1=ALU.mult),
                 reads=[kR, kI], writes=[kR])
            P.op('pool', lambda e: e.tensor_tensor(out=B[:, 0:sn], in0=tR[:, 0:sn], in1=Uh[:, s0:s0 + sn], op=ALU.mult), reads=[kR, uk], writes=[kB])

        def s4(i):
            ch, d, si, s0, sn = items[i]
            A, B = As[i % NA], Bs[i % NB]
            kA, kB = 'As%d' % (i % NA), 'Bs%d' % (i % NB)
            if d == 0:
                init = 0.0 if si == 0 else Hf[:, s0 - 1:s0]
                P.op('dve', lambda e: e.tensor_tensor_scan(out=Hf[:, s0:s0 + sn], data0=A[:, 0:sn], data1=B[:, 0:sn], initial=init,
                                                           op0=ALU.mult, op1=ALU.add), reads=[kA, kB, 'Hf'], writes=['Hf'])
            else:
                init = 0.0 if si == 0 else hc[:, (si - 1) % 2:(si - 1) % 2 + 1]
                P.op('dve', lambda e: e.tensor_tensor_scan(out=B[:, 0:sn][:, ::-1], data0=A[:, 0:sn][:, ::-1], data1=B[:, 0:sn][:, ::-1],
                                                           initial=init, op0=ALU.mult, op1=ALU.add), reads=[kA, kB, 'hc'], writes=[kB])
                P.op('dve', lambda e: e.tensor_copy(out=hc[:, si % 2:si % 2 + 1], in_=B[:, 0:1]), reads=[kB], writes=['hc'])
                par = i % 2
                P.dma('sp', gt[par][:, 0:sn], k.GT[:, ch, s0:s0 + sn], reads=['GT'], writes=['gt%d' % par])
                P.op('pool', lambda e: e.tensor_tensor(out=B[:, 0:sn], in0=Hf[:, s0:s0 + sn], in1=B[:, 0:sn], op=ALU.add), reads=['Hf', kB], writes=[kB])
                P.op('dve', lambda e: e.tensor_tensor(out=lyo[:, 0:sn], in0=B[:, 0:sn], in1=gt[par][:, 0:sn], op=ALU.mult),
                     reads=[kB, 'gt%d' % par], writes=['lyo'])
                P.dma('sp', k.ylT[:, ch, s0:s0 + sn], lyo[:, 0:sn], reads=['lyo'], writes=['ylT'])

        for ch in range(4):
            conv4(ch)
            lo, hi = ch * 10, ch * 10 + 10
            for t in range(lo, hi + 3):
                if t < hi:
                    s1(t)
                if lo <= t - 1 < hi:
                    s2b(t - 1)
                if lo <= t - 2 < hi:
                    s3(t - 2)
                if lo <= t - 3 < hi:
                    s4(t - 3)
                if t < hi:
                    s2a(t)


def phase_conv(k, l):
    P = k.P
    I = k.I
    need_ctx = l < L - 1
    with Ph(k, 'pcv%d' % l) as ph:
        dw = ph.sb('dw', [128, 124], F32)
        vb = ph.sb('vb', [128, 12], F32)
        load_vec_fm(k, ph, I['conv_dw_w'][l].rearrange("t (c p) -> (t c) p", p=128), 124, dw[:], 'dw')
        load_vec_fm(k, ph, I['conv_dw_b'][l].rearrange("(c p) -> c p", p=128), 4, vb[:, 0:4], 'vb0')
        load_vec_fm(k, ph, I['conv_ln_g'][l].rearrange("(c p) -> c p", p=128), 4, vb[:, 4:8], 'vb1')
        load_vec_fm(k, ph, I['conv_ln_b'][l].rearrange("(c p) -> c p", p=128), 4, vb[:, 8:12], 'vb2')
        vkeys = ['vb0', 'vb1', 'vb2']
        dg = ph.sb('dg', [128, 124, 128], BF16)
        for j0 in range(0, 124, 31):
            P.op('dve', lambda e, j0=j0: e.tensor_tensor(out=dg[:, j0:j0 + 31, :], in0=k.identF[:, :].unsqueeze(1).to_broadcast([128, 31, 128]),
                                                         in1=dw[:, j0:j0 + 31].unsqueeze(2).to_broadcast([128, 31, 128]), op=ALU.mult),
                 reads=['identF', 'dw'], writes=['dg'])
        hin = [ph.sb('hin%d' % i, [128, 4, 542], BF16) for i in range(2)]
        cxs = [ph.sb('cx%d' % i, [128, 4, 512], F32) for i in range(2)]
        xbs = [ph.sb('xb%d' % i, [128, 4, 512], BF16) for i in range(2)]
        xss = [ph.sb('xs%d' % i, [128, 4, 512], BF16) for i in range(2)]
        mean = ph.sb('mean', [128, 512], F32)
        var = ph.sb('var', [128, 512], F32)
        rs = ph.sb('rs', [128, 512], F32)
        yc = [ph.sb('yc%d' % i, [128, 4, 512], BF16) for i in range(2)]
        tiles = [ti for ti in range(len(TILES)) if not (ti == 0 and not need_ctx)]

        def load(ti):
            t0, n = TILES[ti]
            par = ti % 2
            hk = 'hin%d' % par
            lo_seq, hi_seq = (0, C) if ti == 0 else (C, NT)
            a0, a1 = max(t0 - 15, lo_seq), min(t0 + n + 15, hi_seq)
            P.op('pool', lambda e: e.memset(hin[par][:], 0.0), writes=[hk])
            P.dma('sp', hin[par][:, :, a0 - (t0 - 15):a1 - (t0 - 15)], k.hcT[:, :, a0:a1], reads=['hcT'], writes=[hk])

        def conv_mm(ti, chs):
            t0, n = TILES[ti]
            par = ti % 2
            hk = 'hin%d' % par
            cx, xb, xs = cxs[par], xbs[par], xss[par]
            ck, bk, sk = 'cx%d' % par, 'xb%d' % par, 'xs%d' % par
            for ch in chs:
                pk = 'ps%d' % ch
                for tap in range(31):
                    P.op('pe', lambda e, ch=ch, tap=tap: e.matmul(k.ps[ch][:, 0:n], lhsT=dg[:, tap * 4 + ch, :], rhs=hin[par][:, ch, tap:tap + n],
                                                                 start=(tap == 0), stop=(tap == 30)), reads=['dg', hk], writes=[pk])
                P.op('act', lambda e, ch=ch: e.activation(out=cx[:, ch, 0:n], in_=k.ps[ch][:, 0:n], func=AF.Identity, bias=vb[:, ch:ch + 1]),
                     reads=[pk] + vkeys, writes=[ck])
                P.op('act', lambda e, ch=ch: e.activation(out=xb[:, ch, 0:n], in_=k.ps[ch][:, 0:n], func=AF.Identity, bias=vb[:, ch:ch + 1]),
                     reads=[pk] + vkeys, writes=[bk])
                P.op('act', lambda e, ch=ch: e.activation(out=xs[:, ch, 0:n], in_=k.ps[ch][:, 0:n], func=AF.Square, bias=vb[:, ch:ch + 1]),
                     reads=[pk] + vkeys, writes=[sk])

        def stats_mm(ti):
            t0, n = TILES[ti]
            par = ti % 2
            xb, xs = xbs[par], xss[par]
            bk, sk = 'xb%d' % par, 'xs%d' % par
            for ch in range(4):
                P.op('pe', lambda e, ch=ch: e.matmul(k.ps[4][:, 0:n], lhsT=k.onesB[:], rhs=xb[:, ch, 0:n], start=(ch == 0), stop=(ch == 3)),
                     reads=['onesB', bk], writes=['ps4'])
            for ch in range(4):
                P.op('pe', lambda e, ch=ch: e.matmul(k.ps[5][:, 0:n], lhsT=k.onesB[:], rhs=xs[:, ch, 0:n], start=(ch == 0), stop=(ch == 3)),
                     reads=['onesB', sk], writes=['ps5'])

        def ln_tail(ti):
            t0, n = TILES[ti]
            par = ti % 2
            cx = cxs[par]
            ck = 'cx%d' % par
            P.op('dve', lambda e: e.tensor_scalar(out=mean[:, 0:n], in0=k.ps[4][:, 0:n], scalar1=1.0 / 512, scalar2=None, op0=ALU.mult), reads=['ps4'], writes=['mean'])
            P.op('dve', lambda e: e.tensor_tensor(out=var[:, 0:n], in0=mean[:, 0:n], in1=mean[:, 0:n], op=ALU.mult), reads=['mean'], writes=['var'])
            P.op('dve', lambda e: e.scalar_tensor_tensor(out=var[:, 0:n], in0=k.ps[5][:, 0:n], scalar=1.0 / 512, in1=var[:, 0:n], op0=ALU.mult, op1=ALU.subtract),
                 reads=['ps5', 'var'], writes=['var'])
            P.op('dve', lambda e: e.tensor_scalar(out=var[:, 0:n], in0=var[:, 0:n], scalar1=0.0, scalar2=None, op0=ALU.max), reads=['var'], writes=['var'])
            P.op('act', lambda e: e.activation(out=var[:, 0:n], in_=var[:, 0:n], func=AF.Sqrt, bias=k.epsT[:, 0:1]), reads=['var', 'epsT'], writes=['var'])
            P.op('dve', lambda e: e.reciprocal(out=rs[:, 0:n], in_=var[:, 0:n]), reads=['var'], writes=['rs'])
            P.op('dve', lambda e: e.tensor_tensor(out=cx[:, :, 0:n], in0=cx[:, :, 0:n], in1=mean[:, 0:n].unsqueeze(1).to_broadcast([128, 4, n]), op=ALU.subtract),
                 reads=[ck, 'mean'], writes=[ck])
            P.op('dve', lambda e: e.tensor_tensor(out=cx[:, :, 0:n], in0=cx[:, :, 0:n], in1=rs[:, 0:n].unsqueeze(1).to_broadcast([128, 4, n]), op=ALU.mult),
                 reads=[ck, 'rs'], writes=[ck])
            for ch in range(4):
                P.op('act', lambda e, ch=ch: e.activation(out=yc[par][:, ch, 0:n], in_=cx[:, ch, 0:n], func=AF.Silu,
                                                          scale=vb[:, 4 + ch:5 + ch], bias=vb[:, 8 + ch:9 + ch]),
                     reads=[ck] + vkeys, writes=['yc%d' % par])
            P.dma('sp', k.ycT[:, :, t0:t0 + n], yc[par][:, :, 0:n], reads=['yc%d' % par], writes=['ycT'])

        load(tiles[0])
        for idx, ti in enumerate(tiles):
            nxt = tiles[idx + 1] if idx + 1 < len(tiles) else None
            if nxt is not None:
                load(nxt)
            conv_mm(ti, [0])
            if idx > 0:
                stats_mm(tiles[idx - 1])
                ln_tail(tiles[idx - 1])
            conv_mm(ti, [1, 2, 3])
        stats_mm(tiles[-1])
        ln_tail(tiles[-1])


def phase_att(k, l):
    P = k.P
    I = k.I
    need_ctx = l < L - 1
    with Ph(k, 'pat%d' % l) as ph:
        kTs = ph.sb('kTs', [128, NT], BF16)
        Vs = ph.sb('Vs', [128, NBLK, 128], BF16)
        P.dma('sp', kTs[:], k.kT, reads=['kT'], writes=['kTs'])
        for part in range(0, NBLK, 11):
            P.dma('sp', Vs[:, part:part + 11, :], k.V[part * 128:(part + 11) * 128, :].rearrange("(s p) d -> p s d", p=128), reads=['V'], writes=['Vs'])
        snk = ph.sb('snk', [1, 8], F32)
        sx = ph.sb('sx', [128, 4], F32)
        P.dma('sp', snk[:], I['attn_sink'][l:l + 1, :], writes=['snk'])
        onesF = ph.sb('onesF', [1, 128], F32)
        P.op('dve', lambda e: e.memset(onesF[:], 1.0), writes=['onesF'])
        P.op('pe', lambda e: e.matmul(k.ps[7][:, 0:8], lhsT=onesF[:], rhs=snk[:], start=True, stop=True), reads=['onesF', 'snk'], writes=['ps7'])
        P.op('act', lambda e: e.activation(out=sx[0:64, :], in_=k.ps[7][0:64, 0:4], func=AF.Exp), reads=['ps7'], writes=['sx'])
        P.op('act', lambda e: e.activation(out=sx[64:128, :], in_=k.ps[7][64:128, 4:8], func=AF.Exp), reads=['ps7'], writes=['sx'])
        qs = [ph.sb('qs%d' % i, [128, 4, 128], BF16) for i in range(2)]
        pT = [ph.sb('pT%d' % i, [128, 512], BF16) for i in range(8)]
        den = ph.sb('den', [128, 512], F32)
        yo = [ph.sb('yo%d' % i, [128, 4, 128], BF16) for i in range(2)]
        cnt = 0
        pcnt = 0
        qblocks = list(range(2, NBLK)) + ([0, 1] if need_ctx else [])
        def qload(qi):
            P.dma('sp', qs[qi % 2][:], k.qT[:, :, qblocks[qi] * 128:qblocks[qi] * 128 + 128], reads=['qT'], writes=['qs%d' % (qi % 2)])
        qload(0)
        for qi, qb in enumerate(qblocks):
            par = qi % 2
            t0 = qb * 128
            qk = 'qs%d' % par
            if qi + 1 < len(qblocks):
                qload(qi + 1)
            if qb >= 2:
                kbs = []
                if qb > 2:
                    kbs.append((qb - 1, k.nmp, 'nmp'))
                kbs.append((qb, None, None))
                if qb < NBLK - 1:
                    kbs.append((qb + 1, k.nmn, 'nmn'))
                kbs += [(0, None, None), (1, None, None)]
            else:
                kbs = [(0, None, None), (1, None, None)]
            bo = 4 + (qi % 2)
            bd_ = 6 + (qi % 2)
            pko, pkd = 'ps%d' % bo, 'ps%d' % bd_
            for g in range(2):
                pb = g * 64
                pl = []
                for (kb, nm, nmk) in kbs:
                    b = cnt % 4
                    cnt += 1
                    pk = 'ps%d' % b
                    P.op('pe', lambda e, b=b, pb=pb, kb=kb, par=par, nm=nm: e.matmul(k.ps[b][:, :], lhsT=kTs[pb:pb + 64, kb * 128:(kb + 1) * 128],
                                                                                    rhs=qs[par][pb:pb + 64, :, :], start=True, stop=(nm is None)),
                         reads=['kTs', qk], writes=[pk])
                    if nm is not None:
                        P.op('pe', lambda e, b=b, nm=nm: e.matmul(k.ps[b][:, :], lhsT=k.identB[:], rhs=nm[:], start=False, stop=True),
                             reads=['identB', nmk], writes=[pk])
                    pi = pcnt % 8
                    pcnt += 1
                    P.op('act', lambda e, b=b, pi=pi: e.activation(out=pT[pi][:], in_=k.ps[b][:, :], func=AF.Exp, scale=0.125), reads=[pk], writes=['pT%d' % pi])
                    pl.append((kb, pi))
                for j, (kb, pi) in enumerate(pl):
                    P.op('pe', lambda e, bo=bo, pb=pb, kb=kb, pi=pi, j=j, nl=len(pl): e.matmul(k.ps[bo][pb:pb + 64, :], lhsT=Vs[:, kb, pb:pb + 64], rhs=pT[pi][:],
                                                                                              start=(j == 0), stop=(j == nl - 1)),
                         reads=['Vs', 'pT%d' % pi], writes=[pko])
                for j, (kb, pi) in enumerate(pl):
                    P.op('pe', lambda e, bd_=bd_, pb=pb, pi=pi, j=j, nl=len(pl): e.matmul(k.ps[bd_][pb:pb + 64, :], lhsT=k.onesB[:, 0:64], rhs=pT[pi][:],
                                                                                         start=(j == 0), stop=(j == nl - 1)),
                         reads=['onesB', 'pT%d' % pi], writes=[pkd])
            P.op('dve', lambda e, bd_=bd_: e.tensor_tensor(out=den[:, :].rearrange("p (c t) -> p c t", t=128), in0=k.ps[bd_][:, :].rearrange("p (c t) -> p c t", t=128),
                                                           in1=sx[:, :].unsqueeze(2).to_broadcast([128, 4, 128]), op=ALU.add), reads=[pkd, 'sx'], writes=['den'])
            P.op('dve', lambda e: e.reciprocal(out=den[:], in_=den[:]), reads=['den'], writes=['den'])
            P.op('dve', lambda e, bo=bo, par=par: e.tensor_tensor(out=yo[par][:, :, :].rearrange("p c t -> p (c t)"), in0=k.ps[bo][:, :], in1=den[:], op=ALU.mult),
                 reads=[pko, 'den'], writes=['yo%d' % par])
            P.dma('sp', k.yaT[:, :, t0:t0 + 128], yo[par][:], reads=['yo%d' % par], writes=['yaT'])


def phase_merge(k, l):
    P = k.P
    I = k.I
    need_ctx = l < L - 1
    with Ph(k, 'pmg%d' % l) as ph:
        wg = ph.sb('wg', [128, 8, 3072], BF16)
        wo3 = ph.sb('wo3', [128, 3, 4, 1024], BF16)
        wout = ph.sb('wout', [128, 8, 1024], BF16)
        for kc in range(8):
            wload(k, wg[:, kc, :], I['w_in'][l, kc * 128:(kc + 1) * 128, 2816:5888], 'wg')
        for c in range(4):
            wload(k, wo3[0:64, 0, c, :], I['w_o_attn'][l, c * 64:(c + 1) * 64, :], 'wo3')
            wload(k, wo3[64:128, 0, c, :], I['w_o_attn'][l, (c + 4) * 64:(c + 5) * 64, :], 'wo3')
            wload(k, wo3[:, 1, c, :], I['w_o_conv'][l, c * 128:(c + 1) * 128, :], 'wo3')
            wload(k, wo3[:, 2, c, :], I['w_o_lru'][l, c * 128:(c + 1) * 128, :], 'wo3')
        for kc in range(8):
            wload(k, wout[:, kc, :], I['w_out'][l, kc * 128:(kc + 1) * 128, :], 'wout')
        aT_t = [ph.sb('aT%d' % i, [128, 8, 512], BF16) for i in range(2)]
        y3 = [ph.sb('y3%d' % i, [128, 3, 4, 512], BF16) for i in range(2)]
        hT_t = [ph.sb('hT%d' % i, [128, 8, 512], F32) for i in range(2)]
        gs = [ph.sb('gs%d' % i, [128, 3, 512], F32) for i in range(2)]
        m1 = ph.sb('m1', [128, 512], F32)
        m2 = ph.sb('m2', [128, 512], F32)
        mg = ph.sb('mg', [128, 8, 512], BF16)
        ysrc = (k.yaT, k.ycT, k.ylT)
        ykeys = ('yaT', 'ycT', 'ylT')
        cnt = 0
        mtiles = [ti for ti in range(len(TILES)) if not (ti == 0 and not need_ctx)]

        def mload(ti):
            t0, n = TILES[ti]
            par = ti % 2
            P.dma('sp', aT_t[par][:, :, 0:n], k.aT[:, :, t0:t0 + n], reads=[('aT', ti)], writes=['aT%d' % par])
            for br in range(3):
                P.dma('sp', y3[par][:, br, :, 0:n], ysrc[br][:, :, t0:t0 + n], reads=[ykeys[br]], writes=['y3%d' % par])
            P.dma('sp', hT_t[par][:, :, 0:n], k.hT[:, :, t0:t0 + n], reads=[('hT', b) for b in range(t0 // 128, (t0 + n) // 128)], writes=['hT%d' % par])
        mload(mtiles[0])
        for mi, ti in enumerate(mtiles):
            t0, n = TILES[ti]
            par = ti % 2
            r = 1 if ti == 0 else 0
            ak, yk, hk = 'aT%d' % par, 'y3%d' % par, 'hT%d' % par
            if mi + 1 < len(mtiles):
                mload(mtiles[mi + 1])
            for m in range(8):
                gp = cnt % 2
                cnt += 1
                for br in range(3):
                    pk = 'ps%d' % br
                    for kc in range(8):
                        P.op('pe', lambda e, br=br, kc=kc, m=m, par=par, n=n: e.matmul(k.ps[br][:, 0:n], lhsT=wg[:, kc, br * 1024 + m * 128:br * 1024 + (m + 1) * 128],
                                                                                      rhs=aT_t[par][:, kc, 0:n], start=(kc == 0), stop=(kc == 7)),
                             reads=['wg', ak], writes=[pk])
                    P.op('act', lambda e, br=br, gp=gp, n=n: e.activation(out=gs[gp][:, br, 0:n], in_=k.ps[br][:, 0:n], func=AF.Sigmoid), reads=[pk], writes=['gs%d' % gp])
                for br in range(3):
                    pk = 'ps%d' % (3 + br)
                    for kc in range(4):
                        P.op('pe', lambda e, br=br, kc=kc, m=m, par=par, n=n: e.matmul(k.ps[3 + br][:, 0:n], lhsT=wo3[:, br, kc, m * 128:(m + 1) * 128],
                                                                                      rhs=y3[par][:, br, kc, 0:n], start=(kc == 0), stop=(kc == 3)),
                             reads=['wo3', yk], writes=[pk])
                gk = 'gs%d' % gp
                P.op('dve', lambda e, gp=gp, n=n: e.tensor_tensor(out=m1[:, 0:n], in0=k.ps[3][:, 0:n], in1=gs[gp][:, 0, 0:n], op=ALU.mult), reads=['ps3', gk], writes=['m1'])
                P.op('dve', lambda e, gp=gp, n=n: e.tensor_tensor(out=m2[:, 0:n], in0=k.ps[4][:, 0:n], in1=gs[gp][:, 1, 0:n], op=ALU.mult), reads=['ps4', gk], writes=['m2'])
                P.op('pool', lambda e, n=n: e.tensor_tensor(out=m1[:, 0:n], in0=m1[:, 0:n], in1=m2[:, 0:n], op=ALU.add), reads=['m1', 'm2'], writes=['m1'])
                P.op('dve', lambda e, gp=gp, n=n: e.tensor_tensor(out=m2[:, 0:n], in0=k.ps[5][:, 0:n], in1=gs[gp][:, 2, 0:n], op=ALU.mult), reads=['ps5', gk], writes=['m2'])
                P.op('pool', lambda e, n=n, m=m: e.tensor_tensor(out=mg[:, m, 0:n], in0=m1[:, 0:n], in1=m2[:, 0:n], op=ALU.add), reads=['m1', 'm2'], writes=['mg'])
            for m in range(8):
                b = 6 + (m % 2)
                pk = 'ps%d' % b
                for kc in range(8):
                    P.op('pe', lambda e, b=b, kc=kc, m=m, n=n: e.matmul(k.ps[b][:, 0:n], lhsT=wout[:, kc, m * 128:(m + 1) * 128], rhs=mg[:, kc, 0:n],
                                                                       start=(kc == 0), stop=(kc == 7)), reads=['wout', 'mg'], writes=[pk])
                P.op('dve', lambda e, b=b, m=m, n=n, par=par, r=r: e.scalar_tensor_tensor(out=hT_t[par][:, m, 0:n], in0=k.ps[b][:, 0:n], scalar=k.mods[:, r, 2, m:m + 1],
                                                                                         in1=hT_t[par][:, m, 0:n], op0=ALU.mult, op1=ALU.add),
                     reads=[pk, 'mods', hk], writes=[hk])
            P.dma('sp', k.hT[:, :, t0:t0 + n], hT_t[par][:, :, 0:n], reads=[hk], writes=[('hT', b) for b in range(t0 // 128, (t0 + n) // 128)])


def phase_ffn(k, l):
    P = k.P
    I = k.I
    need_ctx = l < L - 1
    FT = 256
    n = FT
    with Ph(k, 'pff%d' % l) as ph:
        wup = ph.sb('wup', [128, 8, 2 * FH], BF16)
        wdn = ph.sb('wdn', [128, 22, 1024], BF16)
        for piece in (0, 2, 1, 3):
            for kc in range(8):
                wload(k, wup[:, kc, piece * 1408:(piece + 1) * 1408], I['ffn_w_up'][l, kc * 128:(kc + 1) * 128, piece * 1408:(piece + 1) * 1408], 'wup%d' % piece)
        for kc in range(22):
            wload(k, wdn[:, kc, :], I['ffn_w_down'][l, kc * 128:(kc + 1) * 128, :], 'wdn')
        hT_t = [ph.sb('hT%d' % i, [128, 8, FT], F32) for i in range(2)]
        sq = ph.sb('sq', [128, 8, FT], BF16)
        tmpn = ph.sb('tmpn', [128, FT], F32)
        rstd = ph.sb('rstd', [128, FT], F32)
        hn = ph.sb('hn', [128, 8, FT], F32)
        a2s = [ph.sb('a2%d' % i, [128, 8, FT], BF16) for i in range(2)]
        sg = [ph.sb('sg%d' % i, [128, FT], F32) for i in range(2)]
        hid = ph.sb('hid', [128, 22, FT], BF16)
        tiles = [ti for ti in range(NT // FT) if not (ti == 0 and not need_ctx)]

        def hkeys(ti):
            return [('hT', b) for b in range(ti * FT // 128, (ti * FT + n) // 128)]

        def load(ti):
            P.dma('sp', hT_t[ti % 2][:, :, 0:n], k.hT[:, :, ti * FT:ti * FT + n], reads=hkeys(ti), writes=['hT%d' % (ti % 2)])

        def part1(ti):
            norm_rstd(k, ph, hT_t[ti % 2], 'hT%d' % (ti % 2), n, sq, 'sq', rstd, 'rstd', tmpn, 'tmpn')

        def part2(ti):
            par = ti % 2
            r = 1 if ti == 0 else 0
            P.op('dve', lambda e: e.tensor_tensor(out=hn[:, :, 0:n], in0=hT_t[par][:, :, 0:n],
                                                   in1=rstd[:, 0:n].unsqueeze(1).to_broadcast([128, 8, n]), op=ALU.mult),
                 reads=['hT%d' % par, 'rstd'], writes=['hn'])
            for c in range(8):
                P.op('act', lambda e, c=c: e.activation(out=a2s[par][:, c, 0:n], in_=hn[:, c, 0:n], func=AF.Identity,
                                                        scale=k.A2[:, r, c:c + 1], bias=k.mods[:, r, 3, c:c + 1]),
                     reads=['hn', 'A2', 'mods'], writes=['a2%d' % par])

        def down(ti, m):
            par = ti % 2
            r = 1 if ti == 0 else 0
            hk = 'hT%d' % par
            b = 4 + (m % 3)
            pk = 'ps%d' % b
            for kc in range(22):
                P.op('pe', lambda e, kc=kc: e.matmul(k.ps[b][:, 0:n], lhsT=wdn[:, kc, m * 128:(m + 1) * 128], rhs=hid[:, kc, 0:n],
                                                     start=(kc == 0), stop=(kc == 21)), reads=['wdn', 'hid'], writes=[pk])
            P.op('dve', lambda e: e.scalar_tensor_tensor(out=hT_t[par][:, m, 0:n], in0=k.ps[b][:, 0:n], scalar=k.mods[:, r, 5, m:m + 1],
                                                         in1=hT_t[par][:, m, 0:n], op0=ALU.mult, op1=ALU.add),
                 reads=[pk, 'mods', hk], writes=[hk])

        load(tiles[0])
        part1(tiles[0])
        part2(tiles[0])
        for idx, ti in enumerate(tiles):
            par = ti % 2
            a2 = a2s[par]
            ak = 'a2%d' % par
            nxt = tiles[idx + 1] if idx + 1 < len(tiles) else None
            if nxt is not None:
                load(nxt)
            for j in range(22):
                bu, bg = 2 * (j % 2), 2 * (j % 2) + 1
                pku, pkg = 'ps%d' % bu, 'ps%d' % bg
                s2 = j % 2
                for kc in range(8):
                    P.op('pe', lambda e, bu=bu, kc=kc, j=j, a2=a2: e.matmul(k.ps[bu][:, 0:n], lhsT=wup[:, kc, j * 128:(j + 1) * 128], rhs=a2[:, kc, 0:n],
                                                                    start=(kc == 0), stop=(kc == 7)), reads=['wup%d' % (j // 11), ak], writes=[pku])
                for kc in range(8):
                    P.op('pe', lambda e, bg=bg, kc=kc, j=j, a2=a2: e.matmul(k.ps[bg][:, 0:n], lhsT=wup[:, kc, FH + j * 128:FH + (j + 1) * 128], rhs=a2[:, kc, 0:n],
                                                                    start=(kc == 0), stop=(kc == 7)), reads=['wup%d' % (2 + j // 11), ak], writes=[pkg])
                P.op('act', lambda e, bg=bg, s2=s2: e.activation(out=sg[s2][:, 0:n], in_=k.ps[bg][:, 0:n], func=AF.Silu), reads=[pkg], writes=['sg%d' % s2])
                P.op('dve', lambda e, bu=bu, s2=s2, j=j: e.tensor_tensor(out=hid[:, j, 0:n], in0=k.ps[bu][:, 0:n], in1=sg[s2][:, 0:n], op=ALU.mult),
                     reads=[pku, 'sg%d' % s2], writes=['hid'])
            for m in range(4):
                down(ti, m)
            if nxt is not None:
                part1(nxt)
                part2(nxt)
            for m in range(4, 8):
                down(ti, m)
            P.dma('sp', k.hT[:, :, ti * FT:ti * FT + n], hT_t[par][:, :, 0:n], reads=['hT%d' % par], writes=hkeys(ti))


def phase_out(k):
    P = k.P
    I = k.I
    with Ph(k, 'pout') as ph:
        gf = ph.sb('gf', [128, 8], F32)
        load_vec_fm(k, ph, I['final_norm_g'].rearrange("(c p) -> c p", p=128), 8, gf[:], 'gf')
        hT_t = [ph.sb('hT%d' % i, [128, 8, 512], F32) for i in range(2)]
        sq = ph.sb('sq', [128, 8, 512], BF16)
        tmpn = ph.sb('tmpn', [128, 512], F32)
        rstd = ph.sb('rstd', [128, 512], F32)
        hn = ph.sb('hn', [128, 8, 512], F32)
        ot = [ph.sb('ot%d' % i, [128, D], F32) for i in range(2)]
        cnt = 0

        def oload(ti):
            t0, n = TILES[ti]
            P.dma('sp', hT_t[ti % 2][:, :, 0:n], k.hT[:, :, t0:t0 + n], reads=[('hT', b) for b in range(t0 // 128, (t0 + n) // 128)], writes=['hT%d' % (ti % 2)])
        oload(1)
        for ti, (t0, n) in enumerate(TILES):
            if ti == 0:
                continue
            par = ti % 2
            hk = 'hT%d' % par
            if ti + 1 < len(TILES):
                oload(ti + 1)
            norm_rstd(k, ph, hT_t[par], hk, n, sq, 'sq', rstd, 'rstd', tmpn, 'tmpn')
            P.op('dve', lambda e, par=par, n=n: e.tensor_tensor(out=hn[:, :, 0:n], in0=hT_t[par][:, :, 0:n],
                                                                 in1=rstd[:, 0:n].unsqueeze(1).to_broadcast([128, 8, n]), op=ALU.mult),
                 reads=[hk, 'rstd'], writes=['hn'])
            for c in range(8):
                P.op('act', lambda e, n=n, c=c: e.activation(out=hn[:, c, 0:n], in_=hn[:, c, 0:n], func=AF.Identity, scale=gf[:, c:c + 1]),
                     reads=['hn', 'gf'], writes=['hn'])
            for sblk in range(4):
                op_ = cnt % 2
                cnt += 1
                ok = 'ot%d' % op_
                for half in range(2):
                    b = 2 * op_ + half
                    pk = 'ps%d' % b
                    for j in range(4):
                        c = 4 * half + j
                        P.op('pe', lambda e, b=b, j=j, c=c, sblk=sblk: e.transpose(out=k.ps[b][:, j * 128:(j + 1) * 128], in_=hn[:, c, sblk * 128:(sblk + 1) * 128],
                                                                                   identity=k.identF[:]), reads=['hn', 'identF'], writes=[pk])
                    if half == 0:
                        P.op('act', lambda e, b=b, op_=op_: e.activation(out=ot[op_][:, 0:512], in_=k.ps[b][:, :], func=AF.Copy), reads=[pk], writes=[ok])
                    else:
                        P.op('dve', lambda e, b=b, op_=op_: e.tensor_copy(out=ot[op_][:, 512:1024], in_=k.ps[b][:, :]), reads=[pk], writes=[ok])
                r0 = t0 - C + sblk * 128
                P.dma('sp', k.out[r0:r0 + 128, :], ot[op_][:], reads=[ok], writes=[('out', r0)])
        k.P.barrier(final=True)


def _consts():
    f32 = np.float32
    rows = S // 64
    row = np.repeat(np.arange(rows, dtype=f32), 64)
    col = np.tile(np.arange(64, dtype=f32), rows)
    inv = np.power(f32(10000.0), -np.arange(16, dtype=f32) / f32(16)).astype(f32)
    ang = np.concatenate([row[:, None] * inv[None], col[:, None] * inv[None]], axis=-1).astype(f32)
    cos, sin = np.cos(ang).astype(f32), np.sin(ang).astype(f32)
    pidx = (np.arange(128) % 64) % 32
    cos2 = np.ascontiguousarray(cos[:, pidx].T)
    sin2 = np.ascontiguousarray(sin[:, pidx].T)
    rot = np.zeros((128, 128), f32)
    for m in range(128):
        if (m % 64) < 32:
            rot[m + 32, m] = -1.0
        else:
            rot[m - 32, m] = 1.0
    ident = np.eye(128, dtype=f32)
    j = np.arange(128)[:, None]
    i = np.arange(128)[None, :]
    nm_prev = np.where(j >= i, 0.0, -30000.0).astype(f32)
    nm_next = np.where(j <= i, 0.0, -30000.0).astype(f32)
    return {'cos2': cos2, 'sin2': sin2, 'rotm': rot, 'ident': ident,
            'nm_prev': np.ascontiguousarray(np.tile(nm_prev, (1, 4))), 'nm_next': np.ascontiguousarray(np.tile(nm_next, (1, 4)))}


_NC = None


def kernel(**inputs):
    global _NC
    inp = {n: np.ascontiguousarray(np.asarray(v, dtype=np.float32)) for n, v in inputs.items()}
    if _NC is None:
        _NC = build()
    consts = _consts()
    shared = {n: inp[n] for n in ('mod_w', 'mod_b', 'norm1_g', 'norm2_g', 'w_in', 'attn_sink', 'conv_dw_w', 'conv_dw_b', 'conv_ln_g',
                                  'conv_ln_b', 'lru_conv_w', 'lru_conv_b', 'lru_wa', 'lru_ba', 'lru_wx', 'lru_bx', 'lru_lam',
                                  'w_o_attn', 'w_o_conv', 'w_o_lru', 'w_out', 'ffn_w_up', 'ffn_w_down', 'final_norm_g')}
    shared.update(consts)
    in_maps = []
    for core in range(8):
        b = core % 2
        m = dict(shared)
        m['x'] = inp['x'][b]
        m['ctx'] = inp['ctx'][b]
        m['cvec'] = np.ascontiguousarray(np.stack([inp['c'][b], inp['c_ctx']], axis=0))
        in_maps.append(m)
    res = run_bass_kernel_spmd(_NC, in_maps, core_ids=list(range(8)))
    return np.stack([res.results[0]['out'], res.results[1]['out']], axis=0).astype(np.float32)
```

```python
import contextlib
import numpy as np
import concourse.bass as bass
import concourse.mybir as mybir
from concourse.bass_utils import run_bass_kernel_spmd

F32 = mybir.dt.float32
BF16 = mybir.dt.bfloat16
AF = mybir.ActivationFunctionType
ALU = mybir.AluOpType
AX = mybir.AxisListType

ENGS = ['pe', 'act', 'dve', 'pool', 'sp']
NDMA = 8

S = 8192
C = 256
NT = S + C
D = 1024
L = 2
FH = 2816
EPS = 1e-6
TILES = [(0, 256)] + [(256 + 512 * i, 512) for i in range(16)]
NBLK = NT // 128


class _StopProbe:
    def __init__(self):
        self.stop = True

    def matmul(self, *a, **kw):
        self.stop = bool(kw.get('stop', True))
        return self

    def transpose(self, *a, **kw):
        self.stop = True
        return self


class Prog:
    def __init__(self, nc):
        self.nc = nc
        self.thunks = {e: [] for e in ENGS}
        self.count = {e: 0 for e in ENGS}
        self.waited = {e: {} for e in ENGS}
        self.res_w = {}
        self.res_r = {}
        self.dma_rr = {e: 0 for e in ENGS}
        self.dma_cnt = {}
        self.sem = {}
        self.semnames = list(ENGS[:4]) + ['pe_1', 'pe_2', 'pe_3', 'pe_4', 'pe_5']
        self.engsem = {e: e for e in ENGS}
        self.nspare = 0
        for q in ('sp', 'pool'):
            for i in range(NDMA):
                self.semnames.append('d_%s_%d' % (q, i))

    def alloc_sems(self, stack):
        for s in self.semnames:
            self.sem[s] = stack.enter_context(self.nc.semaphore(s))

    def _collect(self, eng, reads, writes, extra=()):
        waits = {}

        def need(ev):
            if ev is None:
                return
            sk, val, src = ev
            if src == eng and eng == 'pe':
                return
            if self.waited[eng].get(sk, 0) >= val:
                return
            if waits.get(sk, 0) < val:
                waits[sk] = val
        for k in reads:
            need(self.res_w.get(k))
            if isinstance(k, str) and k.startswith('ps'):
                for ev in self.res_r.get(k, {}).values():
                    if ev[2] != eng:
                        need(ev)
        for k in writes:
            need(self.res_w.get(k))
            for ev in self.res_r.get(k, {}).values():
                need(ev)
        for ev in extra:
            need(ev)
        for sk, val in waits.items():
            self.waited[eng][sk] = val
        return list(waits.items())

    def _record(self, ev, reads, writes):
        for k in reads:
            d = self.res_r.setdefault(k, {})
            old = d.get(ev[0])
            if old is None or old[1] < ev[1]:
                d[ev[0]] = ev
        for k in writes:
            self.res_w[k] = ev
            self.res_r[k] = {}

    def op(self, eng, fn, reads=(), writes=()):
        wl = self._collect(eng, reads, writes)
        mysem = self.engsem[eng]
        inc = True
        if eng == 'pe':
            pr = _StopProbe()
            fn(pr)
            inc = pr.stop
        if inc:
            self.count[eng] += 1
            ev = (mysem, self.count[eng], eng)
        else:
            ev = (mysem, self.count[eng] + 1, eng)
        sem = self.sem

        def thunk(e):
            for sk, val in wl:
                e.wait_ge(sem[sk], val)
            ins = fn(e)
            if inc:
                ins.then_inc(sem[mysem], 1)
        self.thunks[eng].append(thunk)
        self._record(ev, reads, writes)

    def dma(self, q, out, in_, reads=(), writes=(), **kw):
        i = self.dma_rr[q] % NDMA
        self.dma_rr[q] += 1
        sk = 'd_%s_%d' % (q, i)
        n = self.dma_cnt.get(sk, 0) + 1
        self.dma_cnt[sk] = n
        extra = [(sk, 16 * (n - 1), 'dma')] if n > 1 else []
        wl = self._collect(q, reads, writes, extra)
        ev = (sk, 16 * n, 'dma')
        sem = self.sem

        def thunk(e):
            for s, val in wl:
                e.wait_ge(sem[s], val)
            e.dma_start(out=out, in_=in_, **kw).then_inc(sem[sk], 16)
        self.thunks[q].append(thunk)
        self._record(ev, reads, writes)

    def barrier(self, final=False):
        tot = []
        for e in ENGS[:4]:
            if self.count[e]:
                tot.append((self.engsem[e], self.count[e], e))
        for sk, n in self.dma_cnt.items():
            tot.append((sk, 16 * n, 'dma'))
        sem = self.sem
        for eng in (['sp'] if final else ENGS):
            wl = []
            for sk, val, src in tot:
                if src == eng:
                    continue
                if self.waited[eng].get(sk, 0) >= val:
                    continue
                self.waited[eng][sk] = val
                wl.append((sk, val))

            def thunk(e, wl=wl):
                for s, val in wl:
                    e.wait_ge(sem[s], val)
            self.thunks[eng].append(thunk)
        self.res_w.clear()
        self.res_r.clear()
        if not final and self.count['pe'] > 12000 and self.nspare < 5:
            self.nspare += 1
            self.engsem['pe'] = 'pe_%d' % self.nspare
            self.count['pe'] = 0

    def flush(self):
        nc = self.nc
        th = self.thunks
        with nc.Block() as block:
            @block.sync
            def _(e):
                for t in th['sp']:
                    t(e)

            @block.tensor
            def _(e):
                for t in th['pe']:
                    t(e)

            @block.scalar
            def _(e):
                for t in th['act']:
                    t(e)

            @block.vector
            def _(e):
                for t in th['dve']:
                    t(e)

            @block.gpsimd
            def _(e):
                for t in th['pool']:
                    t(e)
        self.thunks = {e: [] for e in ENGS}


class K:
    pass


def build(stop=None, debug=False):
    nc = bass.Bass("TRN2", target_bir_lowering=False)
    k = K()
    k.nc = nc

    def din(name, shape):
        return nc.dram_tensor(name, list(shape), F32, kind="ExternalInput").ap()

    def dscr(name, shape, dt):
        if debug:
            return nc.dram_tensor(name, list(shape), dt, kind="ExternalOutput").ap()
        return nc.dram_tensor(name, list(shape), dt).ap()
    I = {}
    I['x'] = din('x', [S, D])
    I['ctx'] = din('ctx', [C, D])
    I['cvec'] = din('cvec', [2, D])
    I['mod_w'] = din('mod_w', [L, D, 6 * D])
    I['mod_b'] = din('mod_b', [L, 6 * D])
    I['norm1_g'] = din('norm1_g', [L, D])
    I['norm2_g'] = din('norm2_g', [L, D])
    I['w_in'] = din('w_in', [L, D, 5888])
    I['attn_sink'] = din('attn_sink', [L, 8])
    I['conv_dw_w'] = din('conv_dw_w', [L, 31, 512])
    I['conv_dw_b'] = din('conv_dw_b', [L, 512])
    I['conv_ln_g'] = din('conv_ln_g', [L, 512])
    I['conv_ln_b'] = din('conv_ln_b', [L, 512])
    I['lru_conv_w'] = din('lru_conv_w', [L, 4, 512])
    I['lru_conv_b'] = din('lru_conv_b', [L, 512])
    I['lru_wa'] = din('lru_wa', [L, 2, 8, 64, 64])
    I['lru_ba'] = din('lru_ba', [L, 2, 512])
    I['lru_wx'] = din('lru_wx', [L, 2, 8, 64, 64])
    I['lru_bx'] = din('lru_bx', [L, 2, 512])
    I['lru_lam'] = din('lru_lam', [L, 2, 512])
    I['w_o_attn'] = din('w_o_attn', [L, 512, D])
    I['w_o_conv'] = din('w_o_conv', [L, 512, D])
    I['w_o_lru'] = din('w_o_lru', [L, 512, D])
    I['w_out'] = din('w_out', [L, D, D])
    I['ffn_w_up'] = din('ffn_w_up', [L, D, 2 * FH])
    I['ffn_w_down'] = din('ffn_w_down', [L, FH, D])
    I['final_norm_g'] = din('final_norm_g', [D])
    I['cos2'] = din('cos2', [128, S])
    I['sin2'] = din('sin2', [128, S])
    I['rotm'] = din('rotm', [128, 128])
    I['ident'] = din('ident', [128, 128])
    I['nm_prev'] = din('nm_prev', [128, 512])
    I['nm_next'] = din('nm_next', [128, 512])
    k.I = I
    k.out = nc.dram_tensor('out', [S, D], F32, kind="ExternalOutput").ap()
    k.hT = dscr('hT', [128, 8, NT], F32)
    k.aT = dscr('aT', [128, 8, NT], BF16)
    k.qT = dscr('qT', [128, 4, NT], BF16)
    k.kT = dscr('kT', [128, NT], BF16)
    k.V = dscr('Vtok', [NT, 128], BF16)
    k.hcT = dscr('hcT', [128, 4, NT], BF16)
    k.lxT = dscr('lxT', [128, 4, NT], BF16)
    k.GT = dscr('GT', [128, 4, NT], BF16)
    k.yaT = dscr('yaT', [128, 4, NT], BF16)
    k.ycT = dscr('ycT', [128, 4, NT], BF16)
    k.ylT = dscr('ylT', [128, 4, NT], BF16)

    P = Prog(nc)
    k.P = P
    with contextlib.ExitStack() as top:
        P.alloc_sems(top)
        sbt = lambda name, shape, dt: top.enter_context(nc.sbuf_tensor(name, list(shape), dt))
        k.ps = [top.enter_context(nc.psum_tensor('ps%d' % i, [128, 512], F32)) for i in range(8)]
        k.identF = sbt('identF', [128, 128], F32)
        k.identB = sbt('identB', [128, 128], BF16)
        k.onesB = sbt('onesB', [128, 128], BF16)
        k.rotB = sbt('rotB', [128, 128], BF16)
        k.nmp = sbt('nmp', [128, 512], BF16)
        k.nmn = sbt('nmn', [128, 512], BF16)
        k.mods = sbt('mods', [128, 2, 6, 8], F32)
        k.A1 = sbt('A1', [128, 2, 8], F32)
        k.A2 = sbt('A2', [128, 2, 8], F32)
        P.dma('sp', k.identF[:], I['ident'], writes=['identF'])
        P.dma('pool', k.identB[:], I['ident'], writes=['identB'])
        P.dma('pool', k.rotB[:], I['rotm'], writes=['rotB'])
        P.dma('pool', k.nmp[:], I['nm_prev'], writes=['nmp'])
        P.dma('pool', k.nmn[:], I['nm_next'], writes=['nmn'])
        P.op('dve', lambda e: e.memset(k.onesB[:], 1.0), writes=['onesB'])
        k.epsT = sbt('epsT', [128, 1], F32)
        k.oneT = sbt('oneT', [128, 1], F32)
        P.op('dve', lambda e: e.memset(k.epsT[:], EPS), writes=['epsT'])
        P.op('dve', lambda e: e.memset(k.oneT[:], 1.0), writes=['oneT'])
        P.barrier()
        P.flush()
        plist = [('in', lambda: phase_in(k))]
        for l in range(L):
            plist += [('mod%d' % l, lambda l=l: phase_mod(k, l)), ('a%d' % l, lambda l=l: phase_a(k, l)),
                      ('lru%d' % l, lambda l=l: phase_lru(k, l)), ('conv%d' % l, lambda l=l: phase_conv(k, l)),
                      ('att%d' % l, lambda l=l: phase_att(k, l)), ('merge%d' % l, lambda l=l: phase_merge(k, l)),
                      ('ffn%d' % l, lambda l=l: phase_ffn(k, l))]
        plist.append(('out', lambda: phase_out(k)))
        for name, fn in plist:
            fn()
            if stop == name:
                break
        if stop is not None and stop != 'out':
            with Ph(k, 'pfin') as ph:
                if debug:
                    dbg = ph.sb('dbg', [128, 2 * 6 * 8 + 32], F32)
                    k.dbgout = nc.dram_tensor('dbgout', [128, 128], F32, kind="ExternalOutput").ap()
                    P.op('dve', lambda e: e.tensor_copy(out=dbg[:, 0:96], in_=k.mods[:, :, :, :].rearrange("p a b c -> p (a b c)")), reads=['mods'], writes=['dbg'])
                    P.op('dve', lambda e: e.tensor_copy(out=dbg[:, 96:112], in_=k.A1[:, :, :].rearrange("p a b -> p (a b)")), reads=['A1', 'dbg'], writes=['dbg'])
                    P.op('dve', lambda e: e.tensor_copy(out=dbg[:, 112:128], in_=k.A2[:, :, :].rearrange("p a b -> p (a b)")), reads=['A2', 'dbg'], writes=['dbg'])
                    P.dma('sp', k.dbgout, dbg[:], reads=['dbg'])
    return nc


class Ph:
    def __init__(self, k, name):
        self.k = k
        self.name = name
        self.st = contextlib.ExitStack()
        self.n = 0

    def __enter__(self):
        self.st.__enter__()
        return self

    def sb(self, name, shape, dt):
        self.n += 1
        return self.st.enter_context(self.k.nc.sbuf_tensor('%s_%s' % (self.name, name), list(shape), dt))

    def __exit__(self, *a):
        self.k.P.barrier()
        self.k.P.flush()
        return self.st.__exit__(*a)


def load_vec_fm(k, ph, src_rows_ap, n, dst_ap, tag, bank=7):
    P = k.P
    stg = ph.sb('vst_' + tag, [128, 128], F32)
    key = 'vst_' + tag
    P.dma('sp', stg[0:n, :], src_rows_ap, writes=[key])
    psb = k.ps[bank]
    pk = 'ps%d' % bank
    P.op('pe', lambda e: e.transpose(out=psb[:, 0:n], in_=stg[0:n, :], identity=k.identF[0:n, 0:n]),
         reads=[key, 'identF'], writes=[pk])
    P.op('dve', lambda e: e.tensor_copy(out=dst_ap, in_=psb[:, 0:n]), reads=[pk], writes=[tag])


def norm_rstd(k, ph, hT_t, hkey, n, sq, sqkey, rstd, rkey, tmp, tkey, bank=7):
    P = k.P
    psb = k.ps[bank]
    pk = 'ps%d' % bank
    P.op('act', lambda e: e.activation(out=sq[:, :, 0:n], in_=hT_t[:, :, 0:n], func=AF.Square), reads=[hkey], writes=[sqkey])
    for c in range(8):
        P.op('pe', lambda e, c=c: e.matmul(psb[:, 0:n], lhsT=k.onesB[:], rhs=sq[:, c, 0:n], start=(c == 0), stop=(c == 7)),
             reads=[sqkey, 'onesB'], writes=[pk])
    P.op('act', lambda e: e.activation(out=tmp[:, 0:n], in_=psb[:, 0:n], func=AF.Sqrt, scale=1.0 / D, bias=k.epsT[:, 0:1]),
         reads=[pk, 'epsT'], writes=[tkey])
    P.op('dve', lambda e: e.reciprocal(out=rstd[:, 0:n], in_=tmp[:, 0:n]), reads=[tkey], writes=[rkey])


def wload(k, dst, src, key):
    k.P.dma('pool', dst, src, writes=[key])


def phase_in(k):
    P = k.P
    with Ph(k, 'pin') as ph:
        xin = [ph.sb('xin%d' % i, [128, D], F32) for i in range(2)]
        hst = [ph.sb('hst%d' % i, [128, 8, 128], F32) for i in range(2)]
        def load(s):
            src = k.I['ctx'][s * 128:(s + 1) * 128, :] if s < 2 else k.I['x'][(s - 2) * 128:(s - 1) * 128, :]
            P.dma('sp', xin[s % 2][:], src, writes=['xin%d' % (s % 2)])
        load(0)
        for s in range(NBLK):
            par = s % 2
            xk, hk = 'xin%d' % par, 'hst%d' % par
            if s + 1 < NBLK:
                load(s + 1)
            for half in range(2):
                b = 2 * par + half
                pk = 'ps%d' % b
                for j in range(4):
                    c = 4 * half + j
                    P.op('pe', lambda e, b=b, j=j, c=c, par=par: e.transpose(out=k.ps[b][:, j * 128:(j + 1) * 128],
                                                                             in_=xin[par][:, c * 128:(c + 1) * 128], identity=k.identF[:]),
                         reads=[xk, 'identF'], writes=[pk])
                eng = 'act' if half == 0 else 'dve'
                if eng == 'act':
                    P.op('act', lambda e, b=b, half=half, par=par: e.activation(
                        out=hst[par][:, 4 * half:4 * half + 4, :], in_=k.ps[b][:, :].rearrange("p (j t) -> p j t", t=128), func=AF.Copy),
                        reads=[pk], writes=[hk])
                else:
                    P.op('dve', lambda e, b=b, half=half, par=par: e.tensor_copy(
                        out=hst[par][:, 4 * half:4 * half + 4, :], in_=k.ps[b][:, :].rearrange("p (j t) -> p j t", t=128)),
                        reads=[pk], writes=[hk])
            P.dma('sp', k.hT[:, :, s * 128:(s + 1) * 128], hst[par][:], reads=[hk], writes=[('hT', s)])


def phase_mod(k, l):
    P = k.P
    I = k.I
    with Ph(k, 'pmod%d' % l) as ph:
        cT = ph.sb('cT', [128, 16], F32)
        load_vec_fm(k, ph, I['cvec'].rearrange("r (c p) -> (r c) p", p=128), 16, cT[:], 'cT')
        scT = ph.sb('scT', [128, 16], F32)
        P.op('act', lambda e: e.activation(out=scT[:], in_=cT[:], func=AF.Silu), reads=['cT'], writes=['scT'])
        modb = ph.sb('modb', [128, 48], F32)
        load_vec_fm(k, ph, I['mod_b'][l].rearrange("(c p) -> c p", p=128), 48, modb[:], 'modb')
        g1 = ph.sb('g1n', [128, 8], F32)
        g2 = ph.sb('g2n', [128, 8], F32)
        load_vec_fm(k, ph, I['norm1_g'][l].rearrange("(c p) -> c p", p=128), 8, g1[:], 'g1n')
        load_vec_fm(k, ph, I['norm2_g'][l].rearrange("(c p) -> c p", p=128), 8, g2[:], 'g2n')
        wm = [ph.sb('wm%d' % i, [128, 8, 1024], F32) for i in range(2)]
        psb = k.ps[0]
        import os
        LVL = int(os.environ.get('MODLVL', '9'))
        if LVL < 1:
            return
        for piece in range(6):
            par = piece % 2
            wk = 'wm%d' % par
            for kc in range(8):
                P.dma('sp', wm[par][:, kc, :], I['mod_w'][l, kc * 128:(kc + 1) * 128, piece * 1024:(piece + 1) * 1024], writes=[wk])
            if LVL < 2:
                continue
            for oc in range(8):
                col = (piece * 8 + oc) * 2
                for kc in range(8):
                    P.op('pe', lambda e, par=par, oc=oc, kc=kc, col=col: e.matmul(
                        psb[:, col:col + 2], lhsT=wm[par][:, kc, oc * 128:(oc + 1) * 128],
                        rhs=scT[:, :].rearrange("p (r c) -> p c r", c=8)[:, kc, :], start=(kc == 0), stop=(kc == 7)),
                        reads=[wk, 'scT'], writes=['ps0'])
        if LVL < 3:
            return
        for r in range(2):
            P.op('dve', lambda e, r=r: e.tensor_tensor(
                out=k.mods[:, r, :, :].rearrange("p a b -> p (a b)"),
                in0=psb[:, 0:96].rearrange("p (o r) -> p r o", r=2)[:, r, :], in1=modb[:], op=ALU.add),
                reads=['ps0', 'modb'], writes=['mods'])
            if LVL < 4:
                continue
            P.op('dve', lambda e, r=r: e.scalar_tensor_tensor(out=k.A1[:, r, :], in0=k.mods[:, r, 1, :], scalar=1.0, in1=g1[:],
                                                              op0=ALU.add, op1=ALU.mult), reads=['mods', 'g1n'], writes=['A1'])
            P.op('dve', lambda e, r=r: e.scalar_tensor_tensor(out=k.A2[:, r, :], in0=k.mods[:, r, 4, :], scalar=1.0, in1=g2[:],
                                                              op0=ALU.add, op1=ALU.mult), reads=['mods', 'g2n'], writes=['A2'])


def phase_a(k, l):
    P = k.P
    I = k.I
    W = I['w_in'][l]
    with Ph(k, 'pa%d' % l) as ph:
        wq = ph.sb('wq', [128, 8, 512], BF16)
        wk_ = ph.sb('wk', [128, 8, 128], BF16)
        wv = ph.sb('wv', [128, 8, 128], BF16)
        wcols = ph.sb('wcols', [128, 8, 2048], BF16)
        Wr = W.rearrange("(kc p) n -> p kc n", p=128)
        for c in range(4):
            wload(k, wq[:, :, c * 128:c * 128 + 64], Wr[:, :, c * 64:(c + 1) * 64], 'wq')
            wload(k, wq[:, :, c * 128 + 64:(c + 1) * 128], Wr[:, :, (c + 4) * 64:(c + 5) * 64], 'wq')
        wload(k, wk_[:], Wr[:, :, 512:640], 'wk')
        wload(k, wv[:], Wr[:, :, 640:768], 'wv')
        for kc in range(8):
            wload(k, wcols[:, kc, :], W[kc * 128:(kc + 1) * 128, 768:2816], 'wcols')
        hT_t = [ph.sb('hT%d' % i, [128, 8, 512], F32) for i in range(2)]
        sq = ph.sb('sq', [128, 8, 512], BF16)
        tmpn = ph.sb('tmpn', [128, 512], F32)
        rstd = ph.sb('rstd', [128, 512], F32)
        hn = ph.sb('hn', [128, 8, 512], F32)
        aT_t = [ph.sb('aT%d' % i, [128, 8, 512], BF16) for i in range(2)]
        cs = [ph.sb('cs%d' % i, [128, 2, 512], F32) for i in range(2)]
        qb = [ph.sb('qb%d' % i, [128, 512], BF16) for i in range(2)]
        t1 = [ph.sb('t1%d' % i, [128, 512], F32) for i in range(2)]
        t2 = [ph.sb('t2%d' % i, [128, 512], F32) for i in range(2)]
        qo = [ph.sb('qo%d' % i, [128, 5, 512], BF16) for i in range(2)]
        vo = [ph.sb('vo%d' % i, [128, 4, 128], BF16) for i in range(2)]
        sg = [ph.sb('sg%d' % i, [128, 512], F32) for i in range(2)]
        oc4 = [ph.sb('oc4%d' % i, [128, 4, 512], BF16) for i in range(3)]
        st = {'cnt': 0}

        def load(ti):
            t0, n = TILES[ti]
            par = ti % 2
            P.dma('sp', hT_t[par][:, :, 0:n], k.hT[:, :, t0:t0 + n], reads=[('hT', b) for b in range(t0 // 128, (t0 + n) // 128)], writes=['hT%d' % par])
            if ti > 0:
                P.dma('sp', cs[par][:, 0, 0:n], I['cos2'][:, t0 - C:t0 - C + n], writes=['cs%d' % par])
                P.dma('sp', cs[par][:, 1, 0:n], I['sin2'][:, t0 - C:t0 - C + n], writes=['cs%d' % par])

        def part1(ti):
            t0, n = TILES[ti]
            par = ti % 2
            norm_rstd(k, ph, hT_t[par], 'hT%d' % par, n, sq, 'sq', rstd, 'rstd', tmpn, 'tmpn')

        def part2(ti):
            t0, n = TILES[ti]
            par = ti % 2
            r = 1 if ti == 0 else 0
            hk, ak = 'hT%d' % par, 'aT%d' % par
            P.op('dve', lambda e: e.tensor_tensor(out=hn[:, :, 0:n], in0=hT_t[par][:, :, 0:n],
                                                   in1=rstd[:, 0:n].unsqueeze(1).to_broadcast([128, 8, n]), op=ALU.mult),
                 reads=[hk, 'rstd'], writes=['hn'])
            for c in range(8):
                P.op('act', lambda e, c=c: e.activation(out=aT_t[par][:, c, 0:n], in_=hn[:, c, 0:n], func=AF.Identity,
                                                        scale=k.A1[:, r, c:c + 1], bias=k.mods[:, r, 0, c:c + 1]),
                     reads=['hn', 'A1', 'mods'], writes=[ak])
            P.dma('sp', k.aT[:, :, t0:t0 + n], aT_t[par][:, :, 0:n], reads=[ak], writes=[('aT', ti)])

        def proj(wsl, wkey, par, n, ak):
            b = st['cnt'] % 4
            st['cnt'] += 1
            pk = 'ps%d' % b
            for kc in range(8):
                P.op('pe', lambda e, kc=kc: e.matmul(k.ps[b][:, 0:n], lhsT=wsl(kc), rhs=aT_t[par][:, kc, 0:n], start=(kc == 0), stop=(kc == 7)),
                     reads=[wkey, ak], writes=[pk])
            return b, pk

        def rope_tail(c, b, pk, p2, par, n):
            b2 = 4 + p2
            pk2 = 'ps%d' % b2
            P.op('pe', lambda e: e.matmul(k.ps[b2][:, 0:n], lhsT=k.rotB[:], rhs=qb[p2][:, 0:n], start=True, stop=True),
                 reads=['rotB', 'qb%d' % p2], writes=[pk2])
            P.op('dve', lambda e: e.tensor_tensor(out=t1[p2][:, 0:n], in0=k.ps[b][:, 0:n], in1=cs[par][:, 0, 0:n], op=ALU.mult),
                 reads=[pk, 'cs%d' % par, 'qb%d' % p2], writes=['t1%d' % p2])
            P.op('dve', lambda e: e.tensor_tensor(out=t2[p2][:, 0:n], in0=k.ps[b2][:, 0:n], in1=cs[par][:, 1, 0:n], op=ALU.mult),
                 reads=[pk2, 'cs%d' % par], writes=['t2%d' % p2])
            P.op('pool', lambda e: e.tensor_tensor(out=qo[par][:, c, 0:n], in0=t1[p2][:, 0:n], in1=t2[p2][:, 0:n], op=ALU.add),
                 reads=['t1%d' % p2, 't2%d' % p2], writes=['qo%d' % par])

        def group(ti, grp):
            t0, n = TILES[ti]
            par = ti % 2
            ak = 'aT%d' % par
            ob = oc4[grp]
            okey = 'oc4%d' % grp
            for c in range(4):
                if grp == 0:
                    b, pk = proj(lambda kc, c=c: wcols[:, kc, c * 128:(c + 1) * 128], 'wcols', par, n, ak)
                    b2 = 4 + (st['cnt'] % 2)
                    pk2 = 'ps%d' % b2
                    s2 = st['cnt'] % 2
                    for kc in range(8):
                        P.op('pe', lambda e, kc=kc, c=c, b2=b2: e.matmul(k.ps[b2][:, 0:n], lhsT=wcols[:, kc, 512 + c * 128:512 + (c + 1) * 128],
                                                                        rhs=aT_t[par][:, kc, 0:n], start=(kc == 0), stop=(kc == 7)),
                             reads=['wcols', ak], writes=[pk2])
                    P.op('act', lambda e, b2=b2, s2=s2: e.activation(out=sg[s2][:, 0:n], in_=k.ps[b2][:, 0:n], func=AF.Sigmoid),
                         reads=[pk2], writes=['sg%d' % s2])
                    P.op('dve', lambda e, b=b, s2=s2, c=c: e.tensor_tensor(out=ob[:, c, 0:n], in0=k.ps[b][:, 0:n], in1=sg[s2][:, 0:n], op=ALU.mult),
                         reads=[pk, 'sg%d' % s2], writes=[okey])
                else:
                    off = 1024 if grp == 1 else 1536
                    b, pk = proj(lambda kc, c=c, off=off: wcols[:, kc, off + c * 128:off + (c + 1) * 128], 'wcols', par, n, ak)
                    fn = AF.Copy if grp == 1 else AF.Gelu_apprx_tanh
                    P.op('act', lambda e, b=b, c=c, fn=fn: e.activation(out=ob[:, c, 0:n], in_=k.ps[b][:, 0:n], func=fn),
                         reads=[pk], writes=[okey])
            dst = (k.hcT, k.lxT, k.GT)[grp]
            P.dma('sp', dst[:, :, t0:t0 + n], ob[:, :, 0:n], reads=[okey], writes=[('hcT', 'lxT', 'GT')[grp]])

        load(0)
        part1(0)
        part2(0)
        for ti, (t0, n) in enumerate(TILES):
            par = ti % 2
            r = 1 if ti == 0 else 0
            ak = 'aT%d' % par
            nxt = ti + 1 if ti + 1 < len(TILES) else None
            if nxt is not None:
                load(nxt)
            pend = None
            for c in range(5):
                wsl = (lambda kc, c=c: wq[:, kc, c * 128:(c + 1) * 128]) if c < 4 else (lambda kc: wk_[:, kc, :])
                b, pk = proj(wsl, 'wq' if c < 4 else 'wk', par, n, ak)
                if r == 1:
                    P.op('act', lambda e, b=b, c=c, par=par, n=n: e.activation(out=qo[par][:, c, 0:n], in_=k.ps[b][:, 0:n], func=AF.Copy),
                         reads=[pk], writes=['qo%d' % par])
                else:
                    p2 = c % 2
                    P.op('act', lambda e, b=b, p2=p2, n=n: e.activation(out=qb[p2][:, 0:n], in_=k.ps[b][:, 0:n], func=AF.Copy),
                         reads=[pk], writes=['qb%d' % p2])
                    if pend is not None:
                        rope_tail(*pend)
                    pend = (c, b, pk, p2, par, n)
            if pend is not None:
                rope_tail(*pend)
            P.dma('sp', k.qT[:, :, t0:t0 + n], qo[par][:, 0:4, 0:n], reads=['qo%d' % par], writes=['qT'])
            P.dma('sp', k.kT[:, t0:t0 + n], qo[par][:, 4, 0:n], reads=['qo%d' % par], writes=['kT'])
            for sblk in range(n // 128):
                b = st['cnt'] % 4
                st['cnt'] += 1
                pk = 'ps%d' % b
                for kc in range(8):
                    P.op('pe', lambda e, b=b, kc=kc, sblk=sblk, par=par: e.matmul(k.ps[b][:, 0:128], lhsT=aT_t[par][:, kc, sblk * 128:(sblk + 1) * 128],
                                                                        rhs=wv[:, kc, :], start=(kc == 0), stop=(kc == 7)),
                         reads=['wv', ak], writes=[pk])
                P.op('dve', lambda e, b=b, sblk=sblk, par=par: e.tensor_copy(out=vo[par][:, sblk, :], in_=k.ps[b][:, 0:128]),
                     reads=[pk], writes=['vo%d' % par])
            P.dma('sp', k.V[t0:t0 + n, :].rearrange("(s p) d -> p s d", p=128), vo[par][:, 0:n // 128, :], reads=['vo%d' % par], writes=['V'])
            group(ti, 0)
            if nxt is not None:
                part1(nxt)
            group(ti, 1)
            if nxt is not None:
                part2(nxt)
            group(ti, 2)


def phase_lru(k, l):
    P = k.P
    I = k.I
    with Ph(k, 'plru%d' % l) as ph:
        cw = ph.sb('cw', [128, 16], F32)
        cb = ph.sb('cb', [128, 4], F32)
        ba = ph.sb('ba', [128, 8], F32)
        bx = ph.sb('bx', [128, 8], F32)
        lam = ph.sb('lam', [128, 8], F32)
        c1 = ph.sb('c1', [128, 8], F32)
        load_vec_fm(k, ph, I['lru_conv_w'][l].rearrange("t (c p) -> (t c) p", p=128), 16, cw[:], 'cw')
        load_vec_fm(k, ph, I['lru_conv_b'][l].rearrange("(c p) -> c p", p=128), 4, cb[:], 'cb')
        load_vec_fm(k, ph, I['lru_ba'][l].rearrange("d (c p) -> (d c) p", p=128), 8, ba[:], 'ba')
        load_vec_fm(k, ph, I['lru_bx'][l].rearrange("d (c p) -> (d c) p", p=128), 8, bx[:], 'bx')
        load_vec_fm(k, ph, I['lru_lam'][l].rearrange("d (c p) -> (d c) p", p=128), 8, lam[:], 'lam')
        P.op('act', lambda e: e.activation(out=c1[:], in_=lam[:], func=AF.Exp, scale=-1.0), reads=['lam'], writes=['c1'])
        P.op('dve', lambda e: e.tensor_scalar(out=c1[:], in0=c1[:], scalar1=1.0, scalar2=None, op0=ALU.add), reads=['c1'], writes=['c1'])
        P.op('act', lambda e: e.activation(out=c1[:], in_=c1[:], func=AF.Ln), reads=['c1'], writes=['c1'])
        P.op('dve', lambda e: e.tensor_scalar(out=c1[:], in0=c1[:], scalar1=-4.0, scalar2=None, op0=ALU.mult), reads=['c1'], writes=['c1'])
        for t_, nm in ((cb, 'cb'), (ba, 'ba'), (bx, 'bx')):
            P.op('dve', lambda e, t_=t_: e.tensor_scalar(out=t_[:], in0=t_[:], scalar1=0.5, scalar2=None, op0=ALU.mult), reads=[nm], writes=[nm])
        bd = ph.sb('bd', [128, 16, 128], BF16)
        P.op('pool', lambda e: e.memset(bd[:], 0.0), writes=['bd'])
        for d in range(2):
            for gi, wn in enumerate(('lru_wa', 'lru_wx')):
                for ch in range(4):
                    idx = (d * 2 + gi) * 4 + ch
                    for hb in range(2):
                        wload(k, bd[hb * 64:(hb + 1) * 64, idx, hb * 64:(hb + 1) * 64], I[wn][l, d, ch * 2 + hb], 'bd')
        dgl = ph.sb('dgl', [128, 16, 128], BF16)
        P.op('dve', lambda e: e.tensor_tensor(out=dgl[:, :, :], in0=k.identF[:, :].unsqueeze(1).to_broadcast([128, 16, 128]),
                                              in1=cw[:, :].unsqueeze(2).to_broadcast([128, 16, 128]), op=ALU.mult),
             reads=['identF', 'cw'], writes=['dgl'])
        xp = ph.sb('xp0', [128, NT + 8], BF16)
        Uh_ = ph.sb('Uh0', [128, NT], BF16)
        Uhs = [Uh_, Uh_]
        Hf = ph.sb('Hf', [128, NT], F32)
        NA, NR, NI, NB = 4, 4, 3, 2
        As = [ph.sb('As%d' % i, [128, 2048], F32) for i in range(NA)]
        tRs = [ph.sb('tR%d' % i, [128, 2048], F32) for i in range(NR)]
        tIs = [ph.sb('tI%d' % i, [128, 2048], F32) for i in range(NI)]
        Bs = [ph.sb('Bs%d' % i, [128, 2048], F32) for i in range(NB)]
        gt = [ph.sb('gt%d' % i, [128, 2048], BF16) for i in range(2)]
        lyo = ph.sb('lyo', [128, 2048], BF16)
        hc = ph.sb('hc', [128, 2], F32)
        CO, LO = 2, 261
        xk = 'xp0'
        st = {'cnt': 0, 'c2': 0}
        spans = [(0, C)] + [(C + 2048 * h, 2048) for h in range(4)]
        items = []
        for ch in range(4):
            for d in range(2):
                order = spans if d == 0 else [spans[0]] + spans[:0:-1]
                for si, (s0, sn) in enumerate(order):
                    items.append((ch, d, si, s0, sn))

        def conv4(ch):
            Uh = Uhs[ch % 2]
            uk = 'Uh0'
            P.op('pool', lambda e: e.memset(xp[:, 0:2], 0.0), writes=[xk])
            P.op('pool', lambda e: e.memset(xp[:, 258:261], 0.0), writes=[xk])
            P.op('pool', lambda e: e.memset(xp[:, LO + S:LO + S + 3], 0.0), writes=[xk])
            P.dma('sp', xp[:, CO:CO + C], k.lxT[:, ch, 0:C], reads=['lxT'], writes=[xk])
            P.dma('sp', xp[:, LO:LO + S], k.lxT[:, ch, C:NT], reads=['lxT'], writes=[xk])
            for (o, u0, n_) in ((CO, 0, C), (LO, C, S)):
                for t0 in range(0, n_, 512):
                    m = min(512, n_ - t0)
                    b = 6 + (st['c2'] % 2)
                    st['c2'] += 1
                    pk = 'ps%d' % b
                    for tap in range(4):
                        P.op('pe', lambda e, b=b, tap=tap, o=o, t0=t0, m=m: e.matmul(k.ps[b][:, 0:m], lhsT=dgl[:, tap * 4 + ch, :],
                                                                                    rhs=xp[:, o + t0 + tap - 2:o + t0 + tap - 2 + m], start=(tap == 0), stop=(tap == 3)),
                             reads=['dgl', xk], writes=[pk])
                    P.op('act', lambda e, b=b, u0=u0, t0=t0, m=m: e.activation(out=Uh[:, u0 + t0:u0 + t0 + m], in_=k.ps[b][:, 0:m], func=AF.Identity,
                                                                              scale=0.5, bias=cb[:, ch:ch + 1]), reads=[pk, 'cb'], writes=[uk])

        def s1(i):
            ch, d, si, s0, sn = items[i]
            Uh, uk = Uhs[ch % 2], 'Uh0'
            tR, tI, A = tRs[i % NR], tIs[i % NI], As[i % NA]
            kR, kI, kA = 'tR%d' % (i % NR), 'tI%d' % (i % NI), 'As%d' % (i % NA)
            for gi, (dstG, gk, bias) in enumerate(((tR, kR, ba), (tI, kI, bx))):
                idx = (d * 2 + gi) * 4 + ch
                for sub in range(0, sn, 512):
                    m = min(512, sn - sub)
                    b = st['cnt'] % 6
                    st['cnt'] += 1
                    pk = 'ps%d' % b
                    P.op('pe', lambda e, b=b, idx=idx, sub=sub, m=m: e.matmul(k.ps[b][:, 0:m], lhsT=bd[:, idx, :], rhs=Uh[:, s0 + sub:s0 + sub + m], start=True, stop=True),
                         reads=['bd', uk], writes=[pk])
                    P.op('act', lambda e, b=b, dstG=dstG, sub=sub, m=m, bias=bias: e.activation(out=dstG[:, sub:sub + m], in_=k.ps[b][:, 0:m], func=AF.Tanh,
                                                                                               bias=bias[:, d * 4 + ch:d * 4 + ch + 1]),
                         reads=[pk, 'ba', 'bx'], writes=[gk])
            P.op('act', lambda e: e.activation(out=A[:, 0:sn], in_=tR[:, 0:sn], func=AF.Exp, scale=c1[:, d * 4 + ch:d * 4 + ch + 1],
                                               bias=c1[:, d * 4 + ch:d * 4 + ch + 1]), reads=[kR, 'c1'], writes=[kA])

        def s2a(i):
            ch, d, si, s0, sn = items[i]
            tR, A = tRs[i % NR], As[i % NA]
            P.op('dve', lambda e: e.tensor_tensor(out=tR[:, 0:sn], in0=A[:, 0:sn], in1=A[:, 0:sn], op=ALU.mult), reads=['As%d' % (i % NA)], writes=['tR%d' % (i % NR)])

        def s2b(i):
            ch, d, si, s0, sn = items[i]
            tR = tRs[i % NR]
            kR = 'tR%d' % (i % NR)
            P.op('act', lambda e: e.activation(out=tR[:, 0:sn], in_=tR[:, 0:sn], func=AF.Sqrt, scale=-1.0, bias=k.oneT[:, 0:1]), reads=[kR, 'oneT'], writes=[kR])

        def s3(i):
            ch, d, si, s0, sn = items[i]
            Uh, uk = Uhs[ch % 2], 'Uh0'
            tR, tI, B = tRs[i % NR], tIs[i % NI], Bs[i % NB]
            kR, kI, kB = 'tR%d' % (i % NR), 'tI%d' % (i % NI), 'Bs%d' % (i % NB)
            P.op('dve', lambda e: e.scalar_tensor_tensor(out=tR[:, 0:sn], in0=tI[:, 0:sn], scalar=1.0, in1=tR[:, 0:sn], op0=ALU.add, op1=ALU.mult),
                 reads=[kR, kI], writes=[kR])
            P.op('pool', lambda e: e.tensor_tensor(out=B[:, 0:sn], in0=tR[:, 0:sn], in1=Uh[:, s0:s0 + sn], op=ALU.mult), reads=[kR, uk], writes=[kB])

        def s4(i):
            ch, d, si, s0, sn = items[i]
            A, B = As[i % NA], Bs[i % NB]
            kA, kB = 'As%d' % (i % NA), 'Bs%d' % (i % NB)
            if d == 0:
                init = 0.0 if si == 0 else Hf[:, s0 - 1:s0]
                P.op('dve', lambda e: e.tensor_tensor_scan(out=Hf[:, s0:s0 + sn], data0=A[:, 0:sn], data1=B[:, 0:sn], initial=init,
                                                           op0=ALU.mult, op1=ALU.add), reads=[kA, kB, 'Hf'], writes=['Hf'])
            else:
                init = 0.0 if si == 0 else hc[:, (si - 1) % 2:(si - 1) % 2 + 1]
                P.op('dve', lambda e: e.tensor_tensor_scan(out=B[:, 0:sn][:, ::-1], data0=A[:, 0:sn][:, ::-1], data1=B[:, 0:sn][:, ::-1],
                                                           initial=init, op0=ALU.mult, op1=ALU.add), reads=[kA, kB, 'hc'], writes=[kB])
                P.op('dve', lambda e: e.tensor_copy(out=hc[:, si % 2:si % 2 + 1], in_=B[:, 0:1]), reads=[kB], writes=['hc'])
                par = i % 2
                P.dma('sp', gt[par][:, 0:sn], k.GT[:, ch, s0:s0 + sn], reads=['GT'], writes=['gt%d' % par])
                P.op('pool', lambda e: e.tensor_tensor(out=B[:, 0:sn], in0=Hf[:, s0:s0 + sn], in1=B[:, 0:sn], op=ALU.add), reads=['Hf', kB], writes=[kB])
                P.op('dve', lambda e: e.tensor_tensor(out=lyo[:, 0:sn], in0=B[:, 0:sn], in1=gt[par][:, 0:sn], op=ALU.mult),
                     reads=[kB, 'gt%d' % par], writes=['lyo'])
                P.dma('sp', k.ylT[:, ch, s0:s0 + sn], lyo[:, 0:sn], reads=['lyo'], writes=['ylT'])

        for ch in range(4):
            conv4(ch)
            lo, hi = ch * 10, ch * 10 + 10
            for t in range(lo, hi + 3):
                if t < hi:
                    s1(t)
                if lo <= t - 1 < hi:
                    s2b(t - 1)
                if lo <= t - 2 < hi:
                    s3(t - 2)
                if lo <= t - 3 < hi:
                    s4(t - 3)
                if t < hi:
                    s2a(t)


def phase_conv(k, l):
    P = k.P
    I = k.I
    need_ctx = l < L - 1
    with Ph(k, 'pcv%d' % l) as ph:
        dw = ph.sb('dw', [128, 124], F32)
        vb = ph.sb('vb', [128, 12], F32)
        load_vec_fm(k, ph, I['conv_dw_w'][l].rearrange("t (c p) -> (t c) p", p=128), 124, dw[:], 'dw')
        load_vec_fm(k, ph, I['conv_dw_b'][l].rearrange("(c p) -> c p", p=128), 4, vb[:, 0:4], 'vb0')
        load_vec_fm(k, ph, I['conv_ln_g'][l].rearrange("(c p) -> c p", p=128), 4, vb[:, 4:8], 'vb1')
        load_vec_fm(k, ph, I['conv_ln_b'][l].rearrange("(c p) -> c p", p=128), 4, vb[:, 8:12], 'vb2')
        vkeys = ['vb0', 'vb1', 'vb2']
        dg = ph.sb('dg', [128, 124, 128], BF16)
        for j0 in range(0, 124, 31):
            P.op('dve', lambda e, j0=j0: e.tensor_tensor(out=dg[:, j0:j0 + 31, :], in0=k.identF[:, :].unsqueeze(1).to_broadcast([128, 31, 128]),
                                                         in1=dw[:, j0:j0 + 31].unsqueeze(2).to_broadcast([128, 31, 128]), op=ALU.mult),
                 reads=['identF', 'dw'], writes=['dg'])
        hin = [ph.sb('hin%d' % i, [128, 4, 542], BF16) for i in range(2)]
        cxs = [ph.sb('cx%d' % i, [128, 4, 512], F32) for i in range(2)]
        xbs = [ph.sb('xb%d' % i, [128, 4, 512], BF16) for i in range(2)]
        xss = [ph.sb('xs%d' % i, [128, 4, 512], BF16) for i in range(2)]
        mean = ph.sb('mean', [128, 512], F32)
        var = ph.sb('var', [128, 512], F32)
        rs = ph.sb('rs', [128, 512], F32)
        yc = [ph.sb('yc%d' % i, [128, 4, 512], BF16) for i in range(2)]
        tiles = [ti for ti in range(len(TILES)) if not (ti == 0 and not need_ctx)]

        def load(ti):
            t0, n = TILES[ti]
            par = ti % 2
            hk = 'hin%d' % par
            lo_seq, hi_seq = (0, C) if ti == 0 else (C, NT)
            a0, a1 = max(t0 - 15, lo_seq), min(t0 + n + 15, hi_seq)
            P.op('pool', lambda e: e.memset(hin[par][:], 0.0), writes=[hk])
            P.dma('sp', hin[par][:, :, a0 - (t0 - 15):a1 - (t0 - 15)], k.hcT[:, :, a0:a1], reads=['hcT'], writes=[hk])

        def conv_mm(ti, chs):
            t0, n = TILES[ti]
            par = ti % 2
            hk = 'hin%d' % par
            cx, xb, xs = cxs[par], xbs[par], xss[par]
            ck, bk, sk = 'cx%d' % par, 'xb%d' % par, 'xs%d' % par
            for ch in chs:
                pk = 'ps%d' % ch
                for tap in range(31):
                    P.op('pe', lambda e, ch=ch, tap=tap: e.matmul(k.ps[ch][:, 0:n], lhsT=dg[:, tap * 4 + ch, :], rhs=hin[par][:, ch, tap:tap + n],
                                                                 start=(tap == 0), stop=(tap == 30)), reads=['dg', hk], writes=[pk])
                P.op('act', lambda e, ch=ch: e.activation(out=cx[:, ch, 0:n], in_=k.ps[ch][:, 0:n], func=AF.Identity, bias=vb[:, ch:ch + 1]),
                     reads=[pk] + vkeys, writes=[ck])
                P.op('act', lambda e, ch=ch: e.activation(out=xb[:, ch, 0:n], in_=k.ps[ch][:, 0:n], func=AF.Identity, bias=vb[:, ch:ch + 1]),
                     reads=[pk] + vkeys, writes=[bk])
                P.op('act', lambda e, ch=ch: e.activation(out=xs[:, ch, 0:n], in_=k.ps[ch][:, 0:n], func=AF.Square, bias=vb[:, ch:ch + 1]),
                     reads=[pk] + vkeys, writes=[sk])

        def stats_mm(ti):
            t0, n = TILES[ti]
            par = ti % 2
            xb, xs = xbs[par], xss[par]
            bk, sk = 'xb%d' % par, 'xs%d' % par
            for ch in range(4):
                P.op('pe', lambda e, ch=ch: e.matmul(k.ps[4][:, 0:n], lhsT=k.onesB[:], rhs=xb[:, ch, 0:n], start=(ch == 0), stop=(ch == 3)),
                     reads=['onesB', bk], writes=['ps4'])
            for ch in range(4):
                P.op('pe', lambda e, ch=ch: e.matmul(k.ps[5][:, 0:n], lhsT=k.onesB[:], rhs=xs[:, ch, 0:n], start=(ch == 0), stop=(ch == 3)),
                     reads=['onesB', sk], writes=['ps5'])

        def ln_tail(ti):
            t0, n = TILES[ti]
            par = ti % 2
            cx = cxs[par]
            ck = 'cx%d' % par
            P.op('dve', lambda e: e.tensor_scalar(out=mean[:, 0:n], in0=k.ps[4][:, 0:n], scalar1=1.0 / 512, scalar2=None, op0=ALU.mult), reads=['ps4'], writes=['mean'])
            P.op('dve', lambda e: e.tensor_tensor(out=var[:, 0:n], in0=mean[:, 0:n], in1=mean[:, 0:n], op=ALU.mult), reads=['mean'], writes=['var'])
            P.op('dve', lambda e: e.scalar_tensor_tensor(out=var[:, 0:n], in0=k.ps[5][:, 0:n], scalar=1.0 / 512, in1=var[:, 0:n], op0=ALU.mult, op1=ALU.subtract),
                 reads=['ps5', 'var'], writes=['var'])
            P.op('dve', lambda e: e.tensor_scalar(out=var[:, 0:n], in0=var[:, 0:n], scalar1=0.0, scalar2=None, op0=ALU.max), reads=['var'], writes=['var'])
            P.op('act', lambda e: e.activation(out=var[:, 0:n], in_=var[:, 0:n], func=AF.Sqrt, bias=k.epsT[:, 0:1]), reads=['var', 'epsT'], writes=['var'])
            P.op('dve', lambda e: e.reciprocal(out=rs[:, 0:n], in_=var[:, 0:n]), reads=['var'], writes=['rs'])
            P.op('dve', lambda e: e.tensor_tensor(out=cx[:, :, 0:n], in0=cx[:, :, 0:n], in1=mean[:, 0:n].unsqueeze(1).to_broadcast([128, 4, n]), op=ALU.subtract),
                 reads=[ck, 'mean'], writes=[ck])
            P.op('dve', lambda e: e.tensor_tensor(out=cx[:, :, 0:n], in0=cx[:, :, 0:n], in1=rs[:, 0:n].unsqueeze(1).to_broadcast([128, 4, n]), op=ALU.mult),
                 reads=[ck, 'rs'], writes=[ck])
            for ch in range(4):
                P.op('act', lambda e, ch=ch: e.activation(out=yc[par][:, ch, 0:n], in_=cx[:, ch, 0:n], func=AF.Silu,
                                                          scale=vb[:, 4 + ch:5 + ch], bias=vb[:, 8 + ch:9 + ch]),
                     reads=[ck] + vkeys, writes=['yc%d' % par])
            P.dma('sp', k.ycT[:, :, t0:t0 + n], yc[par][:, :, 0:n], reads=['yc%d' % par], writes=['ycT'])

        load(tiles[0])
        for idx, ti in enumerate(tiles):
            nxt = tiles[idx + 1] if idx + 1 < len(tiles) else None
            if nxt is not None:
                load(nxt)
            conv_mm(ti, [0])
            if idx > 0:
                stats_mm(tiles[idx - 1])
                ln_tail(tiles[idx - 1])
            conv_mm(ti, [1, 2, 3])
        stats_mm(tiles[-1])
        ln_tail(tiles[-1])


def phase_att(k, l):
    P = k.P
    I = k.I
    need_ctx = l < L - 1
    with Ph(k, 'pat%d' % l) as ph:
        kTs = ph.sb('kTs', [128, NT], BF16)
        Vs = ph.sb('Vs', [128, NBLK, 128], BF16)
        P.dma('sp', kTs[:], k.kT, reads=['kT'], writes=['kTs'])
        for part in range(0, NBLK, 11):
            P.dma('sp', Vs[:, part:part + 11, :], k.V[part * 128:(part + 11) * 128, :].rearrange("(s p) d -> p s d", p=128), reads=['V'], writes=['Vs'])
        snk = ph.sb('snk', [1, 8], F32)
        sx = ph.sb('sx', [128, 4], F32)
        P.dma('sp', snk[:], I['attn_sink'][l:l + 1, :], writes=['snk'])
        onesF = ph.sb('onesF', [1, 128], F32)
        P.op('dve', lambda e: e.memset(onesF[:], 1.0), writes=['onesF'])
        P.op('pe', lambda e: e.matmul(k.ps[7][:, 0:8], lhsT=onesF[:], rhs=snk[:], start=True, stop=True), reads=['onesF', 'snk'], writes=['ps7'])
        P.op('act', lambda e: e.activation(out=sx[0:64, :], in_=k.ps[7][0:64, 0:4], func=AF.Exp), reads=['ps7'], writes=['sx'])
        P.op('act', lambda e: e.activation(out=sx[64:128, :], in_=k.ps[7][64:128, 4:8], func=AF.Exp), reads=['ps7'], writes=['sx'])
        qs = [ph.sb('qs%d' % i, [128, 4, 128], BF16) for i in range(2)]
        pT = [ph.sb('pT%d' % i, [128, 512], BF16) for i in range(8)]
        den = ph.sb('den', [128, 512], F32)
        yo = [ph.sb('yo%d' % i, [128, 4, 128], BF16) for i in range(2)]
        cnt = 0
        pcnt = 0
        qblocks = list(range(2, NBLK)) + ([0, 1] if need_ctx else [])
        def qload(qi):
            P.dma('sp', qs[qi % 2][:], k.qT[:, :, qblocks[qi] * 128:qblocks[qi] * 128 + 128], reads=['qT'], writes=['qs%d' % (qi % 2)])
        qload(0)
        for qi, qb in enumerate(qblocks):
            par = qi % 2
            t0 = qb * 128
            qk = 'qs%d' % par
            if qi + 1 < len(qblocks):
                qload(qi + 1)
            if qb >= 2:
                kbs = []
                if qb > 2:
                    kbs.append((qb - 1, k.nmp, 'nmp'))
                kbs.append((qb, None, None))
                if qb < NBLK - 1:
                    kbs.append((qb + 1, k.nmn, 'nmn'))
                kbs += [(0, None, None), (1, None, None)]
            else:
                kbs = [(0, None, None), (1, None, None)]
            bo = 4 + (qi % 2)
            bd_ = 6 + (qi % 2)
            pko, pkd = 'ps%d' % bo, 'ps%d' % bd_
            for g in range(2):
                pb = g * 64
                pl = []
                for (kb, nm, nmk) in kbs:
                    b = cnt % 4
                    cnt += 1
                    pk = 'ps%d' % b
                    P.op('pe', lambda e, b=b, pb=pb, kb=kb, par=par, nm=nm: e.matmul(k.ps[b][:, :], lhsT=kTs[pb:pb + 64, kb * 128:(kb + 1) * 128],
                                                                                    rhs=qs[par][pb:pb + 64, :, :], start=True, stop=(nm is None)),
                         reads=['kTs', qk], writes=[pk])
                    if nm is not None:
                        P.op('pe', lambda e, b=b, nm=nm: e.matmul(k.ps[b][:, :], lhsT=k.identB[:], rhs=nm[:], start=False, stop=True),
                             reads=['identB', nmk], writes=[pk])
                    pi = pcnt % 8
                    pcnt += 1
                    P.op('act', lambda e, b=b, pi=pi: e.activation(out=pT[pi][:], in_=k.ps[b][:, :], func=AF.Exp, scale=0.125), reads=[pk], writes=['pT%d' % pi])
                    pl.append((kb, pi))
                for j, (kb, pi) in enumerate(pl):
                    P.op('pe', lambda e, bo=bo, pb=pb, kb=kb, pi=pi, j=j, nl=len(pl): e.matmul(k.ps[bo][pb:pb + 64, :], lhsT=Vs[:, kb, pb:pb + 64], rhs=pT[pi][:],
                                                                                              start=(j == 0), stop=(j == nl - 1)),
                         reads=['Vs', 'pT%d' % pi], writes=[pko])
                for j, (kb, pi) in enumerate(pl):
                    P.op('pe', lambda e, bd_=bd_, pb=pb, pi=pi, j=j, nl=len(pl): e.matmul(k.ps[bd_][pb:pb + 64, :], lhsT=k.onesB[:, 0:64], rhs=pT[pi][:],
                                                                                         start=(j == 0), stop=(j == nl - 1)),
                         reads=['onesB', 'pT%d' % pi], writes=[pkd])
            P.op('dve', lambda e, bd_=bd_: e.tensor_tensor(out=den[:, :].rearrange("p (c t) -> p c t", t=128), in0=k.ps[bd_][:, :].rearrange("p (c t) -> p c t", t=128),
                                                           in1=sx[:, :].unsqueeze(2).to_broadcast([128, 4, 128]), op=ALU.add), reads=[pkd, 'sx'], writes=['den'])
            P.op('dve', lambda e: e.reciprocal(out=den[:], in_=den[:]), reads=['den'], writes=['den'])
            P.op('dve', lambda e, bo=bo, par=par: e.tensor_tensor(out=yo[par][:, :, :].rearrange("p c t -> p (c t)"), in0=k.ps[bo][:, :], in1=den[:], op=ALU.mult),
                 reads=[pko, 'den'], writes=['yo%d' % par])
            P.dma('sp', k.yaT[:, :, t0:t0 + 128], yo[par][:], reads=['yo%d' % par], writes=['yaT'])


def phase_merge(k, l):
    P = k.P
    I = k.I
    need_ctx = l < L - 1
    with Ph(k, 'pmg%d' % l) as ph:
        wg = ph.sb('wg', [128, 8, 3072], BF16)
        wo3 = ph.sb('wo3', [128, 3, 4, 1024], BF16)
        wout = ph.sb('wout', [128, 8, 1024], BF16)
        for kc in range(8):
            wload(k, wg[:, kc, :], I['w_in'][l, kc * 128:(kc + 1) * 128, 2816:5888], 'wg')
        for c in range(4):
            wload(k, wo3[0:64, 0, c, :], I['w_o_attn'][l, c * 64:(c + 1) * 64, :], 'wo3')
            wload(k, wo3[64:128, 0, c, :], I['w_o_attn'][l, (c + 4) * 64:(c + 5) * 64, :], 'wo3')
            wload(k, wo3[:, 1, c, :], I['w_o_conv'][l, c * 128:(c + 1) * 128, :], 'wo3')
            wload(k, wo3[:, 2, c, :], I['w_o_lru'][l, c * 128:(c + 1) * 128, :], 'wo3')
        for kc in range(8):
            wload(k, wout[:, kc, :], I['w_out'][l, kc * 128:(kc + 1) * 128, :], 'wout')
        aT_t = [ph.sb('aT%d' % i, [128, 8, 512], BF16) for i in range(2)]
        y3 = [ph.sb('y3%d' % i, [128, 3, 4, 512], BF16) for i in range(2)]
        hT_t = [ph.sb('hT%d' % i, [128, 8, 512], F32) for i in range(2)]
        gs = [ph.sb('gs%d' % i, [128, 3, 512], F32) for i in range(2)]
        m1 = ph.sb('m1', [128, 512], F32)
        m2 = ph.sb('m2', [128, 512], F32)
        mg = ph.sb('mg', [128, 8, 512], BF16)
        ysrc = (k.yaT, k.ycT, k.ylT)
        ykeys = ('yaT', 'ycT', 'ylT')
        cnt = 0
        mtiles = [ti for ti in range(len(TILES)) if not (ti == 0 and not need_ctx)]

        def mload(ti):
            t0, n = TILES[ti]
            par = ti % 2
            P.dma('sp', aT_t[par][:, :, 0:n], k.aT[:, :, t0:t0 + n], reads=[('aT', ti)], writes=['aT%d' % par])
            for br in range(3):
                P.dma('sp', y3[par][:, br, :, 0:n], ysrc[br][:, :, t0:t0 + n], reads=[ykeys[br]], writes=['y3%d' % par])
            P.dma('sp', hT_t[par][:, :, 0:n], k.hT[:, :, t0:t0 + n], reads=[('hT', b) for b in range(t0 // 128, (t0 + n) // 128)], writes=['hT%d' % par])
        mload(mtiles[0])
        for mi, ti in enumerate(mtiles):
            t0, n = TILES[ti]
            par = ti % 2
            r = 1 if ti == 0 else 0
            ak, yk, hk = 'aT%d' % par, 'y3%d' % par, 'hT%d' % par
            if mi + 1 < len(mtiles):
                mload(mtiles[mi + 1])
            for m in range(8):
                gp = cnt % 2
                cnt += 1
                for br in range(3):
                    pk = 'ps%d' % br
                    for kc in range(8):
                        P.op('pe', lambda e, br=br, kc=kc, m=m, par=par, n=n: e.matmul(k.ps[br][:, 0:n], lhsT=wg[:, kc, br * 1024 + m * 128:br * 1024 + (m + 1) * 128],
                                                                                      rhs=aT_t[par][:, kc, 0:n], start=(kc == 0), stop=(kc == 7)),
                             reads=['wg', ak], writes=[pk])
                    P.op('act', lambda e, br=br, gp=gp, n=n: e.activation(out=gs[gp][:, br, 0:n], in_=k.ps[br][:, 0:n], func=AF.Sigmoid), reads=[pk], writes=['gs%d' % gp])
                for br in range(3):
                    pk = 'ps%d' % (3 + br)
                    for kc in range(4):
                        P.op('pe', lambda e, br=br, kc=kc, m=m, par=par, n=n: e.matmul(k.ps[3 + br][:, 0:n], lhsT=wo3[:, br, kc, m * 128:(m + 1) * 128],
                                                                                      rhs=y3[par][:, br, kc, 0:n], start=(kc == 0), stop=(kc == 3)),
                             reads=['wo3', yk], writes=[pk])
                gk = 'gs%d' % gp
                P.op('dve', lambda e, gp=gp, n=n: e.tensor_tensor(out=m1[:, 0:n], in0=k.ps[3][:, 0:n], in1=gs[gp][:, 0, 0:n], op=ALU.mult), reads=['ps3', gk], writes=['m1'])
                P.op('dve', lambda e, gp=gp, n=n: e.tensor_tensor(out=m2[:, 0:n], in0=k.ps[4][:, 0:n], in1=gs[gp][:, 1, 0:n], op=ALU.mult), reads=['ps4', gk], writes=['m2'])
                P.op('pool', lambda e, n=n: e.tensor_tensor(out=m1[:, 0:n], in0=m1[:, 0:n], in1=m2[:, 0:n], op=ALU.add), reads=['m1', 'm2'], writes=['m1'])
                P.op('dve', lambda e, gp=gp, n=n: e.tensor_tensor(out=m2[:, 0:n], in0=k.ps[5][:, 0:n], in1=gs[gp][:, 2, 0:n], op=ALU.mult), reads=['ps5', gk], writes=['m2'])
                P.op('pool', lambda e, n=n, m=m: e.tensor_tensor(out=mg[:, m, 0:n], in0=m1[:, 0:n], in1=m2[:, 0:n], op=ALU.add), reads=['m1', 'm2'], writes=['mg'])
            for m in range(8):
                b = 6 + (m % 2)
                pk = 'ps%d' % b
                for kc in range(8):
                    P.op('pe', lambda e, b=b, kc=kc, m=m, n=n: e.matmul(k.ps[b][:, 0:n], lhsT=wout[:, kc, m * 128:(m + 1) * 128], rhs=mg[:, kc, 0:n],
                                                                       start=(kc == 0), stop=(kc == 7)), reads=['wout', 'mg'], writes=[pk])
                P.op('dve', lambda e, b=b, m=m, n=n, par=par, r=r: e.scalar_tensor_tensor(out=hT_t[par][:, m, 0:n], in0=k.ps[b][:, 0:n], scalar=k.mods[:, r, 2, m:m + 1],
                                                                                         in1=hT_t[par][:, m, 0:n], op0=ALU.mult, op1=ALU.add),
                     reads=[pk, 'mods', hk], writes=[hk])
            P.dma('sp', k.hT[:, :, t0:t0 + n], hT_t[par][:, :, 0:n], reads=[hk], writes=[('hT', b) for b in range(t0 // 128, (t0 + n) // 128)])


def phase_ffn(k, l):
    P = k.P
    I = k.I
    need_ctx = l < L - 1
    FT = 256
    n = FT
    with Ph(k, 'pff%d' % l) as ph:
        wup = ph.sb('wup', [128, 8, 2 * FH], BF16)
        wdn = ph.sb('wdn', [128, 22, 1024], BF16)
        for piece in (0, 2, 1, 3):
            for kc in range(8):
                wload(k, wup[:, kc, piece * 1408:(piece + 1) * 1408], I['ffn_w_up'][l, kc * 128:(kc + 1) * 128, piece * 1408:(piece + 1) * 1408], 'wup%d' % piece)
        for kc in range(22):
            wload(k, wdn[:, kc, :], I['ffn_w_down'][l, kc * 128:(kc + 1) * 128, :], 'wdn')
        hT_t = [ph.sb('hT%d' % i, [128, 8, FT], F32) for i in range(2)]
        sq = ph.sb('sq', [128, 8, FT], BF16)
        tmpn = ph.sb('tmpn', [128, FT], F32)
        rstd = ph.sb('rstd', [128, FT], F32)
        hn = ph.sb('hn', [128, 8, FT], F32)
        a2s = [ph.sb('a2%d' % i, [128, 8, FT], BF16) for i in range(2)]
        sg = [ph.sb('sg%d' % i, [128, FT], F32) for i in range(2)]
        hid = ph.sb('hid', [128, 22, FT], BF16)
        tiles = [ti for ti in range(NT // FT) if not (ti == 0 and not need_ctx)]

        def hkeys(ti):
            return [('hT', b) for b in range(ti * FT // 128, (ti * FT + n) // 128)]

        def load(ti):
            P.dma('sp', hT_t[ti % 2][:, :, 0:n], k.hT[:, :, ti * FT:ti * FT + n], reads=hkeys(ti), writes=['hT%d' % (ti % 2)])

        def part1(ti):
            norm_rstd(k, ph, hT_t[ti % 2], 'hT%d' % (ti % 2), n, sq, 'sq', rstd, 'rstd', tmpn, 'tmpn')

        def part2(ti):
            par = ti % 2
            r = 1 if ti == 0 else 0
            P.op('dve', lambda e: e.tensor_tensor(out=hn[:, :, 0:n], in0=hT_t[par][:, :, 0:n],
                                                   in1=rstd[:, 0:n].unsqueeze(1).to_broadcast([128, 8, n]), op=ALU.mult),
                 reads=['hT%d' % par, 'rstd'], writes=['hn'])
            for c in range(8):
                P.op('act', lambda e, c=c: e.activation(out=a2s[par][:, c, 0:n], in_=hn[:, c, 0:n], func=AF.Identity,
                                                        scale=k.A2[:, r, c:c + 1], bias=k.mods[:, r, 3, c:c + 1]),
                     reads=['hn', 'A2', 'mods'], writes=['a2%d' % par])

        def down(ti, m):
            par = ti % 2
            r = 1 if ti == 0 else 0
            hk = 'hT%d' % par
            b = 4 + (m % 3)
            pk = 'ps%d' % b
            for kc in range(22):
                P.op('pe', lambda e, kc=kc: e.matmul(k.ps[b][:, 0:n], lhsT=wdn[:, kc, m * 128:(m + 1) * 128], rhs=hid[:, kc, 0:n],
                                                     start=(kc == 0), stop=(kc == 21)), reads=['wdn', 'hid'], writes=[pk])
            P.op('dve', lambda e: e.scalar_tensor_tensor(out=hT_t[par][:, m, 0:n], in0=k.ps[b][:, 0:n], scalar=k.mods[:, r, 5, m:m + 1],
                                                         in1=hT_t[par][:, m, 0:n], op0=ALU.mult, op1=ALU.add),
                 reads=[pk, 'mods', hk], writes=[hk])

        load(tiles[0])
        part1(tiles[0])
        part2(tiles[0])
        for idx, ti in enumerate(tiles):
            par = ti % 2
            a2 = a2s[par]
            ak = 'a2%d' % par
            nxt = tiles[idx + 1] if idx + 1 < len(tiles) else None
            if nxt is not None:
                load(nxt)
            for j in range(22):
                bu, bg = 2 * (j % 2), 2 * (j % 2) + 1
                pku, pkg = 'ps%d' % bu, 'ps%d' % bg
                s2 = j % 2
                for kc in range(8):
                    P.op('pe', lambda e, bu=bu, kc=kc, j=j, a2=a2: e.matmul(k.ps[bu][:, 0:n], lhsT=wup[:, kc, j * 128:(j + 1) * 128], rhs=a2[:, kc, 0:n],
                                                                    start=(kc == 0), stop=(kc == 7)), reads=['wup%d' % (j // 11), ak], writes=[pku])
                for kc in range(8):
                    P.op('pe', lambda e, bg=bg, kc=kc, j=j, a2=a2: e.matmul(k.ps[bg][:, 0:n], lhsT=wup[:, kc, FH + j * 128:FH + (j + 1) * 128], rhs=a2[:, kc, 0:n],
                                                                    start=(kc == 0), stop=(kc == 7)), reads=['wup%d' % (2 + j // 11), ak], writes=[pkg])
                P.op('act', lambda e, bg=bg, s2=s2: e.activation(out=sg[s2][:, 0:n], in_=k.ps[bg][:, 0:n], func=AF.Silu), reads=[pkg], writes=['sg%d' % s2])
                P.op('dve', lambda e, bu=bu, s2=s2, j=j: e.tensor_tensor(out=hid[:, j, 0:n], in0=k.ps[bu][:, 0:n], in1=sg[s2][:, 0:n], op=ALU.mult),
                     reads=[pku, 'sg%d' % s2], writes=['hid'])
            for m in range(4):
                down(ti, m)
            if nxt is not None:
                part1(nxt)
                part2(nxt)
            for m in range(4, 8):
                down(ti, m)
            P.dma('sp', k.hT[:, :, ti * FT:ti * FT + n], hT_t[par][:, :, 0:n], reads=['hT%d' % par], writes=hkeys(ti))


def phase_out(k):
    P = k.P
    I = k.I
    with Ph(k, 'pout') as ph:
        gf = ph.sb('gf', [128, 8], F32)
        load_vec_fm(k, ph, I['final_norm_g'].rearrange("(c p) -> c p", p=128), 8, gf[:], 'gf')
        hT_t = [ph.sb('hT%d' % i, [128, 8, 512], F32) for i in range(2)]
        sq = ph.sb('sq', [128, 8, 512], BF16)
        tmpn = ph.sb('tmpn', [128, 512], F32)
        rstd = ph.sb('rstd', [128, 512], F32)
        hn = ph.sb('hn', [128, 8, 512], F32)
        ot = [ph.sb('ot%d' % i, [128, D], F32) for i in range(2)]
        cnt = 0

        def oload(ti):
            t0, n = TILES[ti]
            P.dma('sp', hT_t[ti % 2][:, :, 0:n], k.hT[:, :, t0:t0 + n], reads=[('hT', b) for b in range(t0 // 128, (t0 + n) // 128)], writes=['hT%d' % (ti % 2)])
        oload(1)
        for ti, (t0, n) in enumerate(TILES):
            if ti == 0:
                continue
            par = ti % 2
            hk = 'hT%d' % par
            if ti + 1 < len(TILES):
                oload(ti + 1)
            norm_rstd(k, ph, hT_t[par], hk, n, sq, 'sq', rstd, 'rstd', tmpn, 'tmpn')
            P.op('dve', lambda e, par=par, n=n: e.tensor_tensor(out=hn[:, :, 0:n], in0=hT_t[par][:, :, 0:n],
                                                                 in1=rstd[:, 0:n].unsqueeze(1).to_broadcast([128, 8, n]), op=ALU.mult),
                 reads=[hk, 'rstd'], writes=['hn'])
            for c in range(8):
                P.op('act', lambda e, n=n, c=c: e.activation(out=hn[:, c, 0:n], in_=hn[:, c, 0:n], func=AF.Identity, scale=gf[:, c:c + 1]),
                     reads=['hn', 'gf'], writes=['hn'])
            for sblk in range(4):
                op_ = cnt % 2
                cnt += 1
                ok = 'ot%d' % op_
                for half in range(2):
                    b = 2 * op_ + half
                    pk = 'ps%d' % b
                    for j in range(4):
                        c = 4 * half + j
                        P.op('pe', lambda e, b=b, j=j, c=c, sblk=sblk: e.transpose(out=k.ps[b][:, j * 128:(j + 1) * 128], in_=hn[:, c, sblk * 128:(sblk + 1) * 128],
                                                                                   identity=k.identF[:]), reads=['hn', 'identF'], writes=[pk])
                    if half == 0:
                        P.op('act', lambda e, b=b, op_=op_: e.activation(out=ot[op_][:, 0:512], in_=k.ps[b][:, :], func=AF.Copy), reads=[pk], writes=[ok])
                    else:
                        P.op('dve', lambda e, b=b, op_=op_: e.tensor_copy(out=ot[op_][:, 512:1024], in_=k.ps[b][:, :]), reads=[pk], writes=[ok])
                r0 = t0 - C + sblk * 128
                P.dma('sp', k.out[r0:r0 + 128, :], ot[op_][:], reads=[ok], writes=[('out', r0)])
        k.P.barrier(final=True)


def _consts():
    f32 = np.float32
    rows = S // 64
    row = np.repeat(np.arange(rows, dtype=f32), 64)
    col = np.tile(np.arange(64, dtype=f32), rows)
    inv = np.power(f32(10000.0), -np.arange(16, dtype=f32) / f32(16)).astype(f32)
    ang = np.concatenate([row[:, None] * inv[None], col[:, None] * inv[None]], axis=-1).astype(f32)
    cos, sin = np.cos(ang).astype(f32), np.sin(ang).astype(f32)
    pidx = (np.arange(128) % 64) % 32
    cos2 = np.ascontiguousarray(cos[:, pidx].T)
    sin2 = np.ascontiguousarray(sin[:, pidx].T)
    rot = np.zeros((128, 128), f32)
    for m in range(128):
        if (m % 64) < 32:
            rot[m + 32, m] = -1.0
        else:
            rot[m - 32, m] = 1.0
    ident = np.eye(128, dtype=f32)
    j = np.arange(128)[:, None]
    i = np.arange(128)[None, :]
    nm_prev = np.where(j >= i, 0.0, -30000.0).astype(f32)
    nm_next = np.where(j <= i, 0.0, -30000.0).astype(f32)
    return {'cos2': cos2, 'sin2': sin2, 'rotm': rot, 'ident': ident,
            'nm_prev': np.ascontiguousarray(np.tile(nm_prev, (1, 4))), 'nm_next': np.ascontiguousarray(np.tile(nm_next, (1, 4)))}


_NC = None


def kernel(**inputs):
    global _NC
    inp = {n: np.ascontiguousarray(np.asarray(v, dtype=np.float32)) for n, v in inputs.items()}
    if _NC is None:
        _NC = build()
    consts = _consts()
    shared = {n: inp[n] for n in ('mod_w', 'mod_b', 'norm1_g', 'norm2_g', 'w_in', 'attn_sink', 'conv_dw_w', 'conv_dw_b', 'conv_ln_g',
                                  'conv_ln_b', 'lru_conv_w', 'lru_conv_b', 'lru_wa', 'lru_ba', 'lru_wx', 'lru_bx', 'lru_lam',
                                  'w_o_attn', 'w_o_conv', 'w_o_lru', 'w_out', 'ffn_w_up', 'ffn_w_down', 'final_norm_g')}
    shared.update(consts)
    in_maps = []
    for core in range(8):
        b = core % 2
        m = dict(shared)
        m['x'] = inp['x'][b]
        m['ctx'] = inp['ctx'][b]
        m['cvec'] = np.ascontiguousarray(np.stack([inp['c'][b], inp['c_ctx']], axis=0))
        in_maps.append(m)
    res = run_bass_kernel_spmd(_NC, in_maps, core_ids=list(range(8)))
    return np.stack([res.results[0]['out'], res.results[1]['out']], axis=0).astype(np.float32)
```

```python
import contextlib
import numpy as np
import concourse.bass as bass
import concourse.mybir as mybir
from concourse.bass_utils import run_bass_kernel_spmd

F32 = mybir.dt.float32
BF16 = mybir.dt.bfloat16
AF = mybir.ActivationFunctionType
ALU = mybir.AluOpType
AX = mybir.AxisListType

ENGS = ['pe', 'act', 'dve', 'pool', 'sp']
NDMA = 8

S = 8192
C = 256
NT = S + C
D = 1024
L = 2
FH = 2816
EPS = 1e-6
TILES = [(0, 256)] + [(256 + 512 * i, 512) for i in range(16)]
NBLK = NT // 128


class _StopProbe:
    def __init__(self):
        self.stop = True

    def matmul(self, *a, **kw):
        self.stop = bool(kw.get('stop', True))
        return self

    def transpose(self, *a, **kw):
        self.stop = True
        return self


class Prog:
    def __init__(self, nc):
        self.nc = nc
        self.thunks = {e: [] for e in ENGS}
        self.count = {e: 0 for e in ENGS}
        self.waited = {e: {} for e in ENGS}
        self.res_w = {}
        self.res_r = {}
        self.dma_rr = {e: 0 for e in ENGS}
        self.dma_cnt = {}
        self.sem = {}
        self.semnames = list(ENGS[:4]) + ['pe_1', 'pe_2', 'pe_3', 'pe_4', 'pe_5']
        self.engsem = {e: e for e in ENGS}
        self.nspare = 0
        for q in ('sp', 'pool'):
            for i in range(NDMA):
                self.semnames.append('d_%s_%d' % (q, i))

    def alloc_sems(self, stack):
        for s in self.semnames:
            self.sem[s] = stack.enter_context(self.nc.semaphore(s))

    def _collect(self, eng, reads, writes, extra=()):
        waits = {}

        def need(ev):
            if ev is None:
                return
            sk, val, src = ev
            if src == eng and eng == 'pe':
                return
            if self.waited[eng].get(sk, 0) >= val:
                return
            if waits.get(sk, 0) < val:
                waits[sk] = val
        for k in reads:
            need(self.res_w.get(k))
            if isinstance(k, str) and k.startswith('ps'):
                for ev in self.res_r.get(k, {}).values():
                    if ev[2] != eng:
                        need(ev)
        for k in writes:
            need(self.res_w.get(k))
            for ev in self.res_r.get(k, {}).values():
                need(ev)
        for ev in extra:
            need(ev)
        for sk, val in waits.items():
            self.waited[eng][sk] = val
        return list(waits.items())

    def _record(self, ev, reads, writes):
        for k in reads:
            d = self.res_r.setdefault(k, {})
            old = d.get(ev[0])
            if old is None or old[1] < ev[1]:
                d[ev[0]] = ev
        for k in writes:
            self.res_w[k] = ev
            self.res_r[k] = {}

    def op(self, eng, fn, reads=(), writes=()):
        wl = self._collect(eng, reads, writes)
        mysem = self.engsem[eng]
        inc = True
        if eng == 'pe':
            pr = _StopProbe()
            fn(pr)
            inc = pr.stop
        if inc:
            self.count[eng] += 1
            ev = (mysem, self.count[eng], eng)
        else:
            ev = (mysem, self.count[eng] + 1, eng)
        sem = self.sem

        def thunk(e):
            for sk, val in wl:
                e.wait_ge(sem[sk], val)
            ins = fn(e)
            if inc:
                ins.then_inc(sem[mysem], 1)
        self.thunks[eng].append(thunk)
        self._record(ev, reads, writes)

    def dma(self, q, out, in_, reads=(), writes=(), **kw):
        i = self.dma_rr[q] % NDMA
        self.dma_rr[q] += 1
        sk = 'd_%s_%d' % (q, i)
        n = self.dma_cnt.get(sk, 0) + 1
        self.dma_cnt[sk] = n
        extra = [(sk, 16 * (n - 1), 'dma')] if n > 1 else []
        wl = self._collect(q, reads, writes, extra)
        ev = (sk, 16 * n, 'dma')
        sem = self.sem

        def thunk(e):
            for s, val in wl:
                e.wait_ge(sem[s], val)
            e.dma_start(out=out, in_=in_, **kw).then_inc(sem[sk], 16)
        self.thunks[q].append(thunk)
        self._record(ev, reads, writes)

    def barrier(self, final=False):
        tot = []
        for e in ENGS[:4]:
            if self.count[e]:
                tot.append((self.engsem[e], self.count[e], e))
        for sk, n in self.dma_cnt.items():
            tot.append((sk, 16 * n, 'dma'))
        sem = self.sem
        for eng in (['sp'] if final else ENGS):
            wl = []
            for sk, val, src in tot:
                if src == eng:
                    continue
                if self.waited[eng].get(sk, 0) >= val:
                    continue
                self.waited[eng][sk] = val
                wl.append((sk, val))

            def thunk(e, wl=wl):
                for s, val in wl:
                    e.wait_ge(sem[s], val)
            self.thunks[eng].append(thunk)
        self.res_w.clear()
        self.res_r.clear()
        if not final and self.count['pe'] > 12000 and self.nspare < 5:
            self.nspare += 1
            self.engsem['pe'] = 'pe_%d' % self.nspare
            self.count['pe'] = 0

    def flush(self):
        nc = self.nc
        th = self.thunks
        with nc.Block() as block:
            @block.sync
            def _(e):
                for t in th['sp']:
                    t(e)

            @block.tensor
            def _(e):
                for t in th['pe']:
                    t(e)

            @block.scalar
            def _(e):
                for t in th['act']:
                    t(e)

            @block.vector
            def _(e):
                for t in th['dve']:
                    t(e)

            @block.gpsimd
            def _(e):
                for t in th['pool']:
                    t(e)
        self.thunks = {e: [] for e in ENGS}


class K:
    pass


def build(stop=None, debug=False):
    nc = bass.Bass("TRN2", target_bir_lowering=False)
    k = K()
    k.nc = nc

    def din(name, shape):
        return nc.dram_tensor(name, list(shape), F32, kind="ExternalInput").ap()

    def dscr(name, shape, dt):
        if debug:
            return nc.dram_tensor(name, list(shape), dt, kind="ExternalOutput").ap()
        return nc.dram_tensor(name, list(shape), dt).ap()
    I = {}
    I['x'] = din('x', [S, D])
    I['ctx'] = din('ctx', [C, D])
    I['cvec'] = din('cvec', [2, D])
    I['mod_w'] = din('mod_w', [L, D, 6 * D])
    I['mod_b'] = din('mod_b', [L, 6 * D])
    I['norm1_g'] = din('norm1_g', [L, D])
    I['norm2_g'] = din('norm2_g', [L, D])
    I['w_in'] = din('w_in', [L, D, 5888])
    I['attn_sink'] = din('attn_sink', [L, 8])
    I['conv_dw_w'] = din('conv_dw_w', [L, 31, 512])
    I['conv_dw_b'] = din('conv_dw_b', [L, 512])
    I['conv_ln_g'] = din('conv_ln_g', [L, 512])
    I['conv_ln_b'] = din('conv_ln_b', [L, 512])
    I['lru_conv_w'] = din('lru_conv_w', [L, 4, 512])
    I['lru_conv_b'] = din('lru_conv_b', [L, 512])
    I['lru_wa'] = din('lru_wa', [L, 2, 8, 64, 64])
    I['lru_ba'] = din('lru_ba', [L, 2, 512])
    I['lru_wx'] = din('lru_wx', [L, 2, 8, 64, 64])
    I['lru_bx'] = din('lru_bx', [L, 2, 512])
    I['lru_lam'] = din('lru_lam', [L, 2, 512])
    I['w_o_attn'] = din('w_o_attn', [L, 512, D])
    I['w_o_conv'] = din('w_o_conv', [L, 512, D])
    I['w_o_lru'] = din('w_o_lru', [L, 512, D])
    I['w_out'] = din('w_out', [L, D, D])
    I['ffn_w_up'] = din('ffn_w_up', [L, D, 2 * FH])
    I['ffn_w_down'] = din('ffn_w_down', [L, FH, D])
    I['final_norm_g'] = din('final_norm_g', [D])
    I['cos2'] = din('cos2', [128, S])
    I['sin2'] = din('sin2', [128, S])
    I['rotm'] = din('rotm', [128, 128])
    I['ident'] = din('ident', [128, 128])
    I['nm_prev'] = din('nm_prev', [128, 512])
    I['nm_next'] = din('nm_next', [128, 512])
    k.I = I
    k.out = nc.dram_tensor('out', [S, D], F32, kind="ExternalOutput").ap()
    k.hT = dscr('hT', [128, 8, NT], F32)
    k.aT = dscr('aT', [128, 8, NT], BF16)
    k.qT = dscr('qT', [128, 4, NT], BF16)
    k.kT = dscr('kT', [128, NT], BF16)
    k.V = dscr('Vtok', [NT, 128], BF16)
    k.hcT = dscr('hcT', [128, 4, NT], BF16)
    k.lxT = dscr('lxT', [128, 4, NT], BF16)
    k.GT = dscr('GT', [128, 4, NT], BF16)
    k.yaT = dscr('yaT', [128, 4, NT], BF16)
    k.ycT = dscr('ycT', [128, 4, NT], BF16)
    k.ylT = dscr('ylT', [128, 4, NT], BF16)

    P = Prog(nc)
    k.P = P
    with contextlib.ExitStack() as top:
        P.alloc_sems(top)
        sbt = lambda name, shape, dt: top.enter_context(nc.sbuf_tensor(name, list(shape), dt))
        k.ps = [top.enter_context(nc.psum_tensor('ps%d' % i, [128, 512], F32)) for i in range(8)]
        k.identF = sbt('identF', [128, 128], F32)
        k.identB = sbt('identB', [128, 128], BF16)
        k.onesB = sbt('onesB', [128, 128], BF16)
        k.rotB = sbt('rotB', [128, 128], BF16)
        k.nmp = sbt('nmp', [128, 512], BF16)
        k.nmn = sbt('nmn', [128, 512], BF16)
        k.mods = sbt('mods', [128, 2, 6, 8], F32)
        k.A1 = sbt('A1', [128, 2, 8], F32)
        k.A2 = sbt('A2', [128, 2, 8], F32)
        P.dma('sp', k.identF[:], I['ident'], writes=['identF'])
        P.dma('pool', k.identB[:], I['ident'], writes=['identB'])
        P.dma('pool', k.rotB[:], I['rotm'], writes=['rotB'])
        P.dma('pool', k.nmp[:], I['nm_prev'], writes=['nmp'])
        P.dma('pool', k.nmn[:], I['nm_next'], writes=['nmn'])
        P.op('dve', lambda e: e.memset(k.onesB[:], 1.0), writes=['onesB'])
        k.epsT = sbt('epsT', [128, 1], F32)
        k.oneT = sbt('oneT', [128, 1], F32)
        P.op('dve', lambda e: e.memset(k.epsT[:], EPS), writes=['epsT'])
        P.op('dve', lambda e: e.memset(k.oneT[:], 1.0), writes=['oneT'])
        P.barrier()
        P.flush()
        plist = [('in', lambda: phase_in(k))]
        for l in range(L):
            plist += [('mod%d' % l, lambda l=l: phase_mod(k, l)), ('a%d' % l, lambda l=l: phase_a(k, l)),
                      ('lru%d' % l, lambda l=l: phase_lru(k, l)), ('conv%d' % l, lambda l=l: phase_conv(k, l)),
                      ('att%d' % l, lambda l=l: phase_att(k, l)), ('merge%d' % l, lambda l=l: phase_merge(k, l)),
                      ('ffn%d' % l, lambda l=l: phase_ffn(k, l))]
        plist.append(('out', lambda: phase_out(k)))
        for name, fn in plist:
            fn()
            if stop == name:
                break
        if stop is not None and stop != 'out':
            with Ph(k, 'pfin') as ph:
                if debug:
                    dbg = ph.sb('dbg', [128, 2 * 6 * 8 + 32], F32)
                    k.dbgout = nc.dram_tensor('dbgout', [128, 128], F32, kind="ExternalOutput").ap()
                    P.op('dve', lambda e: e.tensor_copy(out=dbg[:, 0:96], in_=k.mods[:, :, :, :].rearrange("p a b c -> p (a b c)")), reads=['mods'], writes=['dbg'])
                    P.op('dve', lambda e: e.tensor_copy(out=dbg[:, 96:112], in_=k.A1[:, :, :].rearrange("p a b -> p (a b)")), reads=['A1', 'dbg'], writes=['dbg'])
                    P.op('dve', lambda e: e.tensor_copy(out=dbg[:, 112:128], in_=k.A2[:, :, :].rearrange("p a b -> p (a b)")), reads=['A2', 'dbg'], writes=['dbg'])
                    P.dma('sp', k.dbgout, dbg[:], reads=['dbg'])
    return nc


class Ph:
    def __init__(self, k, name):
        self.k = k
        self.name = name
        self.st = contextlib.ExitStack()
        self.n = 0

    def __enter__(self):
        self.st.__enter__()
        return self

    def sb(self, name, shape, dt):
        self.n += 1
        return self.st.enter_context(self.k.nc.sbuf_tensor('%s_%s' % (self.name, name), list(shape), dt))

    def __exit__(self, *a):
        self.k.P.barrier()
        self.k.P.flush()
        return self.st.__exit__(*a)


def load_vec_fm(k, ph, src_rows_ap, n, dst_ap, tag, bank=7):
    P = k.P
    stg = ph.sb('vst_' + tag, [128, 128], F32)
    key = 'vst_' + tag
    P.dma('sp', stg[0:n, :], src_rows_ap, writes=[key])
    psb = k.ps[bank]
    pk = 'ps%d' % bank
    P.op('pe', lambda e: e.transpose(out=psb[:, 0:n], in_=stg[0:n, :], identity=k.identF[0:n, 0:n]),
         reads=[key, 'identF'], writes=[pk])
    P.op('dve', lambda e: e.tensor_copy(out=dst_ap, in_=psb[:, 0:n]), reads=[pk], writes=[tag])


def norm_rstd(k, ph, hT_t, hkey, n, sq, sqkey, rstd, rkey, tmp, tkey, bank=7):
    P = k.P
    psb = k.ps[bank]
    pk = 'ps%d' % bank
    P.op('act', lambda e: e.activation(out=sq[:, :, 0:n], in_=hT_t[:, :, 0:n], func=AF.Square), reads=[hkey], writes=[sqkey])
    for c in range(8):
        P.op('pe', lambda e, c=c: e.matmul(psb[:, 0:n], lhsT=k.onesB[:], rhs=sq[:, c, 0:n], start=(c == 0), stop=(c == 7)),
             reads=[sqkey, 'onesB'], writes=[pk])
    P.op('act', lambda e: e.activation(out=tmp[:, 0:n], in_=psb[:, 0:n], func=AF.Sqrt, scale=1.0 / D, bias=k.epsT[:, 0:1]),
         reads=[pk, 'epsT'], writes=[tkey])
    P.op('dve', lambda e: e.reciprocal(out=rstd[:, 0:n], in_=tmp[:, 0:n]), reads=[tkey], writes=[rkey])


def wload(k, dst, src, key):
    k.P.dma('pool', dst, src, writes=[key])


def phase_in(k):
    P = k.P
    with Ph(k, 'pin') as ph:
        xin = [ph.sb('xin%d' % i, [128, D], F32) for i in range(2)]
        hst = [ph.sb('hst%d' % i, [128, 8, 128], F32) for i in range(2)]
        def load(s):
            src = k.I['ctx'][s * 128:(s + 1) * 128, :] if s < 2 else k.I['x'][(s - 2) * 128:(s - 1) * 128, :]
            P.dma('sp', xin[s % 2][:], src, writes=['xin%d' % (s % 2)])
        load(0)
        for s in range(NBLK):
            par = s % 2
            xk, hk = 'xin%d' % par, 'hst%d' % par
            if s + 1 < NBLK:
                load(s + 1)
            for half in range(2):
                b = 2 * par + half
                pk = 'ps%d' % b
                for j in range(4):
                    c = 4 * half + j
                    P.op('pe', lambda e, b=b, j=j, c=c, par=par: e.transpose(out=k.ps[b][:, j * 128:(j + 1) * 128],
                                                                             in_=xin[par][:, c * 128:(c + 1) * 128], identity=k.identF[:]),
                         reads=[xk, 'identF'], writes=[pk])
                eng = 'act' if half == 0 else 'dve'
                if eng == 'act':
                    P.op('act', lambda e, b=b, half=half, par=par: e.activation(
                        out=hst[par][:, 4 * half:4 * half + 4, :], in_=k.ps[b][:, :].rearrange("p (j t) -> p j t", t=128), func=AF.Copy),
                        reads=[pk], writes=[hk])
                else:
                    P.op('dve', lambda e, b=b, half=half, par=par: e.tensor_copy(
                        out=hst[par][:, 4 * half:4 * half + 4, :], in_=k.ps[b][:, :].rearrange("p (j t) -> p j t", t=128)),
                        reads=[pk], writes=[hk])
            P.dma('sp', k.hT[:, :, s * 128:(s + 1) * 128], hst[par][:], reads=[hk], writes=[('hT', s)])


def phase_mod(k, l):
    P = k.P
    I = k.I
    with Ph(k, 'pmod%d' % l) as ph:
        cT = ph.sb('cT', [128, 16], F32)
        load_vec_fm(k, ph, I['cvec'].rearrange("r (c p) -> (r c) p", p=128), 16, cT[:], 'cT')
        scT = ph.sb('scT', [128, 16], F32)
        P.op('act', lambda e: e.activation(out=scT[:], in_=cT[:], func=AF.Silu), reads=['cT'], writes=['scT'])
        modb = ph.sb('modb', [128, 48], F32)
        load_vec_fm(k, ph, I['mod_b'][l].rearrange("(c p) -> c p", p=128), 48, modb[:], 'modb')
        g1 = ph.sb('g1n', [128, 8], F32)
        g2 = ph.sb('g2n', [128, 8], F32)
        load_vec_fm(k, ph, I['norm1_g'][l].rearrange("(c p) -> c p", p=128), 8, g1[:], 'g1n')
        load_vec_fm(k, ph, I['norm2_g'][l].rearrange("(c p) -> c p", p=128), 8, g2[:], 'g2n')
        wm = [ph.sb('wm%d' % i, [128, 8, 1024], F32) for i in range(2)]
        psb = k.ps[0]
        import os
        LVL = int(os.environ.get('MODLVL', '9'))
        if LVL < 1:
            return
        for piece in range(6):
            par = piece % 2
            wk = 'wm%d' % par
            for kc in range(8):
                P.dma('sp', wm[par][:, kc, :], I['mod_w'][l, kc * 128:(kc + 1) * 128, piece * 1024:(piece + 1) * 1024], writes=[wk])
            if LVL < 2:
                continue
            for oc in range(8):
                col = (piece * 8 + oc) * 2
                for kc in range(8):
                    P.op('pe', lambda e, par=par, oc=oc, kc=kc, col=col: e.matmul(
                        psb[:, col:col + 2], lhsT=wm[par][:, kc, oc * 128:(oc + 1) * 128],
                        rhs=scT[:, :].rearrange("p (r c) -> p c r", c=8)[:, kc, :], start=(kc == 0), stop=(kc == 7)),
                        reads=[wk, 'scT'], writes=['ps0'])
        if LVL < 3:
            return
        for r in range(2):
            P.op('dve', lambda e, r=r: e.tensor_tensor(
                out=k.mods[:, r, :, :].rearrange("p a b -> p (a b)"),
                in0=psb[:, 0:96].rearrange("p (o r) -> p r o", r=2)[:, r, :], in1=modb[:], op=ALU.add),
                reads=['ps0', 'modb'], writes=['mods'])
            if LVL < 4:
                continue
            P.op('dve', lambda e, r=r: e.scalar_tensor_tensor(out=k.A1[:, r, :], in0=k.mods[:, r, 1, :], scalar=1.0, in1=g1[:],
                                                              op0=ALU.add, op1=ALU.mult), reads=['mods', 'g1n'], writes=['A1'])
            P.op('dve', lambda e, r=r: e.scalar_tensor_tensor(out=k.A2[:, r, :], in0=k.mods[:, r, 4, :], scalar=1.0, in1=g2[:],
                                                              op0=ALU.add, op1=ALU.mult), reads=['mods', 'g2n'], writes=['A2'])


def phase_a(k, l):
    P = k.P
    I = k.I
    W = I['w_in'][l]
    with Ph(k, 'pa%d' % l) as ph:
        wq = ph.sb('wq', [128, 8, 512], BF16)
        wk_ = ph.sb('wk', [128, 8, 128], BF16)
        wv = ph.sb('wv', [128, 8, 128], BF16)
        wcols = ph.sb('wcols', [128, 8, 2048], BF16)
        Wr = W.rearrange("(kc p) n -> p kc n", p=128)
        for c in range(4):
            wload(k, wq[:, :, c * 128:c * 128 + 64], Wr[:, :, c * 64:(c + 1) * 64], 'wq')
            wload(k, wq[:, :, c * 128 + 64:(c + 1) * 128], Wr[:, :, (c + 4) * 64:(c + 5) * 64], 'wq')
        wload(k, wk_[:], Wr[:, :, 512:640], 'wk')
        wload(k, wv[:], Wr[:, :, 640:768], 'wv')
        for kc in range(8):
            wload(k, wcols[:, kc, :], W[kc * 128:(kc + 1) * 128, 768:2816], 'wcols')
        hT_t = [ph.sb('hT%d' % i, [128, 8, 512], F32) for i in range(2)]
        sq = ph.sb('sq', [128, 8, 512], BF16)
        tmpn = ph.sb('tmpn', [128, 512], F32)
        rstd = ph.sb('rstd', [128, 512], F32)
        hn = ph.sb('hn', [128, 8, 512], F32)
        aT_t = [ph.sb('aT%d' % i, [128, 8, 512], BF16) for i in range(2)]
        cs = [ph.sb('cs%d' % i, [128, 2, 512], F32) for i in range(2)]
        qb = [ph.sb('qb%d' % i, [128, 512], BF16) for i in range(2)]
        t1 = [ph.sb('t1%d' % i, [128, 512], F32) for i in range(2)]
        t2 = [ph.sb('t2%d' % i, [128, 512], F32) for i in range(2)]
        qo = [ph.sb('qo%d' % i, [128, 5, 512], BF16) for i in range(2)]
        vo = [ph.sb('vo%d' % i, [128, 4, 128], BF16) for i in range(2)]
        sg = [ph.sb('sg%d' % i, [128, 512], F32) for i in range(2)]
        oc4 = [ph.sb('oc4%d' % i, [128, 4, 512], BF16) for i in range(3)]
        st = {'cnt': 0}

        def load(ti):
            t0, n = TILES[ti]
            par = ti % 2
            P.dma('sp', hT_t[par][:, :, 0:n], k.hT[:, :, t0:t0 + n], reads=[('hT', b) for b in range(t0 // 128, (t0 + n) // 128)], writes=['hT%d' % par])
            if ti > 0:
                P.dma('sp', cs[par][:, 0, 0:n], I['cos2'][:, t0 - C:t0 - C + n], writes=['cs%d' % par])
                P.dma('sp', cs[par][:, 1, 0:n], I['sin2'][:, t0 - C:t0 - C + n], writes=['cs%d' % par])

        def part1(ti):
            t0, n = TILES[ti]
            par = ti % 2
            norm_rstd(k, ph, hT_t[par], 'hT%d' % par, n, sq, 'sq', rstd, 'rstd', tmpn, 'tmpn')

        def part2(ti):
            t0, n = TILES[ti]
            par = ti % 2
            r = 1 if ti == 0 else 0
            hk, ak = 'hT%d' % par, 'aT%d' % par
            P.op('dve', lambda e: e.tensor_tensor(out=hn[:, :, 0:n], in0=hT_t[par][:, :, 0:n],
                                                   in1=rstd[:, 0:n].unsqueeze(1).to_broadcast([128, 8, n]), op=ALU.mult),
                 reads=[hk, 'rstd'], writes=['hn'])
            for c in range(8):
                P.op('act', lambda e, c=c: e.activation(out=aT_t[par][:, c, 0:n], in_=hn[:, c, 0:n], func=AF.Identity,
                                                        scale=k.A1[:, r, c:c + 1], bias=k.mods[:, r, 0, c:c + 1]),
                     reads=['hn', 'A1', 'mods'], writes=[ak])
            P.dma('sp', k.aT[:, :, t0:t0 + n], aT_t[par][:, :, 0:n], reads=[ak], writes=[('aT', ti)])

        def proj(wsl, wkey, par, n, ak):
            b = st['cnt'] % 4
            st['cnt'] += 1
            pk = 'ps%d' % b
            for kc in range(8):
                P.op('pe', lambda e, kc=kc: e.matmul(k.ps[b][:, 0:n], lhsT=wsl(kc), rhs=aT_t[par][:, kc, 0:n], start=(kc == 0), stop=(kc == 7)),
                     reads=[wkey, ak], writes=[pk])
            return b, pk

        def rope_tail(c, b, pk, p2, par, n):
            b2 = 4 + p2
            pk2 = 'ps%d' % b2
            P.op('pe', lambda e: e.matmul(k.ps[b2][:, 0:n], lhsT=k.rotB[:], rhs=qb[p2][:, 0:n], start=True, stop=True),
                 reads=['rotB', 'qb%d' % p2], writes=[pk2])
            P.op('dve', lambda e: e.tensor_tensor(out=t1[p2][:, 0:n], in0=k.ps[b][:, 0:n], in1=cs[par][:, 0, 0:n], op=ALU.mult),
                 reads=[pk, 'cs%d' % par, 'qb%d' % p2], writes=['t1%d' % p2])
            P.op('dve', lambda e: e.tensor_tensor(out=t2[p2][:, 0:n], in0=k.ps[b2][:, 0:n], in1=cs[par][:, 1, 0:n], op=ALU.mult),
                 reads=[pk2, 'cs%d' % par], writes=['t2%d' % p2])
            P.op('pool', lambda e: e.tensor_tensor(out=qo[par][:, c, 0:n], in0=t1[p2][:, 0:n], in1=t2[p2][:, 0:n], op=ALU.add),
                 reads=['t1%d' % p2, 't2%d' % p2], writes=['qo%d' % par])

        def group(ti, grp):
            t0, n = TILES[ti]
            par = ti % 2
            ak = 'aT%d' % par
            ob = oc4[grp]
            okey = 'oc4%d' % grp
            for c in range(4):
                if grp == 0:
                    b, pk = proj(lambda kc, c=c: wcols[:, kc, c * 128:(c + 1) * 128], 'wcols', par, n, ak)
                    b2 = 4 + (st['cnt'] % 2)
                    pk2 = 'ps%d' % b2
                    s2 = st['cnt'] % 2
                    for kc in range(8):
                        P.op('pe', lambda e, kc=kc, c=c, b2=b2: e.matmul(k.ps[b2][:, 0:n], lhsT=wcols[:, kc, 512 + c * 128:512 + (c + 1) * 128],
                                                                        rhs=aT_t[par][:, kc, 0:n], start=(kc == 0), stop=(kc == 7)),
                             reads=['wcols', ak], writes=[pk2])
                    P.op('act', lambda e, b2=b2, s2=s2: e.activation(out=sg[s2][:, 0:n], in_=k.ps[b2][:, 0:n], func=AF.Sigmoid),
                         reads=[pk2], writes=['sg%d' % s2])
                    P.op('dve', lambda e, b=b, s2=s2, c=c: e.tensor_tensor(out=ob[:, c, 0:n], in0=k.ps[b][:, 0:n], in1=sg[s2][:, 0:n], op=ALU.mult),
                         reads=[pk, 'sg%d' % s2], writes=[okey])
                else:
                    off = 1024 if grp == 1 else 1536
                    b, pk = proj(lambda kc, c=c, off=off: wcols[:, kc, off + c * 128:off + (c + 1) * 128], 'wcols', par, n, ak)
                    fn = AF.Copy if grp == 1 else AF.Gelu_apprx_tanh
                    P.op('act', lambda e, b=b, c=c, fn=fn: e.activation(out=ob[:, c, 0:n], in_=k.ps[b][:, 0:n], func=fn),
                         reads=[pk], writes=[okey])
            dst = (k.hcT, k.lxT, k.GT)[grp]
            P.dma('sp', dst[:, :, t0:t0 + n], ob[:, :, 0:n], reads=[okey], writes=[('hcT', 'lxT', 'GT')[grp]])

        load(0)
        part1(0)
        part2(0)
        for ti, (t0, n) in enumerate(TILES):
            par = ti % 2
            r = 1 if ti == 0 else 0
            ak = 'aT%d' % par
            nxt = ti + 1 if ti + 1 < len(TILES) else None
            if nxt is not None:
                load(nxt)
            pend = None
            for c in range(5):
                wsl = (lambda kc, c=c: wq[:, kc, c * 128:(c + 1) * 128]) if c < 4 else (lambda kc: wk_[:, kc, :])
                b, pk = proj(wsl, 'wq' if c < 4 else 'wk', par, n, ak)
                if r == 1:
                    P.op('act', lambda e, b=b, c=c, par=par, n=n: e.activation(out=qo[par][:, c, 0:n], in_=k.ps[b][:, 0:n], func=AF.Copy),
                         reads=[pk], writes=['qo%d' % par])
                else:
                    p2 = c % 2
                    P.op('act', lambda e, b=b, p2=p2, n=n: e.activation(out=qb[p2][:, 0:n], in_=k.ps[b][:, 0:n], func=AF.Copy),
                         reads=[pk], writes=['qb%d' % p2])
                    if pend is not None:
                        rope_tail(*pend)
                    pend = (c, b, pk, p2, par, n)
            if pend is not None:
                rope_tail(*pend)
            P.dma('sp', k.qT[:, :, t0:t0 + n], qo[par][:, 0:4, 0:n], reads=['qo%d' % par], writes=['qT'])
            P.dma('sp', k.kT[:, t0:t0 + n], qo[par][:, 4, 0:n], reads=['qo%d' % par], writes=['kT'])
            for sblk in range(n // 128):
                b = st['cnt'] % 4
                st['cnt'] += 1
                pk = 'ps%d' % b
                for kc in range(8):
                    P.op('pe', lambda e, b=b, kc=kc, sblk=sblk, par=par: e.matmul(k.ps[b][:, 0:128], lhsT=aT_t[par][:, kc, sblk * 128:(sblk + 1) * 128],
                                                                        rhs=wv[:, kc, :], start=(kc == 0), stop=(kc == 7)),
                         reads=['wv', ak], writes=[pk])
                P.op('dve', lambda e, b=b, sblk=sblk, par=par: e.tensor_copy(out=vo[par][:, sblk, :], in_=k.ps[b][:, 0:128]),
                     reads=[pk], writes=['vo%d' % par])
            P.dma('sp', k.V[t0:t0 + n, :].rearrange("(s p) d -> p s d", p=128), vo[par][:, 0:n // 128, :], reads=['vo%d' % par], writes=['V'])
            group(ti, 0)
            if nxt is not None:
                part1(nxt)
            group(ti, 1)
            if nxt is not None:
                part2(nxt)
            group(ti, 2)


def phase_lru(k, l):
    P = k.P
    I = k.I
    with Ph(k, 'plru%d' % l) as ph:
        cw = ph.sb('cw', [128, 16], F32)
        cb = ph.sb('cb', [128, 4], F32)
        ba = ph.sb('ba', [128, 8], F32)
        bx = ph.sb('bx', [128, 8], F32)
        lam = ph.sb('lam', [128, 8], F32)
        c1 = ph.sb('c1', [128, 8], F32)
        load_vec_fm(k, ph, I['lru_conv_w'][l].rearrange("t (c p) -> (t c) p", p=128), 16, cw[:], 'cw')
        load_vec_fm(k, ph, I['lru_conv_b'][l].rearrange("(c p) -> c p", p=128), 4, cb[:], 'cb')
        load_vec_fm(k, ph, I['lru_ba'][l].rearrange("d (c p) -> (d c) p", p=128), 8, ba[:], 'ba')
        load_vec_fm(k, ph, I['lru_bx'][l].rearrange("d (c p) -> (d c) p", p=128), 8, bx[:], 'bx')
        load_vec_fm(k, ph, I['lru_lam'][l].rearrange("d (c p) -> (d c) p", p=128), 8, lam[:], 'lam')
        P.op('act', lambda e: e.activation(out=c1[:], in_=lam[:], func=AF.Exp, scale=-1.0), reads=['lam'], writes=['c1'])
        P.op('dve', lambda e: e.tensor_scalar(out=c1[:], in0=c1[:], scalar1=1.0, scalar2=None, op0=ALU.add), reads=['c1'], writes=['c1'])
        P.op('act', lambda e: e.activation(out=c1[:], in_=c1[:], func=AF.Ln), reads=['c1'], writes=['c1'])
        P.op('dve', lambda e: e.tensor_scalar(out=c1[:], in0=c1[:], scalar1=-4.0, scalar2=None, op0=ALU.mult), reads=['c1'], writes=['c1'])
        for t_, nm in ((cb, 'cb'), (ba, 'ba'), (bx, 'bx')):
            P.op('dve', lambda e, t_=t_: e.tensor_scalar(out=t_[:], in0=t_[:], scalar1=0.5, scalar2=None, op0=ALU.mult), reads=[nm], writes=[nm])
        bd = ph.sb('bd', [128, 16, 128], BF16)
        P.op('pool', lambda e: e.memset(bd[:], 0.0), writes=['bd'])
        for d in range(2):
            for gi, wn in enumerate(('lru_wa', 'lru_wx')):
                for ch in range(4):
                    idx = (d * 2 + gi) * 4 + ch
                    for hb in range(2):
                        wload(k, bd[hb * 64:(hb + 1) * 64, idx, hb * 64:(hb + 1) * 64], I[wn][l, d, ch * 2 + hb], 'bd')
        dgl = ph.sb('dgl', [128, 16, 128], BF16)
        P.op('dve', lambda e: e.tensor_tensor(out=dgl[:, :, :], in0=k.identF[:, :].unsqueeze(1).to_broadcast([128, 16, 128]),
                                              in1=cw[:, :].unsqueeze(2).to_broadcast([128, 16, 128]), op=ALU.mult),
             reads=['identF', 'cw'], writes=['dgl'])
        xp = ph.sb('xp0', [128, NT + 8], BF16)
        Uh_ = ph.sb('Uh0', [128, NT], BF16)
        Uhs = [Uh_, Uh_]
        Hf = ph.sb('Hf', [128, NT], F32)
        NA, NR, NI, NB = 4, 4, 3, 2
        As = [ph.sb('As%d' % i, [128, 2048], F32) for i in range(NA)]
        tRs = [ph.sb('tR%d' % i, [128, 2048], F32) for i in range(NR)]
        tIs = [ph.sb('tI%d' % i, [128, 2048], F32) for i in range(NI)]
        Bs = [ph.sb('Bs%d' % i, [128, 2048], F32) for i in range(NB)]
        gt = [ph.sb('gt%d' % i, [128, 2048], BF16) for i in range(2)]
        lyo = ph.sb('lyo', [128, 2048], BF16)
        hc = ph.sb('hc', [128, 2], F32)
        CO, LO = 2, 261
        xk = 'xp0'
        st = {'cnt': 0, 'c2': 0}
        spans = [(0, C)] + [(C + 2048 * h, 2048) for h in range(4)]
        items = []
        for ch in range(4):
            for d in range(2):
                order = spans if d == 0 else [spans[0]] + spans[:0:-1]
                for si, (s0, sn) in enumerate(order):
                    items.append((ch, d, si, s0, sn))

        def conv4(ch):
            Uh = Uhs[ch % 2]
            uk = 'Uh0'
            P.op('pool', lambda e: e.memset(xp[:, 0:2], 0.0), writes=[xk])
            P.op('pool', lambda e: e.memset(xp[:, 258:261], 0.0), writes=[xk])
            P.op('pool', lambda e: e.memset(xp[:, LO + S:LO + S + 3], 0.0), writes=[xk])
            P.dma('sp', xp[:, CO:CO + C], k.lxT[:, ch, 0:C], reads=['lxT'], writes=[xk])
            P.dma('sp', xp[:, LO:LO + S], k.lxT[:, ch, C:NT], reads=['lxT'], writes=[xk])
            for (o, u0, n_) in ((CO, 0, C), (LO, C, S)):
                for t0 in range(0, n_, 512):
                    m = min(512, n_ - t0)
                    b = 6 + (st['c2'] % 2)
                    st['c2'] += 1
                    pk = 'ps%d' % b
                    for tap in range(4):
                        P.op('pe', lambda e, b=b, tap=tap, o=o, t0=t0, m=m: e.matmul(k.ps[b][:, 0:m], lhsT=dgl[:, tap * 4 + ch, :],
                                                                                    rhs=xp[:, o + t0 + tap - 2:o + t0 + tap - 2 + m], start=(tap == 0), stop=(tap == 3)),
                             reads=['dgl', xk], writes=[pk])
                    P.op('act', lambda e, b=b, u0=u0, t0=t0, m=m: e.activation(out=Uh[:, u0 + t0:u0 + t0 + m], in_=k.ps[b][:, 0:m], func=AF.Identity,
                                                                              scale=0.5, bias=cb[:, ch:ch + 1]), reads=[pk, 'cb'], writes=[uk])

        def s1(i):
            ch, d, si, s0, sn = items[i]
            Uh, uk = Uhs[ch % 2], 'Uh0'
            tR, tI, A = tRs[i % NR], tIs[i % NI], As[i % NA]
            kR, kI, kA = 'tR%d' % (i % NR), 'tI%d' % (i % NI), 'As%d' % (i % NA)
            for gi, (dstG, gk, bias) in enumerate(((tR, kR, ba), (tI, kI, bx))):
                idx = (d * 2 + gi) * 4 + ch
                for sub in range(0, sn, 512):
                    m = min(512, sn - sub)
                    b = st['cnt'] % 6
                    st['cnt'] += 1
                    pk = 'ps%d' % b
                    P.op('pe', lambda e, b=b, idx=idx, sub=sub, m=m: e.matmul(k.ps[b][:, 0:m], lhsT=bd[:, idx, :], rhs=Uh[:, s0 + sub:s0 + sub + m], start=True, stop=True),
                         reads=['bd', uk], writes=[pk])
                    P.op('act', lambda e, b=b, dstG=dstG, sub=sub, m=m, bias=bias: e.activation(out=dstG[:, sub:sub + m], in_=k.ps[b][:, 0:m], func=AF.Tanh,
                                                                                               bias=bias[:, d * 4 + ch:d * 4 + ch + 1]),
                         reads=[pk, 'ba', 'bx'], writes=[gk])
            P.op('act', lambda e: e.activation(out=A[:, 0:sn], in_=tR[:, 0:sn], func=AF.Exp, scale=c1[:, d * 4 + ch:d * 4 + ch + 1],
                                               bias=c1[:, d * 4 + ch:d * 4 + ch + 1]), reads=[kR, 'c1'], writes=[kA])

        def s2a(i):
            ch, d, si, s0, sn = items[i]
            tR, A = tRs[i % NR], As[i % NA]
            P.op('dve', lambda e: e.tensor_tensor(out=tR[:, 0:sn], in0=A[:, 0:sn], in1=A[:, 0:sn], op=ALU.mult), reads=['As%d' % (i % NA)], writes=['tR%d' % (i % NR)])

        def s2b(i):
            ch, d, si, s0, sn = items[i]
            tR = tRs[i % NR]
            kR = 'tR%d' % (i % NR)
            P.op('act', lambda e: e.activation(out=tR[:, 0:sn], in_=tR[:, 0:sn], func=AF.Sqrt, scale=-1.0, bias=k.oneT[:, 0:1]), reads=[kR, 'oneT'], writes=[kR])

        def s3(i):
            ch, d, si, s0, sn = items[i]
            Uh, uk = Uhs[ch % 2], 'Uh0'
            tR, tI, B = tRs[i % NR], tIs[i % NI], Bs[i % NB]
            kR, kI, kB = 'tR%d' % (i % NR), 'tI%d' % (i % NI), 'Bs%d' % (i % NB)
            P.op('dve', lambda e: e.scalar_tensor_tensor(out=tR[:, 0:sn], in0=tI[:, 0:sn], scalar=1.0, in1=tR[:, 0:sn], op0=ALU.add, op1=ALU.mult),
                 reads=[kR, kI], writes=[kR])
            P.op('pool', lambda e: e.tensor_tensor(out=B[:, 0:sn], in0=tR[:, 0:sn], in1=Uh[:, s0:s0 + sn], op=ALU.mult), reads=[kR, uk], writes=[kB])

        def s4(i):
            ch, d, si, s0, sn = items[i]
            A, B = As[i % NA], Bs[i % NB]
            kA, kB = 'As%d' % (i % NA), 'Bs%d' % (i % NB)
            if d == 0:
                init = 0.0 if si == 0 else Hf[:, s0 - 1:s0]
                P.op('dve', lambda e: e.tensor_tensor_scan(out=Hf[:, s0:s0 + sn], data0=A[:, 0:sn], data1=B[:, 0:sn], initial=init,
                                                           op0=ALU.mult, op1=ALU.add), reads=[kA, kB, 'Hf'], writes=['Hf'])
            else:
                init = 0.0 if si == 0 else hc[:, (si - 1) % 2:(si - 1) % 2 + 1]
                P.op('dve', lambda e: e.tensor_tensor_scan(out=B[:, 0:sn][:, ::-1], data0=A[:, 0:sn][:, ::-1], data1=B[:, 0:sn][:, ::-1],
                                                           initial=init, op0=ALU.mult, op1=ALU.add), reads=[kA, kB, 'hc'], writes=[kB])
                P.op('dve', lambda e: e.tensor_copy(out=hc[:, si % 2:si % 2 + 1], in_=B[:, 0:1]), reads=[kB], writes=['hc'])
                par = i % 2
                P.dma('sp', gt[par][:, 0:sn], k.GT[:, ch, s0:s0 + sn], reads=['GT'], writes=['gt%d' % par])
                P.op('pool', lambda e: e.tensor_tensor(out=B[:, 0:sn], in0=Hf[:, s0:s0 + sn], in1=B[:, 0:sn], op=ALU.add), reads=['Hf', kB], writes=[kB])
                P.op('dve', lambda e: e.tensor_tensor(out=lyo[:, 0:sn], in0=B[:, 0:sn], in1=gt[par][:, 0:sn], op=ALU.mult),
                     reads=[kB, 'gt%d' % par], writes=['lyo'])
                P.dma('sp', k.ylT[:, ch, s0:s0 + sn], lyo[:, 0:sn], reads=['lyo'], writes=['ylT'])

        for ch in range(4):
            conv4(ch)
            lo, hi = ch * 10, ch * 10 + 10
            for t in range(lo, hi + 3):
                if t < hi:
                    s1(t)
                if lo <= t - 1 < hi:
                    s2b(t - 1)
                if lo <= t - 2 < hi:
                    s3(t - 2)
                if lo <= t - 3 < hi:
                    s4(t - 3)
                if t < hi:
                    s2a(t)


def phase_conv(k, l):
    P = k.P
    I = k.I
    need_ctx = l < L - 1
    with Ph(k, 'pcv%d' % l) as ph:
        dw = ph.sb('dw', [128, 124], F32)
        vb = ph.sb('vb', [128, 12], F32)
        load_vec_fm(k, ph, I['conv_dw_w'][l].rearrange("t (c p) -> (t c) p", p=128), 124, dw[:], 'dw')
        load_vec_fm(k, ph, I['conv_dw_b'][l].rearrange("(c p) -> c p", p=128), 4, vb[:, 0:4], 'vb0')
        load_vec_fm(k, ph, I['conv_ln_g'][l].rearrange("(c p) -> c p", p=128), 4, vb[:, 4:8], 'vb1')
        load_vec_fm(k, ph, I['conv_ln_b'][l].rearrange("(c p) -> c p", p=128), 4, vb[:, 8:12], 'vb2')
        vkeys = ['vb0', 'vb1', 'vb2']
        dg = ph.sb('dg', [128, 124, 128], BF16)
        for j0 in range(0, 124, 31):
            P.op('dve', lambda e, j0=j0: e.tensor_tensor(out=dg[:, j0:j0 + 31, :], in0=k.identF[:, :].unsqueeze(1).to_broadcast([128, 31, 128]),
                                                         in1=dw[:, j0:j0 + 31].unsqueeze(2).to_broadcast([128, 31, 128]), op=ALU.mult),
                 reads=['identF', 'dw'], writes=['dg'])
        hin = [ph.sb('hin%d' % i, [128, 4, 542], BF16) for i in range(2)]
        cxs = [ph.sb('cx%d' % i, [128, 4, 512], F32) for i in range(2)]
        xbs = [ph.sb('xb%d' % i, [128, 4, 512], BF16) for i in range(2)]
        xss = [ph.sb('xs%d' % i, [128, 4, 512], BF16) for i in range(2)]
        mean = ph.sb('mean', [128, 512], F32)
        var = ph.sb('var', [128, 512], F32)
        rs = ph.sb('rs', [128, 512], F32)
        yc = [ph.sb('yc%d' % i, [128, 4, 512], BF16) for i in range(2)]
        tiles = [ti for ti in range(len(TILES)) if not (ti == 0 and not need_ctx)]

        def load(ti):
            t0, n = TILES[ti]
            par = ti % 2
            hk = 'hin%d' % par
            lo_seq, hi_seq = (0, C) if ti == 0 else (C, NT)
            a0, a1 = max(t0 - 15, lo_seq), min(t0 + n + 15, hi_seq)
            P.op('pool', lambda e: e.memset(hin[par][:], 0.0), writes=[hk])
            P.dma('sp', hin[par][:, :, a0 - (t0 - 15):a1 - (t0 - 15)], k.hcT[:, :, a0:a1], reads=['hcT'], writes=[hk])

        def conv_mm(ti, chs):
            t0, n = TILES[ti]
            par = ti % 2
            hk = 'hin%d' % par
            cx, xb, xs = cxs[par], xbs[par], xss[par]
            ck, bk, sk = 'cx%d' % par, 'xb%d' % par, 'xs%d' % par
            for ch in chs:
                pk = 'ps%d' % ch
                for tap in range(31):
                    P.op('pe', lambda e, ch=ch, tap=tap: e.matmul(k.ps[ch][:, 0:n], lhsT=dg[:, tap * 4 + ch, :], rhs=hin[par][:, ch, tap:tap + n],
                                                                 start=(tap == 0), stop=(tap == 30)), reads=['dg', hk], writes=[pk])
                P.op('act', lambda e, ch=ch: e.activation(out=cx[:, ch, 0:n], in_=k.ps[ch][:, 0:n], func=AF.Identity, bias=vb[:, ch:ch + 1]),
                     reads=[pk] + vkeys, writes=[ck])
                P.op('act', lambda e, ch=ch: e.activation(out=xb[:, ch, 0:n], in_=k.ps[ch][:, 0:n], func=AF.Identity, bias=vb[:, ch:ch + 1]),
                     reads=[pk] + vkeys, writes=[bk])
                P.op('act', lambda e, ch=ch: e.activation(out=xs[:, ch, 0:n], in_=k.ps[ch][:, 0:n], func=AF.Square, bias=vb[:, ch:ch + 1]),
                     reads=[pk] + vkeys, writes=[sk])

        def stats_mm(ti):
            t0, n = TILES[ti]
            par = ti % 2
            xb, xs = xbs[par], xss[par]
            bk, sk = 'xb%d' % par, 'xs%d' % par
            for ch in range(4):
                P.op('pe', lambda e, ch=ch: e.matmul(k.ps[4][:, 0:n], lhsT=k.onesB[:], rhs=xb[:, ch, 0:n], start=(ch == 0), stop=(ch == 3)),
                     reads=['onesB', bk], writes=['ps4'])
            for ch in range(4):
                P.op('pe', lambda e, ch=ch: e.matmul(k.ps[5][:, 0:n], lhsT=k.onesB[:], rhs=xs[:, ch, 0:n], start=(ch == 0), stop=(ch == 3)),
                     reads=['onesB', sk], writes=['ps5'])

        def ln_tail(ti):
            t0, n = TILES[ti]
            par = ti % 2
            cx = cxs[par]
            ck = 'cx%d' % par
            P.op('dve', lambda e: e.tensor_scalar(out=mean[:, 0:n], in0=k.ps[4][:, 0:n], scalar1=1.0 / 512, scalar2=None, op0=ALU.mult), reads=['ps4'], writes=['mean'])
            P.op('dve', lambda e: e.tensor_tensor(out=var[:, 0:n], in0=mean[:, 0:n], in1=mean[:, 0:n], op=ALU.mult), reads=['mean'], writes=['var'])
            P.op('dve', lambda e: e.scalar_tensor_tensor(out=var[:, 0:n], in0=k.ps[5][:, 0:n], scalar=1.0 / 512, in1=var[:, 0:n], op0=ALU.mult, op1=ALU.subtract),
                 reads=['ps5', 'var'], writes=['var'])
            P.op('dve', lambda e: e.tensor_scalar(out=var[:, 0:n], in0=var[:, 0:n], scalar1=0.0, scalar2=None, op0=ALU.max), reads=['var'], writes=['var'])
            P.op('act', lambda e: e.activation(out=var[:, 0:n], in_=var[:, 0:n], func=AF.Sqrt, bias=k.epsT[:, 0:1]), reads=['var', 'epsT'], writes=['var'])
            P.op('dve', lambda e: e.reciprocal(out=rs[:, 0:n], in_=var[:, 0:n]), reads=['var'], writes=['rs'])
            P.op('dve', lambda e: e.tensor_tensor(out=cx[:, :, 0:n], in0=cx[:, :, 0:n], in1=mean[:, 0:n].unsqueeze(1).to_broadcast([128, 4, n]), op=ALU.subtract),
                 reads=[ck, 'mean'], writes=[ck])
            P.op('dve', lambda e: e.tensor_tensor(out=cx[:, :, 0:n], in0=cx[:, :, 0:n], in1=rs[:, 0:n].unsqueeze(1).to_broadcast([128, 4, n]), op=ALU.mult),
                 reads=[ck, 'rs'], writes=[ck])
            for ch in range(4):
                P.op('act', lambda e, ch=ch: e.activation(out=yc[par][:, ch, 0:n], in_=cx[:, ch, 0:n], func=AF.Silu,
                                                          scale=vb[:, 4 + ch:5 + ch], bias=vb[:, 8 + ch:9 + ch]),
                     reads=[ck] + vkeys, writes=['yc%d' % par])
            P.dma('sp', k.ycT[:, :, t0:t0 + n], yc[par][:, :, 0:n], reads=['yc%d' % par], writes=['ycT'])

        load(tiles[0])
        for idx, ti in enumerate(tiles):
            nxt = tiles[idx + 1] if idx + 1 < len(tiles) else None
            if nxt is not None:
                load(nxt)
            conv_mm(ti, [0])
            if idx > 0:
                stats_mm(tiles[idx - 1])
                ln_tail(tiles[idx - 1])
            conv_mm(ti, [1, 2, 3])
        stats_mm(tiles[-1])
        ln_tail(tiles[-1])


def phase_att(k, l):
    P = k.P
    I = k.I
    need_ctx = l < L - 1
    with Ph(k, 'pat%d' % l) as ph:
        kTs = ph.sb('kTs', [128, NT], BF16)
        Vs = ph.sb('Vs', [128, NBLK, 128], BF16)
        P.dma('sp', kTs[:], k.kT, reads=['kT'], writes=['kTs'])
        for part in range(0, NBLK, 11):
            P.dma('sp', Vs[:, part:part + 11, :], k.V[part * 128:(part + 11) * 128, :].rearrange("(s p) d -> p s d", p=128), reads=['V'], writes=['Vs'])
        snk = ph.sb('snk', [1, 8], F32)
        sx = ph.sb('sx', [128, 4], F32)
        P.dma('sp', snk[:], I['attn_sink'][l:l + 1, :], writes=['snk'])
        onesF = ph.sb('onesF', [1, 128], F32)
        P.op('dve', lambda e: e.memset(onesF[:], 1.0), writes=['onesF'])
        P.op('pe', lambda e: e.matmul(k.ps[7][:, 0:8], lhsT=onesF[:], rhs=snk[:], start=True, stop=True), reads=['onesF', 'snk'], writes=['ps7'])
        P.op('act', lambda e: e.activation(out=sx[0:64, :], in_=k.ps[7][0:64, 0:4], func=AF.Exp), reads=['ps7'], writes=['sx'])
        P.op('act', lambda e: e.activation(out=sx[64:128, :], in_=k.ps[7][64:128, 4:8], func=AF.Exp), reads=['ps7'], writes=['sx'])
        qs = [ph.sb('qs%d' % i, [128, 4, 128], BF16) for i in range(2)]
        pT = [ph.sb('pT%d' % i, [128, 512], BF16) for i in range(8)]
        den = ph.sb('den', [128, 512], F32)
        yo = [ph.sb('yo%d' % i, [128, 4, 128], BF16) for i in range(2)]
        cnt = 0
        pcnt = 0
        qblocks = list(range(2, NBLK)) + ([0, 1] if need_ctx else [])
        def qload(qi):
            P.dma('sp', qs[qi % 2][:], k.qT[:, :, qblocks[qi] * 128:qblocks[qi] * 128 + 128], reads=['qT'], writes=['qs%d' % (qi % 2)])
        qload(0)
        for qi, qb in enumerate(qblocks):
            par = qi % 2
            t0 = qb * 128
            qk = 'qs%d' % par
            if qi + 1 < len(qblocks):
                qload(qi + 1)
            if qb >= 2:
                kbs = []
                if qb > 2:
                    kbs.append((qb - 1, k.nmp, 'nmp'))
                kbs.append((qb, None, None))
                if qb < NBLK - 1:
                    kbs.append((qb + 1, k.nmn, 'nmn'))
                kbs += [(0, None, None), (1, None, None)]
            else:
                kbs = [(0, None, None), (1, None, None)]
            bo = 4 + (qi % 2)
            bd_ = 6 + (qi % 2)
            pko, pkd = 'ps%d' % bo, 'ps%d' % bd_
            for g in range(2):
                pb = g * 64
                pl = []
                for (kb, nm, nmk) in kbs:
                    b = cnt % 4
                    cnt += 1
                    pk = 'ps%d' % b
                    P.op('pe', lambda e, b=b, pb=pb, kb=kb, par=par, nm=nm: e.matmul(k.ps[b][:, :], lhsT=kTs[pb:pb + 64, kb * 128:(kb + 1) * 128],
                                                                                    rhs=qs[par][pb:pb + 64, :, :], start=True, stop=(nm is None)),
                         reads=['kTs', qk], writes=[pk])
                    if nm is not None:
                        P.op('pe', lambda e, b=b, nm=nm: e.matmul(k.ps[b][:, :], lhsT=k.identB[:], rhs=nm[:], start=False, stop=True),
                             reads=['identB', nmk], writes=[pk])
                    pi = pcnt % 8
                    pcnt += 1
                    P.op('act', lambda e, b=b, pi=pi: e.activation(out=pT[pi][:], in_=k.ps[b][:, :], func=AF.Exp, scale=0.125), reads=[pk], writes=['pT%d' % pi])
                    pl.append((kb, pi))
                for j, (kb, pi) in enumerate(pl):
                    P.op('pe', lambda e, bo=bo, pb=pb, kb=kb, pi=pi, j=j, nl=len(pl): e.matmul(k.ps[bo][pb:pb + 64, :], lhsT=Vs[:, kb, pb:pb + 64], rhs=pT[pi][:],
                                                                                              start=(j == 0), stop=(j == nl - 1)),
                         reads=['Vs', 'pT%d' % pi], writes=[pko])
                for j, (kb, pi) in enumerate(pl):
                    P.op('pe', lambda e, bd_=bd_, pb=pb, pi=pi, j=j, nl=len(pl): e.matmul(k.ps[bd_][pb:pb + 64, :], lhsT=k.onesB[:, 0:64], rhs=pT[pi][:],
                                                                                         start=(j == 0), stop=(j == nl - 1)),
                         reads=['onesB', 'pT%d' % pi], writes=[pkd])
            P.op('dve', lambda e, bd_=bd_: e.tensor_tensor(out=den[:, :].rearrange("p (c t) -> p c t", t=128), in0=k.ps[bd_][:, :].rearrange("p (c t) -> p c t", t=128),
                                                           in1=sx[:, :].unsqueeze(2).to_broadcast([128, 4, 128]), op=ALU.add), reads=[pkd, 'sx'], writes=['den'])
            P.op('dve', lambda e: e.reciprocal(out=den[:], in_=den[:]), reads=['den'], writes=['den'])
            P.op('dve', lambda e, bo=bo, par=par: e.tensor_tensor(out=yo[par][:, :, :].rearrange("p c t -> p (c t)"), in0=k.ps[bo][:, :], in1=den[:], op=ALU.mult),
                 reads=[pko, 'den'], writes=['yo%d' % par])
            P.dma('sp', k.yaT[:, :, t0:t0 + 128], yo[par][:], reads=['yo%d' % par], writes=['yaT'])


def phase_merge(k, l):
    P = k.P
    I = k.I
    need_ctx = l < L - 1
    with Ph(k, 'pmg%d' % l) as ph:
        wg = ph.sb('wg', [128, 8, 3072], BF16)
        wo3 = ph.sb('wo3', [128, 3, 4, 1024], BF16)
        wout = ph.sb('wout', [128, 8, 1024], BF16)
        for half in range(2):
            for br in range(3):
                c0 = br * 1024 + half * 512
                for kc in range(8):
                    wload(k, wg[:, kc, c0:c0 + 512], I['w_in'][l, kc * 128:(kc + 1) * 128, 2816 + c0:2816 + c0 + 512], 'wg%d' % half)
        for c in range(4):
            wload(k, wo3[0:64, 0, c, :], I['w_o_attn'][l, c * 64:(c + 1) * 64, :], 'wo3')
            wload(k, wo3[64:128, 0, c, :], I['w_o_attn'][l, (c + 4) * 64:(c + 5) * 64, :], 'wo3')
            wload(k, wo3[:, 1, c, :], I['w_o_conv'][l, c * 128:(c + 1) * 128, :], 'wo3')
            wload(k, wo3[:, 2, c, :], I['w_o_lru'][l, c * 128:(c + 1) * 128, :], 'wo3')
        for kc in range(8):
            wload(k, wout[:, kc, :], I['w_out'][l, kc * 128:(kc + 1) * 128, :], 'wout')
        aT_t = [ph.sb('aT%d' % i, [128, 8, 512], BF16) for i in range(2)]
        y3 = [ph.sb('y3%d' % i, [128, 3, 4, 512], BF16) for i in range(2)]
        hT_t = [ph.sb('hT%d' % i, [128, 8, 512], F32) for i in range(2)]
        gs = [ph.sb('gs%d' % i, [128, 3, 512], F32) for i in range(2)]
        m1 = ph.sb('m1', [128, 512], F32)
        m2 = ph.sb('m2', [128, 512], F32)
        mg = ph.sb('mg', [128, 8, 512], BF16)
        ysrc = (k.yaT, k.ycT, k.ylT)
        ykeys = ('yaT', 'ycT', 'ylT')
        cnt = 0
        mtiles = [ti for ti in range(len(TILES)) if not (ti == 0 and not need_ctx)]

        def mload(ti):
            t0, n = TILES[ti]
            par = ti % 2
            P.dma('sp', aT_t[par][:, :, 0:n], k.aT[:, :, t0:t0 + n], reads=[('aT', ti)], writes=['aT%d' % par])
            for br in range(3):
                P.dma('sp', y3[par][:, br, :, 0:n], ysrc[br][:, :, t0:t0 + n], reads=[ykeys[br]], writes=['y3%d' % par])
            P.dma('sp', hT_t[par][:, :, 0:n], k.hT[:, :, t0:t0 + n], reads=[('hT', b) for b in range(t0 // 128, (t0 + n) // 128)], writes=['hT%d' % par])
        mload(mtiles[0])
        for mi, ti in enumerate(mtiles):
            t0, n = TILES[ti]
            par = ti % 2
            r = 1 if ti == 0 else 0
            ak, yk, hk = 'aT%d' % par, 'y3%d' % par, 'hT%d' % par
            if mi + 1 < len(mtiles):
                mload(mtiles[mi + 1])
            for m in range(8):
                gp = cnt % 2
                cnt += 1
                for br in range(3):
                    pk = 'ps%d' % br
                    for kc in range(8):
                        P.op('pe', lambda e, br=br, kc=kc, m=m, par=par, n=n: e.matmul(k.ps[br][:, 0:n], lhsT=wg[:, kc, br * 1024 + m * 128:br * 1024 + (m + 1) * 128],
                                                                                      rhs=aT_t[par][:, kc, 0:n], start=(kc == 0), stop=(kc == 7)),
                             reads=['wg%d' % (m // 4), ak], writes=[pk])
                    P.op('act', lambda e, br=br, gp=gp, n=n: e.activation(out=gs[gp][:, br, 0:n], in_=k.ps[br][:, 0:n], func=AF.Sigmoid), reads=[pk], writes=['gs%d' % gp])
                for br in range(3):
                    pk = 'ps%d' % (3 + br)
                    for kc in range(4):
                        P.op('pe', lambda e, br=br, kc=kc, m=m, par=par, n=n: e.matmul(k.ps[3 + br][:, 0:n], lhsT=wo3[:, br, kc, m * 128:(m + 1) * 128],
                                                                                      rhs=y3[par][:, br, kc, 0:n], start=(kc == 0), stop=(kc == 3)),
                             reads=['wo3', yk], writes=[pk])
                gk = 'gs%d' % gp
                P.op('dve', lambda e, gp=gp, n=n: e.tensor_tensor(out=m1[:, 0:n], in0=k.ps[3][:, 0:n], in1=gs[gp][:, 0, 0:n], op=ALU.mult), reads=['ps3', gk], writes=['m1'])
                P.op('dve', lambda e, gp=gp, n=n: e.tensor_tensor(out=m2[:, 0:n], in0=k.ps[4][:, 0:n], in1=gs[gp][:, 1, 0:n], op=ALU.mult), reads=['ps4', gk], writes=['m2'])
                P.op('pool', lambda e, n=n: e.tensor_tensor(out=m1[:, 0:n], in0=m1[:, 0:n], in1=m2[:, 0:n], op=ALU.add), reads=['m1', 'm2'], writes=['m1'])
                P.op('dve', lambda e, gp=gp, n=n: e.tensor_tensor(out=m2[:, 0:n], in0=k.ps[5][:, 0:n], in1=gs[gp][:, 2, 0:n], op=ALU.mult), reads=['ps5', gk], writes=['m2'])
                P.op('pool', lambda e, n=n, m=m: e.tensor_tensor(out=mg[:, m, 0:n], in0=m1[:, 0:n], in1=m2[:, 0:n], op=ALU.add), reads=['m1', 'm2'], writes=['mg'])
            for m in range(8):
                b = 6 + (m % 2)
                pk = 'ps%d' % b
                for kc in range(8):
                    P.op('pe', lambda e, b=b, kc=kc, m=m, n=n: e.matmul(k.ps[b][:, 0:n], lhsT=wout[:, kc, m * 128:(m + 1) * 128], rhs=mg[:, kc, 0:n],
                                                                       start=(kc == 0), stop=(kc == 7)), reads=['wout', 'mg'], writes=[pk])
                P.op('dve', lambda e, b=b, m=m, n=n, par=par, r=r: e.scalar_tensor_tensor(out=hT_t[par][:, m, 0:n], in0=k.ps[b][:, 0:n], scalar=k.mods[:, r, 2, m:m + 1],
                                                                                         in1=hT_t[par][:, m, 0:n], op0=ALU.mult, op1=ALU.add),
                     reads=[pk, 'mods', hk], writes=[hk])
            P.dma('sp', k.hT[:, :, t0:t0 + n], hT_t[par][:, :, 0:n], reads=[hk], writes=[('hT', b) for b in range(t0 // 128, (t0 + n) // 128)])


def phase_ffn(k, l):
    P = k.P
    I = k.I
    need_ctx = l < L - 1
    FT = 256
    n = FT
    with Ph(k, 'pff%d' % l) as ph:
        wup = ph.sb('wup', [128, 8, 2 * FH], BF16)
        wdn = ph.sb('wdn', [128, 22, 1024], BF16)
        for piece in (0, 2, 1, 3):
            for kc in range(8):
                wload(k, wup[:, kc, piece * 1408:(piece + 1) * 1408], I['ffn_w_up'][l, kc * 128:(kc + 1) * 128, piece * 1408:(piece + 1) * 1408], 'wup%d' % piece)
        for kc in range(22):
            wload(k, wdn[:, kc, :], I['ffn_w_down'][l, kc * 128:(kc + 1) * 128, :], 'wdn')
        hT_t = [ph.sb('hT%d' % i, [128, 8, FT], F32) for i in range(2)]
        sq = ph.sb('sq', [128, 8, FT], BF16)
        tmpn = ph.sb('tmpn', [128, FT], F32)
        rstd = ph.sb('rstd', [128, FT], F32)
        hn = ph.sb('hn', [128, 8, FT], F32)
        a2s = [ph.sb('a2%d' % i, [128, 8, FT], BF16) for i in range(2)]
        sg = [ph.sb('sg%d' % i, [128, FT], F32) for i in range(2)]
        hid = ph.sb('hid', [128, 22, FT], BF16)
        tiles = [ti for ti in range(NT // FT) if not (ti == 0 and not need_ctx)]

        def hkeys(ti):
            return [('hT', b) for b in range(ti * FT // 128, (ti * FT + n) // 128)]

        def load(ti):
            P.dma('sp', hT_t[ti % 2][:, :, 0:n], k.hT[:, :, ti * FT:ti * FT + n], reads=hkeys(ti), writes=['hT%d' % (ti % 2)])

        def part1(ti):
            norm_rstd(k, ph, hT_t[ti % 2], 'hT%d' % (ti % 2), n, sq, 'sq', rstd, 'rstd', tmpn, 'tmpn')

        def part2(ti):
            par = ti % 2
            r = 1 if ti == 0 else 0
            P.op('dve', lambda e: e.tensor_tensor(out=hn[:, :, 0:n], in0=hT_t[par][:, :, 0:n],
                                                   in1=rstd[:, 0:n].unsqueeze(1).to_broadcast([128, 8, n]), op=ALU.mult),
                 reads=['hT%d' % par, 'rstd'], writes=['hn'])
            for c in range(8):
                P.op('act', lambda e, c=c: e.activation(out=a2s[par][:, c, 0:n], in_=hn[:, c, 0:n], func=AF.Identity,
                                                        scale=k.A2[:, r, c:c + 1], bias=k.mods[:, r, 3, c:c + 1]),
                     reads=['hn', 'A2', 'mods'], writes=['a2%d' % par])

        def down(ti, m):
            par = ti % 2
            r = 1 if ti == 0 else 0
            hk = 'hT%d' % par
            b = 4 + (m % 3)
            pk = 'ps%d' % b
            for kc in range(22):
                P.op('pe', lambda e, kc=kc: e.matmul(k.ps[b][:, 0:n], lhsT=wdn[:, kc, m * 128:(m + 1) * 128], rhs=hid[:, kc, 0:n],
                                                     start=(kc == 0), stop=(kc == 21)), reads=['wdn', 'hid'], writes=[pk])
            P.op('dve', lambda e: e.scalar_tensor_tensor(out=hT_t[par][:, m, 0:n], in0=k.ps[b][:, 0:n], scalar=k.mods[:, r, 5, m:m + 1],
                                                         in1=hT_t[par][:, m, 0:n], op0=ALU.mult, op1=ALU.add),
                 reads=[pk, 'mods', hk], writes=[hk])

        load(tiles[0])
        part1(tiles[0])
        part2(tiles[0])
        for idx, ti in enumerate(tiles):
            par = ti % 2
            a2 = a2s[par]
            ak = 'a2%d' % par
            nxt = tiles[idx + 1] if idx + 1 < len(tiles) else None
            if nxt is not None:
                load(nxt)
            for j in range(22):
                bu, bg = 2 * (j % 2), 2 * (j % 2) + 1
                pku, pkg = 'ps%d' % bu, 'ps%d' % bg
                s2 = j % 2
                for kc in range(8):
                    P.op('pe', lambda e, bu=bu, kc=kc, j=j, a2=a2: e.matmul(k.ps[bu][:, 0:n], lhsT=wup[:, kc, j * 128:(j + 1) * 128], rhs=a2[:, kc, 0:n],
                                                                    start=(kc == 0), stop=(kc == 7)), reads=['wup%d' % (j // 11), ak], writes=[pku])
                for kc in range(8):
                    P.op('pe', lambda e, bg=bg, kc=kc, j=j, a2=a2: e.matmul(k.ps[bg][:, 0:n], lhsT=wup[:, kc, FH + j * 128:FH + (j + 1) * 128], rhs=a2[:, kc, 0:n],
                                                                    start=(kc == 0), stop=(kc == 7)), reads=['wup%d' % (2 + j // 11), ak], writes=[pkg])
                P.op('act', lambda e, bg=bg, s2=s2: e.activation(out=sg[s2][:, 0:n], in_=k.ps[bg][:, 0:n], func=AF.Silu), reads=[pkg], writes=['sg%d' % s2])
                P.op('dve', lambda e, bu=bu, s2=s2, j=j: e.tensor_tensor(out=hid[:, j, 0:n], in0=k.ps[bu][:, 0:n], in1=sg[s2][:, 0:n], op=ALU.mult),
                     reads=[pku, 'sg%d' % s2], writes=['hid'])
            for m in range(4):
                down(ti, m)
            if nxt is not None:
                part1(nxt)
                part2(nxt)
            for m in range(4, 8):
                down(ti, m)
            P.dma('sp', k.hT[:, :, ti * FT:ti * FT + n], hT_t[par][:, :, 0:n], reads=['hT%d' % par], writes=hkeys(ti))


def phase_out(k):
    P = k.P
    I = k.I
    with Ph(k, 'pout') as ph:
        gf = ph.sb('gf', [128, 8], F32)
        load_vec_fm(k, ph, I['final_norm_g'].rearrange("(c p) -> c p", p=128), 8, gf[:], 'gf')
        hT_t = [ph.sb('hT%d' % i, [128, 8, 512], F32) for i in range(2)]
        sq = ph.sb('sq', [128, 8, 512], BF16)
        tmpn = ph.sb('tmpn', [128, 512], F32)
        rstd = ph.sb('rstd', [128, 512], F32)
        hn = ph.sb('hn', [128, 8, 512], F32)
        ot = [ph.sb('ot%d' % i, [128, D], F32) for i in range(2)]
        cnt = 0

        def oload(ti):
            t0, n = TILES[ti]
            P.dma('sp', hT_t[ti % 2][:, :, 0:n], k.hT[:, :, t0:t0 + n], reads=[('hT', b) for b in range(t0 // 128, (t0 + n) // 128)], writes=['hT%d' % (ti % 2)])
        oload(1)
        for ti, (t0, n) in enumerate(TILES):
            if ti == 0:
                continue
            par = ti % 2
            hk = 'hT%d' % par
            if ti + 1 < len(TILES):
                oload(ti + 1)
            norm_rstd(k, ph, hT_t[par], hk, n, sq, 'sq', rstd, 'rstd', tmpn, 'tmpn')
            P.op('dve', lambda e, par=par, n=n: e.tensor_tensor(out=hn[:, :, 0:n], in0=hT_t[par][:, :, 0:n],
                                                                 in1=rstd[:, 0:n].unsqueeze(1).to_broadcast([128, 8, n]), op=ALU.mult),
                 reads=[hk, 'rstd'], writes=['hn'])
            for c in range(8):
                P.op('act', lambda e, n=n, c=c: e.activation(out=hn[:, c, 0:n], in_=hn[:, c, 0:n], func=AF.Identity, scale=gf[:, c:c + 1]),
                     reads=['hn', 'gf'], writes=['hn'])
            for sblk in range(4):
                op_ = cnt % 2
                cnt += 1
                ok = 'ot%d' % op_
                for half in range(2):
                    b = 2 * op_ + half
                    pk = 'ps%d' % b
                    for j in range(4):
                        c = 4 * half + j
                        P.op('pe', lambda e, b=b, j=j, c=c, sblk=sblk: e.transpose(out=k.ps[b][:, j * 128:(j + 1) * 128], in_=hn[:, c, sblk * 128:(sblk + 1) * 128],
                                                                                   identity=k.identF[:]), reads=['hn', 'identF'], writes=[pk])
                    if half == 0:
                        P.op('act', lambda e, b=b, op_=op_: e.activation(out=ot[op_][:, 0:512], in_=k.ps[b][:, :], func=AF.Copy), reads=[pk], writes=[ok])
                    else:
                        P.op('dve', lambda e, b=b, op_=op_: e.tensor_copy(out=ot[op_][:, 512:1024], in_=k.ps[b][:, :]), reads=[pk], writes=[ok])
                r0 = t0 - C + sblk * 128
                P.dma('sp', k.out[r0:r0 + 128, :], ot[op_][:], reads=[ok], writes=[('out', r0)])
        k.P.barrier(final=True)


def _consts():
    f32 = np.float32
    rows = S // 64
    row = np.repeat(np.arange(rows, dtype=f32), 64)
    col = np.tile(np.arange(64, dtype=f32), rows)
    inv = np.power(f32(10000.0), -np.arange(16, dtype=f32) / f32(16)).astype(f32)
    ang = np.concatenate([row[:, None] * inv[None], col[:, None] * inv[None]], axis=-1).astype(f32)
    cos, sin = np.cos(ang).astype(f32), np.sin(ang).astype(f32)
    pidx = (np.arange(128) % 64) % 32
    cos2 = np.ascontiguousarray(cos[:, pidx].T)
    sin2 = np.ascontiguousarray(sin[:, pidx].T)
    rot = np.zeros((128, 128), f32)
    for m in range(128):
        if (m % 64) < 32:
            rot[m + 32, m] = -1.0
        else:
            rot[m - 32, m] = 1.0
    ident = np.eye(128, dtype=f32)
    j = np.arange(128)[:, None]
    i = np.arange(128)[None, :]
    nm_prev = np.where(j >= i, 0.0, -30000.0).astype(f32)
    nm_next = np.where(j <= i, 0.0, -30000.0).astype(f32)
    return {'cos2': cos2, 'sin2': sin2, 'rotm': rot, 'ident': ident,
            'nm_prev': np.ascontiguousarray(np.tile(nm_prev, (1, 4))), 'nm_next': np.ascontiguousarray(np.tile(nm_next, (1, 4)))}


_NC = None


def kernel(**inputs):
    global _NC
    inp = {n: np.ascontiguousarray(np.asarray(v, dtype=np.float32)) for n, v in inputs.items()}
    if _NC is None:
        _NC = build()
    consts = _consts()
    shared = {n: inp[n] for n in ('mod_w', 'mod_b', 'norm1_g', 'norm2_g', 'w_in', 'attn_sink', 'conv_dw_w', 'conv_dw_b', 'conv_ln_g',
                                  'conv_ln_b', 'lru_conv_w', 'lru_conv_b', 'lru_wa', 'lru_ba', 'lru_wx', 'lru_bx', 'lru_lam',
                                  'w_o_attn', 'w_o_conv', 'w_o_lru', 'w_out', 'ffn_w_up', 'ffn_w_down', 'final_norm_g')}
    shared.update(consts)
    in_maps = []
    for core in range(8):
        b = core % 2
        m = dict(shared)
        m['x'] = inp['x'][b]
        m['ctx'] = inp['ctx'][b]
        m['cvec'] = np.ascontiguousarray(np.stack([inp['c'][b], inp['c_ctx']], axis=0))
        in_maps.append(m)
    res = run_bass_kernel_spmd(_NC, in_maps, core_ids=list(range(8)))
    return np.stack([res.results[0]['out'], res.results[1]['out']], axis=0).astype(np.float32)
```
